# Optimizing a Trainium2 kernel written in Bass

```python
import math
import jax, jax.numpy as jnp
from jax import lax
import numpy as np

D_MODEL = 1024
BATCH = 8
SEQ = 2048
DEPTH = 1
DEC_BATCH = 128
DEC_SEQ = 8
PAST_LEN = 2048
PAGE_SIZE = 128

HEAD_DIM = 64
H_FOX = 8
H_DIFF = 4
V_DIFF = 2 * HEAD_DIM
W_FOX = H_FOX * HEAD_DIM
W_DIFF_QK = H_DIFF * 2 * HEAD_DIM
W_DIFF_V = H_DIFF * V_DIFF
D_FF = -(-8 * D_MODEL // (3 * 256)) * 256
Q_BLOCK = 128
ALPHA = (2 * DEPTH) ** 0.25
BETA = (8 * DEPTH) ** -0.25
LN_EPS = 1e-5
NEG_INF = -1e30
FORGET_BIAS_INIT = 3.0
COL_SPLITS = (W_FOX, W_FOX, W_FOX, H_FOX, W_DIFF_QK, W_DIFF_QK, W_DIFF_V, D_MODEL, D_MODEL)
D_IN = sum(COL_SPLITS)

kernel_name = "fox_diff_gated_hybrid_step"


def layer_norm(x, g, b):
    xf = x.astype(jnp.float32)
    mu = jnp.mean(xf, axis=-1, keepdims=True)
    var = jnp.mean(jnp.square(xf - mu), axis=-1, keepdims=True)
    return ((xf - mu) * lax.rsqrt(var + LN_EPS) * g + b).astype(x.dtype)


def rms_norm(x, g):
    xf = x.astype(jnp.float32)
    return (xf * lax.rsqrt(jnp.mean(jnp.square(xf), axis=-1, keepdims=True) + LN_EPS) * g).astype(x.dtype)


def split_cols(p):
    outs, start = [], 0
    for w in COL_SPLITS:
        outs.append(p[..., start:start + w])
        start += w
    return outs


def adaln(c, w_ada, b_ada):
    mod = c @ w_ada + b_ada
    return jnp.split(mod[:, None, :], 6, axis=-1)


def mixer_inputs(x, shift, scale, w_in, b_forget):
    B, T = x.shape[0], x.shape[1]
    h = x * (1 + scale) + shift
    qa, ka, va, fa, qb, kb, vb, ga, gb = split_cols(h @ w_in)
    qa = qa.reshape(B, T, H_FOX, HEAD_DIM)
    ka = ka.reshape(B, T, H_FOX, HEAD_DIM)
    va = va.reshape(B, T, H_FOX, HEAD_DIM)
    logf = jax.nn.log_sigmoid((fa + b_forget).astype(jnp.float32))
    qb = qb.reshape(B, T, H_DIFF, 2, HEAD_DIM)
    kb = kb.reshape(B, T, H_DIFF, 2, HEAD_DIM)
    vb = vb.reshape(B, T, H_DIFF, V_DIFF)
    return qa, ka, va, logf, qb, kb, vb, ga, gb


def fox_attention(q, k, v, cq, ck, pos_q, pos_k):
    B, Tq = q.shape[0], q.shape[1]
    s = jnp.einsum('bqhd,bkhd->bhqk', q, k).astype(jnp.float32) * (HEAD_DIM ** -0.5)
    s = s + (jnp.transpose(cq, (0, 2, 1))[..., :, None] - jnp.transpose(ck, (0, 2, 1))[..., None, :])
    mask = pos_k[None, :] <= pos_q[:, None]
    p = jax.nn.softmax(jnp.where(mask, s, NEG_INF), axis=-1)
    o = jnp.einsum('bhqk,bkhd->bqhd', p.astype(v.dtype), v)
    return o.reshape(B, Tq, W_FOX)


def diff_attention(q, k, v, lam, slopes, subln_gain, lambda_init, pos_q, pos_k):
    B, Tq = q.shape[0], q.shape[1]
    s = jnp.einsum('bqhid,bkhid->bihqk', q, k).astype(jnp.float32) * (HEAD_DIM ** -0.5)
    dist = (pos_q[:, None] - pos_k[None, :]).astype(jnp.float32)
    bias = -slopes[:, None, None] * dist
    mask = pos_k[None, :] <= pos_q[:, None]
    p = jax.nn.softmax(jnp.where(mask, s + bias, NEG_INF), axis=-1)
    a = p[:, 0] - lam * p[:, 1]
    o = jnp.einsum('bhqk,bkhe->bqhe', a.astype(v.dtype), v)
    o = rms_norm(o, subln_gain) * (1.0 - lambda_init)
    return o.reshape(B, Tq, W_DIFF_V)


def sweep_query_blocks(fn, q_args, pos_q):
    n_blk = pos_q.shape[0] // Q_BLOCK

    def one(i):
        start = i * Q_BLOCK
        sl = [lax.dynamic_slice_in_dim(a, start, Q_BLOCK, axis=1) for a in q_args]
        return fn(*sl, lax.dynamic_slice_in_dim(pos_q, start, Q_BLOCK, axis=0))

    out = lax.map(one, jnp.arange(n_blk))
    out = jnp.moveaxis(out, 0, 1)
    return out.reshape((out.shape[0], n_blk * Q_BLOCK) + out.shape[3:])


def gather_pages(cache, page_table):
    g = cache[page_table]
    return g.reshape((g.shape[0], g.shape[1] * g.shape[2]) + g.shape[3:])


def merge_and_ffn(x, oa, ob, ga, gb, gate1, shift2, scale2, gate2,
                  w_branch_a, w_branch_b, w_out, ln1_gain, ln1_bias,
                  w_ffn_gate, w_ffn_up, w_ffn_down, ln2_gain, ln2_bias):
    m = jax.nn.sigmoid(ga) * (oa @ w_branch_a) + jax.nn.sigmoid(gb) * (ob @ w_branch_b)
    x = layer_norm(ALPHA * x + gate1 * (m @ w_out), ln1_gain, ln1_bias)
    h = x * (1 + scale2) + shift2
    f = (jax.nn.silu(h @ w_ffn_gate) * (h @ w_ffn_up)) @ w_ffn_down
    return layer_norm(ALPHA * x + gate2 * f, ln2_gain, ln2_bias)


def setup_inputs(seed: int = 0) -> dict:
    key = jax.random.key(seed)
    ks = jax.random.split(key, 40)
    n_pages = PAST_LEN // PAGE_SIZE
    n_used = DEC_BATCH * n_pages
    n_phys = (5 * n_used) // 4

    def nrm(k, shape, scale=1.0):
        return jax.random.normal(k, shape, jnp.float32) * scale

    page_table = jax.random.permutation(ks[0], n_phys)[:n_used].reshape(DEC_BATCH, n_pages).astype(jnp.int32)
    return {
        "x_prompt": nrm(ks[1], (BATCH, SEQ, D_MODEL)),
        "x_sample": nrm(ks[2], (DEC_BATCH, DEC_SEQ, D_MODEL)),
        "c_prompt": nrm(ks[3], (BATCH, D_MODEL)),
        "c_sample": nrm(ks[4], (DEC_BATCH, D_MODEL)),
        "cache_k_fox": nrm(ks[5], (DEPTH, n_phys, PAGE_SIZE, H_FOX, HEAD_DIM)),
        "cache_v_fox": nrm(ks[6], (DEPTH, n_phys, PAGE_SIZE, H_FOX, HEAD_DIM)),
        "cache_logf_fox": jax.nn.log_sigmoid(FORGET_BIAS_INIT + nrm(ks[7], (DEPTH, n_phys, PAGE_SIZE, H_FOX))),
        "cache_k_diff": nrm(ks[8], (DEPTH, n_phys, PAGE_SIZE, H_DIFF, 2, HEAD_DIM)),
        "cache_v_diff": nrm(ks[9], (DEPTH, n_phys, PAGE_SIZE, H_DIFF, V_DIFF)),
        "page_table": page_table,
        "w_ada": nrm(ks[10], (DEPTH, D_MODEL, 6 * D_MODEL), 0.5 * D_MODEL ** -0.5),
        "b_ada": nrm(ks[11], (DEPTH, 6 * D_MODEL), 0.01),
        "w_in": nrm(ks[12], (DEPTH, D_MODEL, D_IN), D_MODEL ** -0.5),
        "b_forget": FORGET_BIAS_INIT + nrm(ks[13], (DEPTH, H_FOX), 0.1),
        "lambda_q1": nrm(ks[14], (DEPTH, HEAD_DIM), 0.1),
        "lambda_k1": nrm(ks[15], (DEPTH, HEAD_DIM), 0.1),
        "lambda_q2": nrm(ks[16], (DEPTH, HEAD_DIM), 0.1),
        "lambda_k2": nrm(ks[17], (DEPTH, HEAD_DIM), 0.1),
        "subln_gain": 1.0 + nrm(ks[18], (DEPTH, V_DIFF), 0.02),
        "w_branch_a": nrm(ks[19], (DEPTH, W_FOX, D_MODEL), BETA * W_FOX ** -0.5),
        "w_branch_b": nrm(ks[20], (DEPTH, W_DIFF_V, D_MODEL), BETA * W_DIFF_V ** -0.5),
        "w_out": nrm(ks[21], (DEPTH, D_MODEL, D_MODEL), BETA * D_MODEL ** -0.5),
        "ln1_gain": 1.0 + nrm(ks[22], (DEPTH, D_MODEL), 0.02),
        "ln1_bias": nrm(ks[23], (DEPTH, D_MODEL), 0.02),
        "w_ffn_gate": nrm(ks[24], (DEPTH, D_MODEL, D_FF), BETA * D_MODEL ** -0.5),
        "w_ffn_up": nrm(ks[25], (DEPTH, D_MODEL, D_FF), BETA * D_MODEL ** -0.5),
        "w_ffn_down": nrm(ks[26], (DEPTH, D_FF, D_MODEL), BETA * D_FF ** -0.5),
        "ln2_gain": 1.0 + nrm(ks[27], (DEPTH, D_MODEL), 0.02),
        "ln2_bias": nrm(ks[28], (DEPTH, D_MODEL), 0.02),
    }


def reference(x_prompt, x_sample, c_prompt, c_sample, cache_k_fox, cache_v_fox, cache_logf_fox,
              cache_k_diff, cache_v_diff, page_table, w_ada, b_ada, w_in, b_forget,
              lambda_q1, lambda_k1, lambda_q2, lambda_k2, subln_gain, w_branch_a, w_branch_b,
              w_out, ln1_gain, ln1_bias, w_ffn_gate, w_ffn_up, w_ffn_down, ln2_gain, ln2_bias):
    slopes = 2.0 ** (-8.0 * jnp.arange(1, H_DIFF + 1, dtype=jnp.float32) / H_DIFF)
    pos_p = jnp.arange(SEQ)
    pos_kd = jnp.arange(PAST_LEN + DEC_SEQ)
    pos_qd = PAST_LEN + jnp.arange(DEC_SEQ)

    xp, xs = x_prompt, x_sample
    kfp, vfp, lfp, kdp, vdp = [], [], [], [], []
    kfs, vfs, lfs, kds, vds = [], [], [], [], []
    for l in range(DEPTH):
        lambda_init = 0.8 - 0.6 * math.exp(-0.3 * l)
        lam = (jnp.exp(jnp.sum(lambda_q1[l].astype(jnp.float32) * lambda_k1[l].astype(jnp.float32)))
               - jnp.exp(jnp.sum(lambda_q2[l].astype(jnp.float32) * lambda_k2[l].astype(jnp.float32)))
               + lambda_init)
        ffn_w = (w_branch_a[l], w_branch_b[l], w_out[l], ln1_gain[l], ln1_bias[l],
                 w_ffn_gate[l], w_ffn_up[l], w_ffn_down[l], ln2_gain[l], ln2_bias[l])

        sh1, sc1, g1, sh2, sc2, g2 = adaln(c_prompt, w_ada[l], b_ada[l])
        qa, ka, va, logf, qb, kb, vb, ga, gb = mixer_inputs(xp, sh1, sc1, w_in[l], b_forget[l])
        c_cum = jnp.cumsum(logf, axis=1)
        oa = sweep_query_blocks(
            lambda q, cq, pq: fox_attention(q, ka, va, cq, c_cum, pq, pos_p), (qa, c_cum), pos_p)
        ob = sweep_query_blocks(
            lambda q, pq: diff_attention(q, kb, vb, lam, slopes, subln_gain[l], lambda_init, pq, pos_p),
            (qb,), pos_p)
        xp = merge_and_ffn(xp, oa, ob, ga, gb, g1, sh2, sc2, g2, *ffn_w)
        kfp.append(ka); vfp.append(va); lfp.append(logf); kdp.append(kb); vdp.append(vb)

        sh1, sc1, g1, sh2, sc2, g2 = adaln(c_sample, w_ada[l], b_ada[l])
        qa_d, ka_d, va_d, logf_d, qb_d, kb_d, vb_d, ga_d, gb_d = mixer_inputs(xs, sh1, sc1, w_in[l], b_forget[l])
        k_fox = jnp.concatenate([gather_pages(cache_k_fox[l], page_table), ka_d], axis=1)
        v_fox = jnp.concatenate([gather_pages(cache_v_fox[l], page_table), va_d], axis=1)
        c_past = jnp.cumsum(gather_pages(cache_logf_fox[l], page_table).astype(jnp.float32), axis=1)
        c_new = c_past[:, -1:, :] + jnp.cumsum(logf_d, axis=1)
        c_keys = jnp.concatenate([c_past, c_new], axis=1)
        oa_d = fox_attention(qa_d, k_fox, v_fox, c_new, c_keys, pos_qd, pos_kd)
        k_diff = jnp.concatenate([gather_pages(cache_k_diff[l], page_table), kb_d], axis=1)
        v_diff = jnp.concatenate([gather_pages(cache_v_diff[l], page_table), vb_d], axis=1)
        ob_d = diff_attention(qb_d, k_diff, v_diff, lam, slopes, subln_gain[l], lambda_init, pos_qd, pos_kd)
        xs = merge_and_ffn(xs, oa_d, ob_d, ga_d, gb_d, g1, sh2, sc2, g2, *ffn_w)
        kfs.append(ka_d); vfs.append(va_d); lfs.append(logf_d); kds.append(kb_d); vds.append(vb_d)

    return (xp, xs,
            jnp.stack(kfp, 0), jnp.stack(vfp, 0), jnp.stack(lfp, 0), jnp.stack(kdp, 0), jnp.stack(vdp, 0),
            jnp.stack(kfs, 0), jnp.stack(vfs, 0), jnp.stack(lfs, 0), jnp.stack(kds, 0), jnp.stack(vds, 0))
```

```python
import os
from contextlib import ExitStack
import numpy as np
import concourse.bass as bass
import concourse.mybir as mybir
from concourse.bass_utils import run_bass_kernel_spmd

F32 = mybir.dt.float32
BF16 = mybir.dt.bfloat16
I32 = mybir.dt.int32
AF = mybir.ActivationFunctionType
ALU = mybir.AluOpType

D = 1024
KC = 8
T = 2048
NCH = 4
DFF = 2816
FC = 22
DIN = 5128
ALPHA = 2.0 ** 0.25
LN_EPS = 1e-5
LAMBDA_INIT = 0.2
SLOPES = [2.0 ** (-8.0 * (h + 1) / 4) for h in range(4)]
C_QA, C_KA, C_VA, C_FA, C_QB, C_KB, C_VB, C_GA, C_GB = 0, 512, 1024, 1536, 1544, 2056, 2568, 3080, 4104


class Buf:
    __slots__ = ("w", "r")

    def __init__(self):
        self.w = None
        self.r = {}


class DSem:
    def __init__(self, nc, name):
        self.sem = nc.alloc_semaphore(name)
        self.count = 0


class Eng:
    def __init__(self, nc, name, eng):
        self.eng = eng
        self.sem = nc.alloc_semaphore("sem_" + name)
        self.count = 0
        self.waited = {}

    def _wait(self, deps):
        for sem, val in deps.items():
            if self.waited.get(sem, 0) < val:
                self.eng.wait_ge(sem, val)
                self.waited[sem] = val

    @staticmethod
    def _deps(reads, writes):
        deps = {}
        for b in reads:
            if b.w is not None and deps.get(b.w[0], 0) < b.w[1]:
                deps[b.w[0]] = b.w[1]
        for b in writes:
            if b.w is not None and deps.get(b.w[0], 0) < b.w[1]:
                deps[b.w[0]] = b.w[1]
            for s, v in b.r.items():
                if deps.get(s, 0) < v:
                    deps[s] = v
        return deps

    @staticmethod
    def _mark(tok, reads, writes):
        s, v = tok
        for b in reads:
            if b.r.get(s, 0) < v:
                b.r[s] = v
        for b in writes:
            b.w = tok
            b.r = {}

    def op(self, fn, reads=(), writes=()):
        return self.group([fn], reads, writes)

    def group(self, fns, reads=(), writes=()):
        self._wait(self._deps(reads, writes))
        ins = None
        for fn in fns:
            ins = fn(self.eng)
        self.count += 1
        ins.then_inc(self.sem, 1)
        tok = (self.sem, self.count)
        self._mark(tok, reads, writes)
        return tok

    def dma(self, dsem, out, in_, reads=(), writes=()):
        self._wait(self._deps(reads, writes))
        ins = self.eng.dma_start(out=out, in_=in_)
        dsem.count += 16
        ins.then_inc(dsem.sem, 16)
        tok = (dsem.sem, dsem.count)
        self._mark(tok, reads, writes)
        return tok


def build_program(NP=2, NS=2, n_phys=2560, do_sample=True):
    NR = NP + 16 * NS
    STOP = os.environ.get('KDEV_STOP', '')
    nc = bass.Bass("TRN2", target_bir_lowering=False)

    def din(name, shape, dt=F32):
        return nc.dram_tensor(name, list(shape), dt, kind="ExternalInput").ap()

    def dout(name, shape, dt=F32):
        return nc.dram_tensor(name, list(shape), dt, kind="ExternalOutput").ap()

    xp = din("xp", [NP * T, D]); xs_in = din("xs", [NS * 128, D])
    cp = din("cp", [NP, D]); cs = din("cs", [NS * 16, D])
    pt = din("pt", [1, NS * 256], I32)
    ckf = din("ckf", [n_phys * 128, 512]); cvf = din("cvf", [n_phys * 128, 512])
    ckd = din("ckd", [n_phys * 128, 512]); cvd = din("cvd", [n_phys * 128, 512]); clf = din("clf", [n_phys, 1024])
    w_ada = din("w_ada", [D, 6 * D]); b_ada = din("b_ada", [1, 6 * D])
    w_in = din("w_in", [D, DIN]); b_forget = din("b_forget", [1, 8])
    lq1 = din("lq1", [1, 64]); lk1 = din("lk1", [1, 64]); lq2 = din("lq2", [1, 64]); lk2 = din("lk2", [1, 64])
    subln = din("subln", [1, 128])
    w_ba = din("w_ba", [512, D]); w_bb = din("w_bb", [512, D]); w_out = din("w_out", [D, D])
    ln1g = din("ln1g", [1, D]); ln1b = din("ln1b", [1, D])
    w_fg = din("w_fg", [D, DFF]); w_fu = din("w_fu", [D, DFF]); w_fd = din("w_fd", [DFF, D])
    ln2g = din("ln2g", [1, D]); ln2b = din("ln2b", [1, D])
    yp = dout("yp", [NP * T, D]); ys = dout("ys", [NS * 128, D])
    kfp = dout("kfp", [NP * T, 512]); vfp = dout("vfp", [NP * T, 512]); lfp = dout("lfp", [NP * T, 8])
    kdp = dout("kdp", [NP * T, 512]); vdp = dout("vdp", [NP * T, 512])
    kfs = dout("kfs", [NS * 128, 512]); vfs = dout("vfs", [NS * 128, 512]); lfs = dout("lfs", [NS * 128, 8])
    kds = dout("kds", [NS * 128, 512]); vds = dout("vds", [NS * 128, 512])
    osc = nc.dram_tensor("osc", [NS * 128, 1024], BF16, kind="Internal").ap()

    es = ExitStack()
    with es:
        def sb(name, shape, dt=F32):
            return es.enter_context(nc.sbuf_tensor(name, list(shape), dt))

        cst = sb("cst", [128, 8, 128])
        bfb = cst[:, 0, 0:8]; gsub = cst[:, 1, :]
        wfa_t = sb("wfa", [128, 256], BF16)
        wfa = wfa_t[:, 0:64].rearrange("p (k n) -> p k n", n=8)
        lng = sb("lng", [128, 4, D])
        xs = sb("xs_sb", [128, 4, D])
        wflat = [sb(f"wring{i}", [128, KC * 512], BF16) for i in range(3)]
        stage = [sb(f"stage{i}", [128, 512]) for i in range(4)]
        ident_f = sb("ident_f", [128, 128]); ident_b = sb("ident_b", [128, 128], BF16)
        tri_f = sb("tri_f", [128, 128]); tri_b = sb("tri_b", [128, 128], BF16)
        elast_f = sb("elast_f", [128, 128]); ones_f = sb("ones_f", [128, 128])
        ones_b = sb("ones_b", [128, 128], BF16)
        posS = sb("posS", [128, 16, 4])
        modT = sb("modT", [128, 48, NR])
        lamc = sb("lamc", [128, 4])
        small = sb("small", [128, 64])
        cT = sb("cT", [128, KC, NR], BF16)
        KTf = sb("KTf", [128, 4, T + 128], BF16); KTd = sb("KTd", [128, 4, T + 128], BF16)
        Vf = sb("Vf", [128, 17, 8, 65], BF16); Vd = sb("Vd", [128, 17, 4, 129], BF16)
        c_all = sb("c_all", [128, 17, 8]); lfacc = sb("lfacc", [128, 8]); lf_t = sb("lf_t", [128, 4, 8])
        fa_t = sb("fa_t", [128, 8]); rc_bc = sb("rc_bc", [128, 8])
        biasF = sb("biasF", [128, 16, 8]); biasD = sb("biasD", [128, 16, 4])
        hT = sb("hT", [128, KC, 512], BF16)
        big = sb("big", [128, FC * 512], BF16)
        oAB = sb("oAB", [128, 4, D], BF16)
        oT = sb("oT", [128, KC, 512], BF16)
        QTf = oT[:, 0:4, :]; QTd = oT[:, 4:8, :]
        tmpA = [sb(f"tmpA{i}", [128, 512]) for i in range(2)]
        tmpB = [sb(f"tmpB{i}", [128, 512]) for i in range(2)]
        dtmp = sb("dtmp", [128, 2, 128]); d1buf = sb("d1buf", [128, 4, 128])
        stat = sb("stat", [128, 16])
        bnst = sb("bnst", [128, 12])
        wring = [w[:, :].rearrange("p (k n) -> p k n", n=512) for w in wflat]
        wdring = [w[:, 0:FC * 128].rearrange("p (k n) -> p k n", n=128) for w in wflat]
        modrow = stage[0][0:NR, :]; badab = stage[1][0:NR, :]
        mtm = big[:, 0:4096].bitcast(F32).rearrange("p (j n) -> p j n", n=512)
        psum = [es.enter_context(nc.psum_tensor(f"ps{i}", [128, 512], F32)) for i in range(8)]
        print("sbuf bytes remaining/partition:", nc.sbuf_bytes_remaining)
        es.enter_context(nc.Block())

        PE = Eng(nc, "pe", nc.tensor); ACT = Eng(nc, "act", nc.scalar); DVE = Eng(nc, "dve", nc.vector)
        POOL = Eng(nc, "pool", nc.gpsimd); SP = Eng(nc, "sp", nc.sync)
        _bufs = {}

        def B(name):
            if name not in _bufs:
                _bufs[name] = Buf()
            return _bufs[name]
        _ds = {}

        def DS(name):
            if name not in _ds:
                _ds[name] = DSem(nc, "d_" + name)
            return _ds[name]

        ps_i = [0]

        def next_ps():
            i = ps_i[0] % 8
            ps_i[0] += 1
            return psum[i], B(f"ps{i}")

        st_i = [0]

        def next_stage():
            i = st_i[0] % 3
            st_i[0] += 1
            return stage[i], B(f"stage{i}"), DS(f"stage{i}")

        wr_i = [0]

        def load_w(src3, kc, ncols):
            i = wr_i[0] % 3
            wr_i[0] += 1
            POOL.dma(DS(f"wring{i}"), wring[i][:, 0:kc, 0:ncols], src3, writes=[B(f"wring{i}")])
            return wring[i], B(f"wring{i}")

        def wsrc(w, r0, nr, c0, ncols):
            return w[r0:r0 + nr, c0:c0 + ncols].rearrange("(k p) n -> p k n", p=128)

        ev_i = [0]

        def evac_eng():
            ev_i[0] += 1
            return ACT if ev_i[0] % 2 else DVE

        def copy_op(E, out, in_, scale=None):
            if E is ACT:
                if scale is None:
                    return lambda e: e.copy(out=out, in_=in_)
                return lambda e: e.mul(out=out, in_=in_, mul=scale)
            if scale is None:
                return lambda e: e.tensor_copy(out=out, in_=in_)
            return lambda e: e.tensor_scalar_mul(out=out, in0=in_, scalar1=scale)

        def mm(out, pairs):
            n = len(pairs)
            return [(lambda e, l=l, r=r, i=i: e.matmul(out, lhsT=l, rhs=r, start=(i == 0), stop=(i == n - 1)))
                    for i, (l, r) in enumerate(pairs)]

        POOL.op(lambda e: e.memset(ones_f[:], 1.0), writes=[B("ones_f")])
        POOL.op(lambda e: e.memset(ident_f[:], 0.0), writes=[B("ident_f")])
        POOL.op(lambda e: e.affine_select(out=ident_f[:], in_=ident_f[:], pattern=[[-1, 128]], compare_op=ALU.not_equal,
                                          fill=1.0, base=0, channel_multiplier=1), reads=[B("ident_f")], writes=[B("ident_f")])
        POOL.op(lambda e: e.affine_select(out=tri_f[:], in_=ones_f[:], pattern=[[1, 128]], compare_op=ALU.is_ge,
                                          fill=0.0, base=0, channel_multiplier=-1), reads=[B("ones_f")], writes=[B("tri_f")])
        POOL.op(lambda e: e.affine_select(out=elast_f[:], in_=ones_f[:], pattern=[[0, 128]], compare_op=ALU.is_equal,
                                          fill=0.0, base=-127, channel_multiplier=1), reads=[B("ones_f")], writes=[B("elast_f")])
        DVE.op(lambda e: e.tensor_copy(out=ident_b[:], in_=ident_f[:]), reads=[B("ident_f")], writes=[B("ident_b")])
        DVE.op(lambda e: e.tensor_copy(out=tri_b[:], in_=tri_f[:]), reads=[B("tri_f")], writes=[B("tri_b")])
        DVE.op(lambda e: e.tensor_copy(out=ones_b[:], in_=ones_f[:]), reads=[B("ones_f")], writes=[B("ones_b")])
        POOL.op(lambda e: e.iota(posS[:, :, 0], pattern=[[128, 16]], base=0, channel_multiplier=1,
                                 allow_small_or_imprecise_dtypes=True), writes=[B("posS")])
        for h in (1, 2, 3):
            DVE.op(lambda e, h=h: e.tensor_scalar_mul(out=posS[:, :, h], in0=posS[:, :, 0], scalar1=SLOPES[h]),
                   reads=[B("posS")], writes=[B("posS")])
        DVE.op(lambda e: e.tensor_scalar_mul(out=posS[:, :, 0], in0=posS[:, :, 0], scalar1=SLOPES[0]),
               reads=[B("posS")], writes=[B("posS")])
        POOL.op(lambda e: e.memset(Vf[:, :, :, 64:65], 1.0), writes=[B("Vf")])
        POOL.op(lambda e: e.memset(Vd[:, :, :, 128:129], 1.0), writes=[B("Vd")])
        POOL.op(lambda e: e.memset(lfacc[:], 0.0), writes=[B("lfacc")])
        SP.dma(DS("c0"), bfb, b_forget.partition_broadcast(128), writes=[B("bfb")])
        for i, v in enumerate((ln1g, ln1b, ln2g, ln2b)):
            SP.dma(DS("c0"), lng[:, i, :], v.partition_broadcast(128), writes=[B("lng")])
        SP.dma(DS("c0"), gsub, subln.partition_broadcast(128), writes=[B("gsub")])
        for i, v in enumerate((lq1, lk1, lq2, lk2)):
            SP.dma(DS("c0"), cst[:, 2 + i, 0:64], v.partition_broadcast(128), writes=[B("cstl")])
        for nm in ("bfb", "lng", "gsub", "cstl"):
            B(nm).w = (DS("c0").sem, DS("c0").count)
        DVE.op(lambda e: e.tensor_scalar_mul(out=gsub, in0=gsub, scalar1=1.0 - LAMBDA_INIT), reads=[B("gsub")], writes=[B("gsub")])
        DVE.op(lambda e: e.tensor_tensor(out=stage[2][:, 256:320], in0=cst[:, 2, 0:64], in1=cst[:, 3, 0:64], op=ALU.mult),
               reads=[B("cstl")], writes=[B("stage2")])
        DVE.op(lambda e: e.tensor_tensor(out=stage[2][:, 320:384], in0=cst[:, 4, 0:64], in1=cst[:, 5, 0:64], op=ALU.mult),
               reads=[B("cstl")], writes=[B("stage2")])
        DVE.op(lambda e: e.reduce_sum(out=small[:, 0:2], in_=stage[2][:, 256:384].rearrange("p (a b) -> p a b", b=64),
                                      axis=mybir.AxisListType.X), reads=[B("stage2")], writes=[B("small")])
        ACT.op(lambda e: e.activation(out=small[:, 2:4], in_=small[:, 0:2], func=AF.Exp), reads=[B("small")], writes=[B("small")])
        DVE.op(lambda e: e.tensor_tensor(out=small[:, 4:5], in0=small[:, 3:4], in1=small[:, 2:3], op=ALU.subtract),
               reads=[B("small")], writes=[B("small")])
        DVE.op(lambda e: e.tensor_scalar_add(out=lamc[:, 0:1], in0=small[:, 4:5], scalar1=-LAMBDA_INIT), reads=[B("small")], writes=[B("lamc")])

        SP.dma(DS("c1"), xs[0:NP, 0, :], cp, writes=[B("xs0")])
        SP.dma(DS("c1"), xs[NP:NR, 0, :], cs, writes=[B("xs0")])
        for k in range(KC):
            p_, pb_ = next_ps()
            PE.op(lambda e, k=k, p_=p_: e.transpose(out=p_[:, 0:NR], in_=xs[0:NR, 0, k * 128:(k + 1) * 128], identity=ident_f[0:NR, 0:NR]),
                  reads=[B("xs0"), B("ident_f")], writes=[pb_])
            DVE.op(lambda e, k=k, p_=p_: e.tensor_copy(out=cT[:, k, :], in_=p_[:, 0:NR]), reads=[pb_], writes=[B("cT")])
        for blk in range(12):
            wt, wb = load_w(wsrc(w_ada, 0, D, blk * 512, 512), KC, 512)
            SP.dma(DS("bada"), badab, b_ada[0:1, blk * 512:(blk + 1) * 512].partition_broadcast(NR), writes=[B("stage1")])
            p_, pb_ = next_ps()
            PE.group(mm(p_[0:NR, :], [(cT[:, k, :], wt[:, k, :]) for k in range(KC)]), reads=[B("cT"), wb], writes=[pb_])
            DVE.op(lambda e, p_=p_: e.tensor_tensor(out=modrow, in0=p_[0:NR, :], in1=badab, op=ALU.add),
                   reads=[pb_, B("stage1")], writes=[B("stage0")])
            if blk in (2, 3, 8, 9):
                DVE.op(lambda e: e.tensor_scalar_add(out=modrow, in0=modrow, scalar1=1.0), reads=[B("stage0")], writes=[B("stage0")])
            p2, pb2 = next_ps()
            for q in range(4):
                PE.op(lambda e, q=q, p2=p2: e.transpose(out=p2[:, q * NR:(q + 1) * NR], in_=modrow[:, q * 128:(q + 1) * 128],
                                                        identity=ident_f[0:NR, 0:NR]), reads=[B("stage0"), B("ident_f")], writes=[pb2])
            DVE.op(lambda e, blk=blk, p2=p2: e.tensor_copy(out=modT[:, blk * 4:(blk + 1) * 4, :],
                                                          in_=p2[:, 0:4 * NR].rearrange("p (q r) -> p q r", r=NR)),
                   reads=[pb2], writes=[B("modT")])
        SH1, SC1, G1, SH2, SC2, G2 = 0, 8, 16, 24, 32, 40

        POOL.dma(DS("wfa"), wfa, wsrc(w_in, 0, D, C_FA, 8), writes=[B("wfa")])

        def phase_c(NT, SAMPLE, mcol, mbc, y_rows, tiles, hbufs):
            ntok = NT * 128
            v3 = lambda ap: ap.rearrange("p (s q) -> p s q", q=8)

            def tm_block(wt, wb, j, ncols=512):
                p_, pb_ = next_ps()
                PE.group(mm(p_[:, 0:ncols], [(hT[:, k, j * 128:(j + 1) * 128], wt[:, k, 0:ncols]) for k in range(KC)]),
                         reads=hbufs + [wb], writes=[pb_])
                return p_, pb_
            oTb = [B(f"oT{k}") for k in range(KC)]

            def transpose8(src_fn, src_buf, j):
                for half in range(2):
                    p_, pb_ = next_ps()
                    PE.group([(lambda e, q=q, p_=p_: e.matmul(p_[:, q * 128:(q + 1) * 128], lhsT=src_fn(4 * half + q), rhs=ident_b[:],
                                                              start=True, stop=True)) for q in range(4)],
                             reads=[src_buf, B("ident_b")], writes=[pb_])
                    E = evac_eng()
                    E.op(copy_op(E, oT[:, 4 * half:4 * half + 4, j * 128:(j + 1) * 128], p_[:, :].rearrange("p (q t) -> p q t", t=128)),
                         reads=[pb_], writes=oTb[4 * half:4 * half + 4])

            for j in range(NT):
                transpose8(lambda q, j=j: oAB[:, j, q * 128:(q + 1) * 128], B(f"oAB{j}"), j)
            for n in range(2):
                i3 = wr_i[0] % 3
                wr_i[0] += 1
                wab_t, wab_b = wring[i3], B(f"wring{i3}")
                POOL.dma(DS(f"wring{i3}"), wab_t[:, 0:4, :], wsrc(w_ba, 0, 512, n * 512, 512), writes=[wab_b])
                POOL.dma(DS(f"wring{i3}"), wab_t[:, 4:8, :], wsrc(w_bb, 0, 512, n * 512, 512), writes=[wab_b])
                for br in range(2):
                    wga, wgab = load_w(wsrc(w_in, 0, D, (C_GA if br == 0 else C_GB) + n * 512, 512), KC, 512)
                    for j in range(NT):
                        mtb = [B(f"big{2 * j}"), B(f"big{2 * j + 1}")]
                        pm, pmb = next_ps()
                        PE.group(mm(pm[:, :], [(oT[:, 4 * br + q, j * 128:(j + 1) * 128], wab_t[:, 4 * br + q, :]) for q in range(4)]),
                                 reads=oTb + [wab_b], writes=[pmb])
                        pg, pgb = tm_block(wga, wgab, j)
                        ACT.op(lambda e, pg=pg, j=j: e.activation(out=tmpA[j % 2][:], in_=pg[:, :], func=AF.Sigmoid),
                               reads=[pgb], writes=[B(f"tmpA{j % 2}")])
                        if br == 0:
                            DVE.op(lambda e, pm=pm, j=j: e.tensor_tensor(out=mtm[:, j, :], in0=pm[:, :], in1=tmpA[j % 2][:], op=ALU.mult),
                                   reads=[pmb, B(f"tmpA{j % 2}")], writes=mtb)
                        else:
                            DVE.op(lambda e, pm=pm, j=j: e.tensor_tensor(out=tmpB[j % 2][:], in0=pm[:, :], in1=tmpA[j % 2][:], op=ALU.mult),
                                   reads=[pmb, B(f"tmpA{j % 2}")], writes=[B(f"tmpB{j % 2}")])
                            DVE.op(lambda e, j=j, n=n: e.tensor_tensor(out=oAB[:, j, n * 512:(n + 1) * 512], in0=mtm[:, j, :], in1=tmpB[j % 2][:], op=ALU.add),
                                   reads=mtb + [B(f"tmpB{j % 2}")], writes=[B(f"oAB{j}")])
            for j in range(NT):
                transpose8(lambda q, j=j: oAB[:, j, q * 128:(q + 1) * 128], B(f"oAB{j}"), j)

            def lin_back(get_w, nk, rhs_fn, rhs_bufs, gidx):
                for nchunk in range(8):
                    wt, wb, wsel = get_w(nchunk)
                    p_, pb_ = next_ps()
                    PE.group(mm(p_[:, 0:ntok], [(wsel(wt, k), rhs_fn(k)) for k in range(nk)]), reads=rhs_bufs + [wb], writes=[pb_])
                    tb = tmpA[nchunk % 2]
                    tbb = B(f"tmpA{nchunk % 2}")
                    if not SAMPLE:
                        ACT.op(lambda e, p_=p_, tb=tb, nchunk=nchunk: e.activation(out=tb[:, 0:ntok], in_=p_[:, 0:ntok], func=AF.Identity,
                                                                                    scale=mcol(gidx + nchunk)),
                               reads=[pb_, B("modT")], writes=[tbb])
                    else:
                        DVE.op(lambda e, p_=p_, tb=tb, nchunk=nchunk: e.tensor_tensor(out=v3(tb[:, 0:128]), in0=v3(p_[:, 0:128]),
                                                                                       in1=mbc(gidx + nchunk), op=ALU.mult),
                               reads=[pb_, B("modT")], writes=[tbb])
                    p2, pb2 = next_ps()
                    for j in range(NT):
                        PE.op(lambda e, j=j, p2=p2, tb=tb: e.transpose(out=p2[:, j * 128:(j + 1) * 128], in_=tb[:, j * 128:(j + 1) * 128],
                                                                       identity=ident_f[:]), reads=[tbb, B("ident_f")], writes=[pb2])
                    for j in range(NT):
                        DVE.op(lambda e, j=j, p2=p2, nchunk=nchunk: e.scalar_tensor_tensor(
                            out=xs[:, j, nchunk * 128:(nchunk + 1) * 128], in0=xs[:, j, nchunk * 128:(nchunk + 1) * 128], scalar=ALPHA,
                            in1=p2[:, j * 128:(j + 1) * 128], op0=ALU.mult, op1=ALU.add), reads=[pb2, B(f"xs{j}")], writes=[B(f"xs{j}")])

            def layer_norm(j, gi):
                xb = B(f"xs{j}")
                for s_ in range(2):
                    DVE.op(lambda e, s_=s_: e.bn_stats(out=bnst[:, s_ * 6:(s_ + 1) * 6], in_=xs[:, j, s_ * 512:(s_ + 1) * 512]), reads=[xb], writes=[B("bnst")])
                DVE.op(lambda e: e.bn_aggr(out=stat[:, 13:15], in_=bnst[:]), reads=[B("bnst")], writes=[B("lnstat")])
                ACT.op(lambda e: e.activation(out=stat[:, 15:16], in_=stat[:, 14:15], func=AF.Sqrt, bias=LN_EPS, scale=1.0),
                       reads=[B("lnstat")], writes=[B("lnstat2")])
                DVE.op(lambda e: e.reciprocal(out=stat[:, 15:16], in_=stat[:, 15:16]), reads=[B("lnstat2")], writes=[B("lnstat2")])
                DVE.op(lambda e: e.tensor_scalar(out=xs[:, j, :], in0=xs[:, j, :], scalar1=stat[:, 13:14], scalar2=stat[:, 15:16],
                                                 op0=ALU.subtract, op1=ALU.mult), reads=[xb, B("lnstat"), B("lnstat2")], writes=[xb])
                DVE.op(lambda e: e.tensor_tensor(out=xs[:, j, :], in0=xs[:, j, :], in1=lng[:, gi, :], op=ALU.mult),
                       reads=[xb, B("lng")], writes=[xb])
                DVE.op(lambda e: e.tensor_tensor(out=xs[:, j, :], in0=xs[:, j, :], in1=lng[:, gi + 1, :], op=ALU.add),
                       reads=[xb, B("lng")], writes=[xb])

            if STOP == 'C3':
                return
            cur = [None]

            def get_wout(nchunk):
                if nchunk % 4 == 0:
                    cur[0] = load_w(wsrc(w_out, 0, D, nchunk * 128, 512), KC, 512)
                wt, wb = cur[0]
                c0 = (nchunk % 4) * 128
                return wt, wb, (lambda wt, k: wt[:, k, c0:c0 + 128])
            lin_back(get_wout, KC, lambda k: oT[:, k, 0:ntok], oTb, G1)
            for j in range(NT):
                layer_norm(j, 0)
            for k in range(KC):
                p_, pb_ = next_ps()
                for j in range(NT):
                    PE.op(lambda e, j=j, k=k, p_=p_: e.transpose(out=p_[:, j * 128:(j + 1) * 128], in_=xs[:, j, k * 128:(k + 1) * 128],
                                                                 identity=ident_f[:]), reads=[B(f"xs{j}"), B("ident_f")], writes=[pb_])
                if not SAMPLE:
                    ACT.op(lambda e, k=k, p_=p_: e.activation(out=oT[:, k, 0:ntok], in_=p_[:, 0:ntok], func=AF.Identity,
                                                              bias=mcol(SH2 + k), scale=mcol(SC2 + k)),
                           reads=[pb_, B("modT")], writes=[B(f"oT{k}")])
                else:
                    DVE.op(lambda e, k=k, p_=p_: e.tensor_tensor(out=v3(tmpA[0][:, 0:128]), in0=v3(p_[:, 0:128]), in1=mbc(SC2 + k), op=ALU.mult),
                           reads=[pb_, B("modT")], writes=[B("tmpA0")])
                    DVE.op(lambda e, k=k: e.tensor_tensor(out=v3(oT[:, k, 0:128]), in0=v3(tmpA[0][:, 0:128]), in1=mbc(SH2 + k), op=ALU.add),
                           reads=[B("tmpA0"), B("modT")], writes=[B(f"oT{k}")])
            if STOP == 'C6':
                return
            aT = big[:, :].rearrange("p (f t) -> p f t", t=512)
            for blk in range(11):
                i3 = wr_i[0] % 3
                wr_i[0] += 1
                wt, wb = wring[i3], B(f"wring{i3}")
                POOL.dma(DS(f"wring{i3}"), wt[:, :, 0:256], wsrc(w_fg, 0, D, blk * 256, 256), writes=[wb])
                POOL.dma(DS(f"wring{i3}"), wt[:, :, 256:512], wsrc(w_fu, 0, D, blk * 256, 256), writes=[wb])
                for s_ in range(2):
                    fc = blk * 2 + s_
                    pg, pgb = next_ps()
                    PE.group(mm(pg[:, 0:ntok], [(wt[:, k, s_ * 128:(s_ + 1) * 128], oT[:, k, 0:ntok]) for k in range(KC)]), reads=oTb + [wb], writes=[pgb])
                    pu, pub = next_ps()
                    PE.group(mm(pu[:, 0:ntok], [(wt[:, k, 256 + s_ * 128:256 + (s_ + 1) * 128], oT[:, k, 0:ntok]) for k in range(KC)]), reads=oTb + [wb], writes=[pub])
                    tb = tmpB[fc % 2]
                    ACT.op(lambda e, pg=pg, tb=tb: e.activation(out=tb[:, 0:ntok], in_=pg[:, 0:ntok], func=AF.Silu), reads=[pgb], writes=[B(f"tmpB{fc % 2}")])
                    DVE.op(lambda e, pu=pu, tb=tb, fc=fc: e.tensor_tensor(out=aT[:, fc, 0:ntok], in0=pu[:, 0:ntok], in1=tb[:, 0:ntok], op=ALU.mult),
                           reads=[pub, B(f"tmpB{fc % 2}")], writes=[B(f"big{fc}")])
            def get_wd(nchunk):
                i3 = wr_i[0] % 3
                wr_i[0] += 1
                POOL.dma(DS(f"wring{i3}"), wdring[i3], w_fd[:, nchunk * 128:(nchunk + 1) * 128].rearrange("(k p) n -> p k n", p=128),
                         writes=[B(f"wring{i3}")])
                return wdring[i3], B(f"wring{i3}"), (lambda wt, k: wt[:, k, :])
            lin_back(get_wd, FC, lambda k: aT[:, k, 0:ntok], [B(f"big{q}") for q in range(FC)], G2)
            for j, i in enumerate(tiles):
                layer_norm(j, 2)
                SP.dma(DS(f"x{j}"), y_rows(i), xs[:, j, :], reads=[B(f"xs{j}")])


        def chunk(c, pi):
            NT = 4
            ntok = 512
            tiles = [4 * c + j for j in range(4)]
            tok0 = 512 * c
            R0 = pi * T
            mcol = lambda idx: modT[:, idx, pi:pi + 1]
            if c == 0:
                DVE.op(lambda e: e.memset(lfacc[:], 0.0), writes=[B("lfacc")])
            for j, i in enumerate(tiles):
                SP.dma(DS(f"x{j}"), xs[:, j, :], xp[R0 + i * 128:R0 + (i + 1) * 128, :], writes=[B(f"xs{j}")])
            for k in range(KC):
                p_, pb_ = next_ps()
                for j in range(4):
                    PE.op(lambda e, j=j, k=k, p_=p_: e.transpose(out=p_[:, j * 128:(j + 1) * 128], in_=xs[:, j, k * 128:(k + 1) * 128],
                                                                 identity=ident_f[:]), reads=[B(f"xs{j}"), B("ident_f")], writes=[pb_])
                ACT.op(lambda e, k=k, p_=p_: e.activation(out=hT[:, k, :], in_=p_[:, :], func=AF.Identity,
                                                          bias=mcol(SH1 + k), scale=mcol(SC1 + k)),
                       reads=[pb_, B("modT")], writes=[B(f"hT{k}")])
            hbufs = [B(f"hT{k}") for k in range(KC)]
            if STOP == 'A1':
                return

            def fm_block(wt, wb, col0, dst, dbuf, scale):
                p_, pb_ = next_ps()
                PE.group(mm(p_[:, :], [(wt[:, k, col0:col0 + 128], hT[:, k, :]) for k in range(KC)]), reads=hbufs + [wb], writes=[pb_])
                E = evac_eng()
                E.op(copy_op(E, dst, p_[:, :], scale), reads=[pb_], writes=[dbuf])

            def tm_block(wt, wb, j, ncols=512):
                p_, pb_ = next_ps()
                PE.group(mm(p_[:, 0:ncols], [(hT[:, k, j * 128:(j + 1) * 128], wt[:, k, 0:ncols]) for k in range(KC)]),
                         reads=hbufs + [wb], writes=[pb_])
                return p_, pb_

            wt, wb = load_w(wsrc(w_in, 0, D, C_QA, 512), KC, 512)
            for hp in range(4):
                fm_block(wt, wb, hp * 128, QTf[:, hp, :], B(f"oT{hp}"), 0.125)
            wt, wb = load_w(wsrc(w_in, 0, D, C_KA, 512), KC, 512)
            for hp in range(4):
                fm_block(wt, wb, hp * 128, KTf[:, hp, tok0:tok0 + 512], B(f"KTf{hp}"), None)
            for j, i in enumerate(tiles):
                p_, pb_ = tm_block(wt, wb, j)
                st, sbuf_, sds = next_stage()
                E = evac_eng()
                E.op(copy_op(E, st[:], p_[:, :]), reads=[pb_], writes=[sbuf_])
                SP.dma(sds, kfp[R0 + i * 128:R0 + (i + 1) * 128, :], st[:], reads=[sbuf_])
            if STOP == 'A2':
                return
            wt, wb = load_w(wsrc(w_in, 0, D, C_VA, 512), KC, 512)
            for j, i in enumerate(tiles):
                p_, pb_ = tm_block(wt, wb, j)
                st, sbuf_, sds = next_stage()
                ACT.op(copy_op(ACT, st[:], p_[:, :]), reads=[pb_], writes=[sbuf_])
                DVE.op(lambda e, i=i, st=st: e.tensor_copy(out=Vf[:, i, :, 0:64], in_=st[:, :].rearrange("p (h d) -> p h d", d=64)),
                       reads=[sbuf_], writes=[B("Vf")])
                SP.dma(sds, vfp[R0 + i * 128:R0 + (i + 1) * 128, :], st[:], reads=[sbuf_])
            if STOP == 'A3':
                return
            for j, i in enumerate(tiles):
                p_, pb_ = next_ps()
                PE.group(mm(p_[:, 0:8], [(hT[:, k, j * 128:(j + 1) * 128], wfa[:, k, :]) for k in range(KC)]),
                         reads=hbufs + [B("wfa")], writes=[pb_])
                DVE.op(lambda e, p_=p_: e.tensor_tensor(out=fa_t[:], in0=p_[:, 0:8], in1=bfb, op=ALU.add),
                       reads=[pb_, B("bfb")], writes=[B("fa_t")])
                ACT.op(lambda e: e.activation(out=fa_t[:], in_=fa_t[:], func=AF.Exp, scale=-1.0), reads=[B("fa_t")], writes=[B("fa_t")])
                ACT.op(lambda e: e.activation(out=fa_t[:], in_=fa_t[:], func=AF.Ln, bias=1.0, scale=1.0), reads=[B("fa_t")], writes=[B("fa_t")])
                DVE.op(lambda e, j=j: e.tensor_scalar_mul(out=lf_t[:, j, :], in0=fa_t[:], scalar1=-1.0), reads=[B("fa_t")], writes=[B(f"lf{j}")])
                SP.dma(DS(f"lf{j}"), lfp[R0 + i * 128:R0 + (i + 1) * 128, :], lf_t[:, j, :], reads=[B(f"lf{j}")])
                p2, pb2 = next_ps()
                PE.group(mm(p2[:, 0:8], [(tri_f[:], lf_t[:, j, :]), (ones_f[:], lfacc[:])]),
                         reads=[B(f"lf{j}"), B("lfacc"), B("tri_f"), B("ones_f")], writes=[pb2])
                DVE.op(lambda e, i=i, p2=p2: e.tensor_copy(out=c_all[:, i, :], in_=p2[:, 0:8]), reads=[pb2], writes=[B("c_all")])
                DVE.op(lambda e, j=j: e.tensor_tensor(out=lfacc[:], in0=lfacc[:], in1=lf_t[:, j, :], op=ALU.add),
                       reads=[B("lfacc"), B(f"lf{j}")], writes=[B("lfacc")])
            if STOP == 'A4':
                return
            wt, wb = load_w(wsrc(w_in, 0, D, C_QB, 512), KC, 512)
            for h in range(4):
                fm_block(wt, wb, h * 128, QTd[:, h, :], B(f"oT{4 + h}"), 0.125)
            wt, wb = load_w(wsrc(w_in, 0, D, C_KB, 512), KC, 512)
            for h in range(4):
                fm_block(wt, wb, h * 128, KTd[:, h, tok0:tok0 + 512], B(f"KTd{h}"), None)
            for j, i in enumerate(tiles):
                p_, pb_ = tm_block(wt, wb, j)
                st, sbuf_, sds = next_stage()
                E = evac_eng()
                E.op(copy_op(E, st[:], p_[:, :]), reads=[pb_], writes=[sbuf_])
                SP.dma(sds, kdp[R0 + i * 128:R0 + (i + 1) * 128, :], st[:], reads=[sbuf_])
            wt, wb = load_w(wsrc(w_in, 0, D, C_VB, 512), KC, 512)
            for j, i in enumerate(tiles):
                p_, pb_ = tm_block(wt, wb, j)
                st, sbuf_, sds = next_stage()
                ACT.op(copy_op(ACT, st[:], p_[:, :]), reads=[pb_], writes=[sbuf_])
                DVE.op(lambda e, i=i, st=st: e.tensor_copy(out=Vd[:, i, :, 0:128], in_=st[:, :].rearrange("p (h d) -> p h d", d=128)),
                       reads=[sbuf_], writes=[B("Vd")])
                SP.dma(sds, vdp[R0 + i * 128:R0 + (i + 1) * 128, :], st[:], reads=[sbuf_])

            if STOP == 'A':
                return
            nkb = 4 * c + 4
            p_, pb_ = next_ps()
            PE.op(lambda e, p_=p_: e.matmul(p_[:, 0:8], lhsT=elast_f[:], rhs=c_all[:, 4 * c + 1, :], start=True, stop=True),
                  reads=[B("elast_f"), B("c_all")], writes=[pb_])
            DVE.op(lambda e, p_=p_: e.tensor_copy(out=rc_bc[:], in_=p_[:, 0:8]), reads=[pb_], writes=[B("rc_bc")])
            DVE.op(lambda e: e.tensor_tensor(out=biasF[:, 0:nkb, :], in0=rc_bc[:].unsqueeze(1).to_broadcast([128, nkb, 8]),
                                             in1=c_all[:, 0:nkb, :], op=ALU.subtract),
                   reads=[B("rc_bc"), B("c_all")], writes=[B("biasF")])
            for h in range(4):
                DVE.op(lambda e, h=h: e.tensor_scalar_add(out=biasD[:, 0:nkb, h], in0=posS[:, 0:nkb, h],
                                                          scalar1=-SLOPES[h] * (512 * c + 256)),
                       reads=[B("posS")], writes=[B("biasD")])
            PT = big[:, 0:16 * 512].rearrange("p (k q) -> p k q", q=512)

            def scores(KT, ktb, QT, qtb, kslot, r0, bias_ap, bias_buf):
                for kb in range(nkb):
                    j0 = max(0, kb - 4 * c)
                    nq = (4 - j0) * 128
                    p_, pb_ = next_ps()
                    PE.op(lambda e, p_=p_, kb=kb, j0=j0, nq=nq: e.matmul(
                        p_[:, 0:nq], lhsT=KT[r0:r0 + 64, kslot, kb * 128:(kb + 1) * 128], rhs=QT[r0:r0 + 64, kslot, j0 * 128:512],
                        start=True, stop=True), reads=[ktb, qtb], writes=[pb_])
                    ACT.op(lambda e, p_=p_, kb=kb, j0=j0, nq=nq: e.activation(
                        out=PT[:, kb, j0 * 128:512], in_=p_[:, 0:nq], func=AF.Exp, bias=bias_ap(kb), scale=1.0),
                        reads=[pb_, bias_buf], writes=[B(f"big{kb}")])
                    if kb >= 4 * c:
                        DVE.op(lambda e, kb=kb, j0=j0: e.tensor_tensor(out=PT[:, kb, j0 * 128:(j0 + 1) * 128],
                                                                       in0=PT[:, kb, j0 * 128:(j0 + 1) * 128], in1=tri_b[:], op=ALU.mult),
                               reads=[B(f"big{kb}"), B("tri_b")], writes=[B(f"big{kb}")])

            def pv(j, V, vbuf, h, ncol):
                i = 4 * c + j
                p_, pb_ = next_ps()
                PE.group(mm(p_[:, 0:ncol], [(PT[:, kb, j * 128:(j + 1) * 128], V[:, kb, h, :]) for kb in range(i + 1)]),
                         reads=[B(f"big{kb}") for kb in range(i + 1)] + [vbuf], writes=[pb_])
                return p_, pb_

            for h in range(8):
                hp, r0 = h // 2, (h % 2) * 64
                scores(KTf, B(f"KTf{hp}"), QTf, B(f"oT{hp}"), hp, r0, lambda kb, h=h: biasF[:, kb, h:h + 1], B("biasF"))
                for j in range(4):
                    p_, pb_ = pv(j, Vf, B("Vf"), h, 65)
                    DVE.op(lambda e, p_=p_: e.reciprocal(out=stat[:, 0:1], in_=p_[:, 64:65]), reads=[pb_], writes=[B("stat")])
                    DVE.op(lambda e, p_=p_, j=j, h=h: e.tensor_scalar_mul(out=oAB[:, j, h * 64:(h + 1) * 64], in0=p_[:, 0:64], scalar1=stat[:, 0:1]),
                           reads=[pb_, B("stat")], writes=[B(f"oAB{j}")])
            for h in range(4):
                for m in range(2):
                    scores(KTd, B(f"KTd{h}"), QTd, B(f"oT{4 + h}"), h, m * 64, lambda kb, h=h: biasD[:, kb, h:h + 1], B("biasD"))
                    for j in range(4):
                        p_, pb_ = pv(j, Vd, B("Vd"), h, 129)
                        if m == 0:
                            DVE.op(lambda e, p_=p_, j=j: e.reciprocal(out=stat[:, 4 + j:5 + j], in_=p_[:, 128:129]), reads=[pb_], writes=[B("statd")])
                            DVE.op(lambda e, p_=p_, j=j: e.tensor_scalar_mul(out=d1buf[:, j, :], in0=p_[:, 0:128], scalar1=stat[:, 4 + j:5 + j]),
                                   reads=[pb_, B("statd")], writes=[B(f"d1_{j}")])
                        else:
                            DVE.op(lambda e, p_=p_: e.reciprocal(out=stat[:, 8:9], in_=p_[:, 128:129]), reads=[pb_], writes=[B("stat2")])
                            DVE.op(lambda e: e.tensor_tensor(out=stat[:, 9:10], in0=stat[:, 8:9], in1=lamc[:, 0:1], op=ALU.mult),
                                   reads=[B("stat2"), B("lamc")], writes=[B("stat2")])
                            DVE.op(lambda e, p_=p_, j=j: e.scalar_tensor_tensor(out=dtmp[:, 0, :], in0=p_[:, 0:128], scalar=stat[:, 9:10],
                                                                                 in1=d1buf[:, j, :], op0=ALU.mult, op1=ALU.add),
                                   reads=[pb_, B("stat2"), B(f"d1_{j}")], writes=[B("dtmp0")])
                            ACT.op(lambda e: e.activation(out=dtmp[:, 1, :], in_=dtmp[:, 0, :], func=AF.Square),
                                   reads=[B("dtmp0")], writes=[B("dtmp1")])
                            DVE.op(lambda e: e.reduce_sum(out=stat[:, 10:11], in_=dtmp[:, 1, :], axis=mybir.AxisListType.X),
                                   reads=[B("dtmp1")], writes=[B("stat3")])
                            ACT.op(lambda e: e.activation(out=stat[:, 11:12], in_=stat[:, 10:11], func=AF.Sqrt, bias=LN_EPS, scale=1.0 / 128),
                                   reads=[B("stat3")], writes=[B("stat3")])
                            DVE.op(lambda e: e.reciprocal(out=stat[:, 12:13], in_=stat[:, 11:12]), reads=[B("stat3")], writes=[B("stat3")])
                            DVE.op(lambda e, j=j, h=h: e.scalar_tensor_tensor(out=oAB[:, j, 512 + h * 128:512 + (h + 1) * 128], in0=dtmp[:, 0, :],
                                                                               scalar=stat[:, 12:13], in1=gsub, op0=ALU.mult, op1=ALU.mult),
                                   reads=[B("dtmp0"), B("stat3"), B("gsub")], writes=[B(f"oAB{j}")])

            if STOP == 'B':
                return
            phase_c(4, False, mcol, None, lambda i: yp[R0 + i * 128:R0 + (i + 1) * 128, :], tiles, hbufs)

        idx_all = stage[3][:, :].bitcast(I32)
        SP.dma(DS("pts"), idx_all[:, 0:NS * 256], pt.partition_broadcast(128), writes=[B("stage3")])
        DVE.op(lambda e: e.tensor_copy(out=stage[2][:, 0:NS * 256], in_=idx_all[:, 0:NS * 256]), reads=[B("stage3")], writes=[B("stage2")])
        POOL.op(lambda e: e.iota(small[:, 20:21], pattern=[[0, 1]], base=0, channel_multiplier=1, allow_small_or_imprecise_dtypes=True),
                writes=[B("small20")])
        DVE.op(lambda e: e.tensor_scalar(out=stage[2][:, 0:NS * 256], in0=stage[2][:, 0:NS * 256], scalar1=128.0, scalar2=small[:, 20:21],
                                         op0=ALU.mult, op1=ALU.add), reads=[B("stage2"), B("small20")], writes=[B("stage2")])
        DVE.op(lambda e: e.tensor_copy(out=idx_all[:, 0:NS * 256], in_=stage[2][:, 0:NS * 256]), reads=[B("stage2")], writes=[B("stage3")])
        ptT = cst[:, 6, :].bitcast(I32)
        with nc.allow_non_contiguous_dma(reason="tiny page-table transpose"):
            SP.dma(DS("ptT"), ptT[:, 0:NS * 2], pt.rearrange("o (c p) -> p (o c)", p=128), writes=[B("ptT")])

        def gather(dsem, out, src2d, idx_col):
            ins = nc.gpsimd.indirect_dma_start(out=out, out_offset=None, in_=src2d,
                                               in_offset=bass.IndirectOffsetOnAxis(ap=idx_col, axis=0))
            dsem.count += 16
            ins.then_inc(dsem.sem, 16)
        maskS_f = sb("maskS_f", [128, 128]); maskS_b = sb("maskS_b", [128, 128], BF16); MB_f = sb("MB_f", [128, 128])
        alibS = sb("alibS", [128, 16, 8]); alibN = sb("alibN", [128, 16]); negcs = sb("negcs", [128, 8])
        m3 = maskS_f[:, :].rearrange("p (b q) -> p b q", q=8)
        POOL.op(lambda e: e.affine_select(out=m3, in_=ones_f[:, :].rearrange("p (b q) -> p b q", q=8), pattern=[[-8, 16], [0, 8]],
                                          compare_op=ALU.is_ge, fill=0.0, base=0, channel_multiplier=1), reads=[B("ones_f")], writes=[B("maskS")])
        POOL.op(lambda e: e.affine_select(out=m3, in_=m3, pattern=[[8, 16], [1, 8]], compare_op=ALU.is_ge, fill=0.0, base=0,
                                          channel_multiplier=-1), reads=[B("maskS")], writes=[B("maskS")])
        DVE.op(lambda e: e.tensor_copy(out=maskS_b[:], in_=maskS_f[:]), reads=[B("maskS")], writes=[B("maskSb")])
        mb3 = MB_f[:, :].rearrange("p (b w) -> p b w", w=16)
        POOL.op(lambda e: e.affine_select(out=mb3, in_=ones_f[:, :].rearrange("p (b w) -> p b w", w=16), pattern=[[-16, 8], [-1, 16]],
                                          compare_op=ALU.is_gt, fill=0.0, base=0, channel_multiplier=1), reads=[B("ones_f")], writes=[B("MB")])
        POOL.op(lambda e: e.affine_select(out=mb3, in_=mb3, pattern=[[16, 8], [0, 16]], compare_op=ALU.is_ge, fill=0.0, base=15,
                                          channel_multiplier=-1), reads=[B("MB")], writes=[B("MB")])
        for hm in range(8):
            DVE.op(lambda e, hm=hm: e.tensor_scalar_add(out=alibS[:, :, hm], in0=posS[:, :, hm // 2], scalar1=-2048.0 * SLOPES[hm // 2]),
                   reads=[B("posS")], writes=[B("alibS")])
        DVE.op(lambda e: e.reduce_sum(out=alibN[:, 8:9], in_=maskS_f[:, :], axis=mybir.AxisListType.X), reads=[B("maskS")], writes=[B("alibN")])
        DVE.op(lambda e: e.tensor_scalar(out=alibN[:, 9:10], in0=alibN[:, 8:9], scalar1=-1.0, scalar2=8.0, op0=ALU.mult, op1=ALU.add),
               reads=[B("alibN")], writes=[B("alibN")])
        for hm in range(8):
            DVE.op(lambda e, hm=hm: e.tensor_scalar_mul(out=alibN[:, hm:hm + 1], in0=alibN[:, 9:10], scalar1=SLOPES[hm // 2]),
                   reads=[B("alibN")], writes=[B("alibN")])
        QBD = oAB[:, 1:3, :].rearrange("p a (k s c) -> p (a k) s c", s=16, c=16)
        biasAll = xs[:, 1:3, :].rearrange("p a (h b g) -> p (a h) b g", b=16, g=16)
        Lst = xs[:, 3, :]
        Pf = big[:, 0:2048].bitcast(F32).rearrange("p (r h) -> p h r", h=8)
        KTs = big[:, 0:4096].rearrange("p (k n) -> p k n", n=512)
        PTs = big[:, 4096:4096 + 17 * 128].rearrange("p (k n) -> p k n", n=128)
        PTb = [B(f"big{q}") for q in range(8, 13)]
        KTsb = [B(f"big{q}") for q in range(8)]
        osb = tmpB[0][:, :].bitcast(BF16)
        ckf3 = ckf.rearrange("(n p) c -> n p c", p=128); cvf3 = cvf.rearrange("(n p) c -> n p c", p=128)
        ckd3 = ckd.rearrange("(n p) c -> n p c", p=128); cvd3 = cvd.rearrange("(n p) c -> n p c", p=128)
        v3 = lambda ap: ap.rearrange("p (s q) -> p s q", q=8)
        clf3 = clf.rearrange("n (a c) -> n a c", a=1)

        def sample_tile(ts):
            r0 = NP + 16 * ts
            hbufs = [B(f"hT{k}") for k in range(KC)]
            mbc = lambda idx: modT[:, idx, r0:r0 + 16].unsqueeze(2).to_broadcast([128, 16, 8])
            rows = slice(ts * 128, (ts + 1) * 128)
            SP.dma(DS("x0"), xs[:, 0, :], xs_in[rows, :], writes=[B("xs0")])
            for k in range(KC):
                p_, pb_ = next_ps()
                PE.op(lambda e, k=k, p_=p_: e.transpose(out=p_[:, 0:128], in_=xs[:, 0, k * 128:(k + 1) * 128], identity=ident_f[:]),
                      reads=[B("xs0"), B("ident_f")], writes=[pb_])
                DVE.op(lambda e, k=k, p_=p_: e.tensor_tensor(out=v3(tmpA[0][:, 0:128]), in0=v3(p_[:, 0:128]), in1=mbc(SC1 + k), op=ALU.mult),
                       reads=[pb_, B("modT")], writes=[B("tmpA0")])
                DVE.op(lambda e, k=k: e.tensor_tensor(out=v3(hT[:, k, 0:128]), in0=v3(tmpA[0][:, 0:128]), in1=mbc(SH1 + k), op=ALU.add),
                       reads=[B("tmpA0"), B("modT")], writes=[B(f"hT{k}")])
            if ts == 0:
                DVE.op(lambda e: e.memset(oAB[:, 1:3, :], 0.0), writes=[B("oAB1"), B("oAB2")])
            qb = [B("oAB1"), B("oAB2")]

            def fm(wt, wb, col0):
                p_, pb_ = next_ps()
                PE.group(mm(p_[:, 0:128], [(wt[:, k, col0:col0 + 128], hT[:, k, 0:128]) for k in range(KC)]), reads=hbufs + [wb], writes=[pb_])
                return p_, pb_

            def tmj(wt, wb, ncols=512):
                p_, pb_ = next_ps()
                PE.group(mm(p_[:, 0:ncols], [(hT[:, k, 0:128], wt[:, k, 0:ncols]) for k in range(KC)]), reads=hbufs + [wb], writes=[pb_])
                return p_, pb_

            def q_block(col, blk0):
                wt, wb = load_w(wsrc(w_in, 0, D, col, 512), KC, 512)
                for q in range(4):
                    p_, pb_ = fm(wt, wb, q * 128)
                    DVE.op(lambda e, p_=p_, q=q: e.tensor_scalar_mul(out=QBD[0:64, blk0 + q, :, 0:8], in0=v3(p_[0:64, 0:128]), scalar1=0.125),
                           reads=[pb_], writes=qb)
                    DVE.op(lambda e, p_=p_, q=q: e.tensor_scalar_mul(out=QBD[64:128, blk0 + q, :, 8:16], in0=v3(p_[64:128, 0:128]), scalar1=0.125),
                           reads=[pb_], writes=qb)

            def k_block(col, KT, nm, oap):
                wt, wb = load_w(wsrc(w_in, 0, D, col, 512), KC, 512)
                for q in range(4):
                    p_, pb_ = fm(wt, wb, q * 128)
                    E = evac_eng()
                    E.op(copy_op(E, KT[:, q, T:T + 128], p_[:, 0:128]), reads=[pb_], writes=[B(f"{nm}{q}")])
                p_, pb_ = tmj(wt, wb)
                st, sbuf_, sds = next_stage()
                E = evac_eng()
                E.op(copy_op(E, st[:], p_[:, :]), reads=[pb_], writes=[sbuf_])
                SP.dma(sds, oap[rows, :], st[:], reads=[sbuf_])

            def v_block(col, V, vb, hd, oap):
                wt, wb = load_w(wsrc(w_in, 0, D, col, 512), KC, 512)
                p_, pb_ = tmj(wt, wb)
                st, sbuf_, sds = next_stage()
                ACT.op(copy_op(ACT, st[:], p_[:, :]), reads=[pb_], writes=[sbuf_])
                DVE.op(lambda e, st=st: e.tensor_copy(out=V[:, 16, :, 0:hd], in_=st[:, :].rearrange("p (h d) -> p h d", d=hd)),
                       reads=[sbuf_], writes=[vb])
                SP.dma(sds, oap[rows, :], st[:], reads=[sbuf_])

            q_block(C_QA, 0)
            k_block(C_KA, KTf, "KTf", kfs)
            v_block(C_VA, Vf, B("Vf"), 64, vfs)
            p_, pb_ = next_ps()
            PE.group(mm(p_[:, 0:8], [(hT[:, k, 0:128], wfa[:, k, :]) for k in range(KC)]), reads=hbufs + [B("wfa")], writes=[pb_])
            DVE.op(lambda e: e.tensor_tensor(out=fa_t[:], in0=p_[:, 0:8], in1=bfb, op=ALU.add), reads=[pb_, B("bfb")], writes=[B("fa_t")])
            ACT.op(lambda e: e.activation(out=fa_t[:], in_=fa_t[:], func=AF.Exp, scale=-1.0), reads=[B("fa_t")], writes=[B("fa_t")])
            ACT.op(lambda e: e.activation(out=fa_t[:], in_=fa_t[:], func=AF.Ln, bias=1.0, scale=1.0), reads=[B("fa_t")], writes=[B("fa_t")])
            DVE.op(lambda e: e.tensor_scalar_mul(out=lf_t[:, 0, :], in0=fa_t[:], scalar1=-1.0), reads=[B("fa_t")], writes=[B("lf0")])
            SP.dma(DS("lf0"), lfs[rows, :], lf_t[:, 0, :], reads=[B("lf0")])
            p2, pb2 = next_ps()
            PE.op(lambda e: e.matmul(p2[:, 0:8], lhsT=maskS_f[:], rhs=lf_t[:, 0, :], start=True, stop=True),
                  reads=[B("maskS"), B("lf0")], writes=[pb2])
            DVE.op(lambda e: e.tensor_scalar_mul(out=negcs[:], in0=p2[:, 0:8], scalar1=-1.0), reads=[pb2], writes=[B("negcs")])
            q_block(C_QB, 4)
            k_block(C_KB, KTd, "KTd", kds)
            v_block(C_VB, Vd, B("Vd"), 128, vds)
            if STOP == 'SA':
                return

            for hf in range(2):
                dsl = DS("lg")
                POOL._wait(Eng._deps([B("ptT")], [B("xs3")]))
                gather(dsl, Lst[:, :], clf, ptT[:, ts * 2 + hf:ts * 2 + hf + 1])
                B("xs3").w = (dsl.sem, dsl.count)
                B("xs3").r = {}
                L3 = Lst.rearrange("p (r h) -> p h r", h=8)
                for h in range(8):
                    DVE.op(lambda e, h=h: e.tensor_tensor_scan(out=Pf[:, h, :], data0=ones_f[:, :], data1=L3[:, h, :], initial=0.0,
                                                               op0=ALU.mult, op1=ALU.add), reads=[B("xs3"), B("ones_f")], writes=[B(f"big{h // 2}")])
                pfb = [B(f"big{q}") for q in range(4)]
                pE, pEb = next_ps()
                PE.op(lambda e, pE=pE: e.matmul(pE[:, 0:8], lhsT=MB_f[:], rhs=Pf[:, :, 127], start=True, stop=True),
                      reads=pfb + [B("MB")], writes=[pEb])
                DVE.op(lambda e, pE=pE: e.tensor_tensor(out=small[:, 8:16], in0=pE[:, 0:8], in1=Pf[:, :, 127], op=ALU.add),
                       reads=pfb + [pEb], writes=[B("smallTE")])
                DVE.op(lambda e: e.tensor_tensor(out=Pf, in0=small[:, 8:16].unsqueeze(2).to_broadcast([128, 8, 128]), in1=Pf, op=ALU.subtract),
                       reads=pfb + [B("smallTE")], writes=pfb)
                for hh in range(2):
                    p_, pb_ = next_ps()
                    for q in range(4):
                        h = hh * 4 + q
                        PE.op(lambda e, p_=p_, q=q, h=h: e.transpose(out=p_[:, q * 128:(q + 1) * 128], in_=Pf[:, h, :], identity=ident_f[:]),
                              reads=pfb + [B("ident_f")], writes=[pb_])
                    E = evac_eng()
                    E.op(copy_op(E, biasAll[:, hh * 4:hh * 4 + 4, hf * 8:hf * 8 + 8, :],
                                 p_[:, :].rearrange("p (q b g) -> p q b g", b=8, g=16)), reads=[pb_], writes=[B("xs1"), B("xs2")])
            if STOP == 'SB':
                return

            dso = DS("osc")
            for bl in range(16):
                for pg in range(4):
                    iK = wr_i[0] % 3; wr_i[0] += 1
                    iV = wr_i[0] % 3; wr_i[0] += 1
                    sK, sKb, sV, sVb = wring[iK], B(f"wring{iK}"), wring[iV], B(f"wring{iV}")
                    POOL._wait(Eng._deps([B("stage3")], [sKb, sVb]))
                    for j in range(4):
                        col = ts * 256 + bl * 16 + pg * 4 + j
                        ic = idx_all[:, col:col + 1]
                        gather(DS(f"wring{iK}"), sK[:, j, :], ckf, ic)
                        gather(DS(f"wring{iK}"), sK[:, 4 + j, :], ckd, ic)
                        gather(DS(f"wring{iV}"), sV[:, j, :], cvf, ic)
                        gather(DS(f"wring{iV}"), sV[:, 4 + j, :], cvd, ic)
                    sKb.w = (DS(f"wring{iK}").sem, DS(f"wring{iK}").count); sKb.r = {}
                    sVb.w = (DS(f"wring{iV}").sem, DS(f"wring{iV}").count); sVb.r = {}
                    for j in range(4):
                        g = pg * 4 + j
                        DVE.op(lambda e, j=j, g=g, sV=sV: e.tensor_copy(out=Vf[:, g, :, 0:64], in_=sV[:, j, :].rearrange("p (h d) -> p h d", d=64)),
                               reads=[sVb], writes=[B("Vf")])
                        DVE.op(lambda e, j=j, g=g, sV=sV: e.tensor_copy(out=Vd[:, g, :, 0:128], in_=sV[:, 4 + j, :].rearrange("p (h d) -> p h d", d=128)),
                               reads=[sVb], writes=[B("Vd")])
                    for j in range(4):
                        for hh in range(2):
                            p_, pb_ = next_ps()
                            PE.group([(lambda e, p_=p_, q=q, j=j, hh=hh, sK=sK: e.matmul(p_[:, q * 128:(q + 1) * 128], lhsT=sK[:, 4 * hh + j, q * 128:(q + 1) * 128],
                                                                                             rhs=ident_b[:], start=True, stop=True)) for q in range(4)],
                                     reads=[sKb, B("ident_b")], writes=[pb_])
                            E = evac_eng()
                            E.op(copy_op(E, KTs[:, 4 * hh:4 * hh + 4, j * 128:(j + 1) * 128], p_[:, :].rearrange("p (q n) -> p q n", n=128)),
                                 reads=[pb_], writes=KTsb[4 * hh:4 * hh + 4])
                    pS, pSb = next_ps()
                    PE.group([(lambda e, pS=pS, j=j, blk=blk: e.matmul(pS[:, j * 128 + blk * 16:j * 128 + blk * 16 + 16], lhsT=KTs[:, blk, j * 128:(j + 1) * 128],
                                                                        rhs=QBD[:, blk, bl, :], start=True, stop=True)) for j in range(4) for blk in range(8)],
                             reads=KTsb + qb, writes=[pSb])
                    tv = tmpA[0][:, :].rearrange("p (j c) -> p j c", c=128)
                    sv = pS[:, :].rearrange("p (j c) -> p j c", c=128)
                    hq = lambda ap: ap.rearrange("p j (h q) -> p j h q", q=8)
                    DVE.op(lambda e, pg=pg: e.tensor_tensor(out=hq(tv[:, :, 0:64]), in0=hq(sv[:, :, 0:64]),
                                                            in1=biasAll[:, :, bl, pg * 4:pg * 4 + 4].rearrange("p h g -> p g h").unsqueeze(3).to_broadcast([128, 4, 8, 8]),
                                                            op=ALU.add), reads=[pSb, B("xs1"), B("xs2")], writes=[B("tmpA0")])
                    DVE.op(lambda e, pg=pg: e.tensor_tensor(out=hq(tv[:, :, 64:128]), in0=hq(sv[:, :, 64:128]),
                                                            in1=alibS[:, pg * 4:pg * 4 + 4, :].unsqueeze(3).to_broadcast([128, 4, 8, 8]),
                                                            op=ALU.add), reads=[pSb, B("alibS")], writes=[B("tmpA0")])
                    ACT.op(lambda e, pg=pg: e.activation(out=PTs[:, pg * 4:pg * 4 + 4, :], in_=tv, func=AF.Exp), reads=[B("tmpA0")], writes=PTb)
                pS, pSb = next_ps()
                PE.group([(lambda e, pS=pS, blk=blk: e.matmul(pS[:, blk * 16:blk * 16 + 16], lhsT=(KTf if blk < 4 else KTd)[:, blk % 4, T:T + 128],
                                                               rhs=QBD[:, blk, bl, :], start=True, stop=True)) for blk in range(8)],
                         reads=[B(f"KTf{q}") for q in range(4)] + [B(f"KTd{q}") for q in range(4)] + qb, writes=[pSb])
                t1 = tmpA[1][:, 0:128]
                h2 = lambda ap: ap.rearrange("p (h q) -> p h q", q=8)
                DVE.op(lambda e: e.tensor_tensor(out=h2(t1[:, 0:64]), in0=h2(pS[:, 0:64]), in1=negcs[:, :].unsqueeze(2).to_broadcast([128, 8, 8]), op=ALU.add),
                       reads=[pSb, B("negcs")], writes=[B("tmpA1")])
                DVE.op(lambda e: e.tensor_tensor(out=h2(t1[:, 64:128]), in0=h2(pS[:, 64:128]), in1=alibN[:, 0:8].unsqueeze(2).to_broadcast([128, 8, 8]), op=ALU.add),
                       reads=[pSb, B("alibN")], writes=[B("tmpA1")])
                ACT.op(lambda e: e.activation(out=tmpA[1][:, 128:256], in_=t1, func=AF.Exp), reads=[B("tmpA1")], writes=[B("tmpA1")])
                DVE.op(lambda e: e.tensor_tensor(out=h2(PTs[:, 16, :]), in0=h2(tmpA[1][:, 128:256]),
                                                 in1=maskS_f[:, bl * 8:bl * 8 + 8].unsqueeze(1).to_broadcast([128, 16, 8]), op=ALU.mult),
                       reads=[B("tmpA1"), B("maskS")], writes=PTb)
                banks = [(list(range(0, 4)), 65), (list(range(4, 8)), 65), ([8, 9, 10], 129), ([11, 12, 13], 129), ([14, 15], 129)]
                res = {}
                for hms, ncol in banks:
                    p_, pb_ = next_ps()
                    fns = []
                    for n_, hm in enumerate(hms):
                        h = hm if hm < 8 else (hm - 8) // 2
                        V = Vf if hm < 8 else Vd
                        fns += mm(p_[0:8, n_ * ncol:(n_ + 1) * ncol], [(PTs[:, kb, hm * 8:(hm + 1) * 8], V[:, kb, h, :]) for kb in range(17)])
                        res[hm] = (p_, pb_, n_ * ncol)
                    PE.group(fns, reads=PTb + [B("Vf"), B("Vd")], writes=[pb_])
                for hm in range(16):
                    p_, pb_, off = res[hm]
                    if hm < 8:
                        DVE.op(lambda e, p_=p_, off=off: e.reciprocal(out=stat[0:8, 0:1], in_=p_[0:8, off + 64:off + 65]), reads=[pb_], writes=[B("stat")])
                        DVE.op(lambda e, p_=p_, off=off, hm=hm: e.tensor_scalar_mul(out=osb[0:8, hm * 64:(hm + 1) * 64], in0=p_[0:8, off:off + 64],
                                                                                     scalar1=stat[0:8, 0:1]), reads=[pb_, B("stat")], writes=[B("tmpB0")])
                    else:
                        h, m = (hm - 8) // 2, (hm - 8) % 2
                        DVE.op(lambda e, p_=p_, off=off: e.reciprocal(out=stat[0:8, 8:9], in_=p_[0:8, off + 128:off + 129]), reads=[pb_], writes=[B("stat2")])
                        if m == 0:
                            DVE.op(lambda e, p_=p_, off=off, h=h: e.tensor_scalar_mul(out=d1buf[0:8, h, :], in0=p_[0:8, off:off + 128], scalar1=stat[0:8, 8:9]),
                                   reads=[pb_, B("stat2")], writes=[B("d1_0")])
                        else:
                            DVE.op(lambda e: e.tensor_tensor(out=stat[0:8, 9:10], in0=stat[0:8, 8:9], in1=lamc[0:8, 0:1], op=ALU.mult),
                                   reads=[B("stat2"), B("lamc")], writes=[B("stat2")])
                            DVE.op(lambda e, p_=p_, off=off, h=h: e.scalar_tensor_tensor(out=d1buf[0:8, h, :], in0=p_[0:8, off:off + 128], scalar=stat[0:8, 9:10],
                                                                                          in1=d1buf[0:8, h, :], op0=ALU.mult, op1=ALU.add),
                                   reads=[pb_, B("stat2"), B("d1_0")], writes=[B("d1_0")])
                sq = tmpA[1][0:8, 0:512].rearrange("p (h e) -> p h e", e=128)
                ACT.op(lambda e: e.activation(out=sq, in_=d1buf[0:8, :, :], func=AF.Square), reads=[B("d1_0")], writes=[B("tmpA1")])
                DVE.op(lambda e: e.reduce_sum(out=stat[0:8, 10:14], in_=sq, axis=mybir.AxisListType.X), reads=[B("tmpA1")], writes=[B("stat3")])
                ACT.op(lambda e: e.activation(out=stat[0:8, 10:14], in_=stat[0:8, 10:14], func=AF.Sqrt, bias=LN_EPS, scale=1.0 / 128),
                       reads=[B("stat3")], writes=[B("stat3")])
                DVE.op(lambda e: e.reciprocal(out=stat[0:8, 10:14], in_=stat[0:8, 10:14]), reads=[B("stat3")], writes=[B("stat3")])
                DVE.op(lambda e: e.tensor_tensor(out=d1buf[0:8, :, :], in0=d1buf[0:8, :, :], in1=stat[0:8, 10:14].unsqueeze(2).to_broadcast([8, 4, 128]), op=ALU.mult),
                       reads=[B("d1_0"), B("stat3")], writes=[B("d1_0")])
                DVE.op(lambda e: e.tensor_tensor(out=osb[0:8, 512:1024].rearrange("p (h e) -> p h e", e=128), in0=d1buf[0:8, :, :],
                                                 in1=gsub[0:8, :].unsqueeze(1).to_broadcast([8, 4, 128]), op=ALU.mult),
                       reads=[B("d1_0"), B("gsub")], writes=[B("tmpB0")])
                SP.dma(dso, osc[ts * 128 + bl * 8:ts * 128 + bl * 8 + 8, :], osb[0:8, :], reads=[B("tmpB0")], writes=[B("osc")])
            B("osc").w = (dso.sem, dso.count)
            SP.dma(dso, oAB[:, 0, :], osc[rows, :], reads=[B("osc")], writes=[B("oAB0")])
            if STOP == 'SC':
                return
            phase_c(1, True, None, mbc, lambda i: ys[rows, :], [0], hbufs)

        nch = int(os.environ.get("KDEV_NCH", NCH))
        if STOP == 'setup':
            nch = 0
        for pi in range(NP):
            for c in range(nch):
                chunk(c, pi)
        if STOP != 'setup' and os.environ.get('KDEV_NOSAMPLE') is None:
            for ts in range(NS):
                sample_tile(ts)

        SP._wait({d.sem: d.count for d in _ds.values() if d.count})
        SP._wait({E.sem: E.count for E in (PE, ACT, DVE, POOL)})
    return nc


_INPUT_KEYS = None


def kernel(**inputs):
    ncores = int(os.environ.get("KDEV_CORES", 4))
    NP = int(os.environ.get("KDEV_NP", 8 // ncores))
    NS = int(os.environ.get("KDEV_NS", 8 // ncores))
    n_phys = int(inputs["cache_k_fox"].shape[1])
    nc = build_program(NP=NP, NS=NS, n_phys=n_phys)
    f = lambda a: np.ascontiguousarray(np.asarray(a))
    pools = {
        "ckf": f(inputs["cache_k_fox"]).reshape(n_phys * 128, 512), "cvf": f(inputs["cache_v_fox"]).reshape(n_phys * 128, 512),
        "ckd": f(inputs["cache_k_diff"]).reshape(n_phys * 128, 512), "cvd": f(inputs["cache_v_diff"]).reshape(n_phys * 128, 512),
        "clf": f(inputs["cache_logf_fox"]).reshape(n_phys, 1024),
    }
    shared = {
        "w_ada": f(inputs["w_ada"][0]), "b_ada": f(inputs["b_ada"]), "w_in": f(inputs["w_in"][0]),
        "b_forget": f(inputs["b_forget"]), "lq1": f(inputs["lambda_q1"]), "lk1": f(inputs["lambda_k1"]),
        "lq2": f(inputs["lambda_q2"]), "lk2": f(inputs["lambda_k2"]), "subln": f(inputs["subln_gain"]),
        "w_ba": f(inputs["w_branch_a"][0]), "w_bb": f(inputs["w_branch_b"][0]), "w_out": f(inputs["w_out"][0]),
        "ln1g": f(inputs["ln1_gain"]), "ln1b": f(inputs["ln1_bias"]),
        "w_fg": f(inputs["w_ffn_gate"][0]), "w_fu": f(inputs["w_ffn_up"][0]), "w_fd": f(inputs["w_ffn_down"][0]),
        "ln2g": f(inputs["ln2_gain"]), "ln2b": f(inputs["ln2_bias"]),
    }
    in_maps = []
    for b in range(ncores):
        sq = slice(16 * NS * b, 16 * NS * (b + 1))
        m = {
            "xp": f(inputs["x_prompt"][NP * b:NP * (b + 1)]).reshape(NP * T, D),
            "xs": f(inputs["x_sample"][sq]).reshape(NS * 128, D),
            "cp": f(inputs["c_prompt"][NP * b:NP * (b + 1)]), "cs": f(inputs["c_sample"][sq]),
            "pt": f(inputs["page_table"][sq]).reshape(1, NS * 256).astype(np.int32),
        }
        m.update(pools)
        m.update(shared)
        in_maps.append(m)
    res = run_bass_kernel_spmd(nc, in_maps, core_ids=list(range(ncores)))
    R = res.results
    nb = len(R)

    def cat(name, shape):
        return np.stack([R[b][name] for b in range(nb)], 0).reshape(shape)
    npq, nsq = nb * NP, nb * NS * 16
    outs = (cat("yp", (npq, T, D)), cat("ys", (nsq, 8, D)),
            cat("kfp", (1, npq, T, 8, 64)), cat("vfp", (1, npq, T, 8, 64)), cat("lfp", (1, npq, T, 8)),
            cat("kdp", (1, npq, T, 4, 2, 64)), cat("vdp", (1, npq, T, 4, 128)),
            cat("kfs", (1, nsq, 8, 8, 64)), cat("vfs", (1, nsq, 8, 8, 64)), cat("lfs", (1, nsq, 8, 8)),
            cat("kds", (1, nsq, 8, 4, 2, 64)), cat("vds", (1, nsq, 8, 4, 128)))
    return tuple(np.ascontiguousarray(o.astype(np.float32)) for o in outs)
```

```python
import os
from contextlib import ExitStack
import numpy as np
import concourse.bass as bass
import concourse.mybir as mybir
from concourse.bass_utils import run_bass_kernel_spmd

F32 = mybir.dt.float32
BF16 = mybir.dt.bfloat16
I32 = mybir.dt.int32
AF = mybir.ActivationFunctionType
ALU = mybir.AluOpType

D = 1024
KC = 8
T = 2048
NCH = 4
DFF = 2816
FC = 22
DIN = 5128
ALPHA = 2.0 ** 0.25
LN_EPS = 1e-5
LAMBDA_INIT = 0.2
SLOPES = [2.0 ** (-8.0 * (h + 1) / 4) for h in range(4)]
C_QA, C_KA, C_VA, C_FA, C_QB, C_KB, C_VB, C_GA, C_GB = 0, 512, 1024, 1536, 1544, 2056, 2568, 3080, 4104


class Buf:
    __slots__ = ("w", "r")

    def __init__(self):
        self.w = None
        self.r = {}


class DSem:
    def __init__(self, nc, name):
        self.sem = nc.alloc_semaphore(name)
        self.count = 0


class Eng:
    def __init__(self, nc, name, eng):
        self.eng = eng
        self.sem = nc.alloc_semaphore("sem_" + name)
        self.count = 0
        self.waited = {}

    def _wait(self, deps):
        for sem, val in deps.items():
            if self.waited.get(sem, 0) < val:
                self.eng.wait_ge(sem, val)
                self.waited[sem] = val

    @staticmethod
    def _deps(reads, writes):
        deps = {}
        for b in reads:
            if b.w is not None and deps.get(b.w[0], 0) < b.w[1]:
                deps[b.w[0]] = b.w[1]
        for b in writes:
            if b.w is not None and deps.get(b.w[0], 0) < b.w[1]:
                deps[b.w[0]] = b.w[1]
            for s, v in b.r.items():
                if deps.get(s, 0) < v:
                    deps[s] = v
        return deps

    @staticmethod
    def _mark(tok, reads, writes):
        s, v = tok
        for b in reads:
            if b.r.get(s, 0) < v:
                b.r[s] = v
        for b in writes:
            b.w = tok
            b.r = {}

    def op(self, fn, reads=(), writes=()):
        return self.group([fn], reads, writes)

    def group(self, fns, reads=(), writes=()):
        self._wait(self._deps(reads, writes))
        ins = None
        for fn in fns:
            ins = fn(self.eng)
        self.count += 1
        ins.then_inc(self.sem, 1)
        tok = (self.sem, self.count)
        self._mark(tok, reads, writes)
        return tok

    def dma(self, dsem, out, in_, reads=(), writes=()):
        self._wait(self._deps(reads, writes))
        ins = self.eng.dma_start(out=out, in_=in_)
        dsem.count += 16
        ins.then_inc(dsem.sem, 16)
        tok = (dsem.sem, dsem.count)
        self._mark(tok, reads, writes)
        return tok


def build_program(NP=2, NS=2, n_phys=2560, do_sample=True):
    WCACHE = os.environ.get('KDEV_NOWCACHE') is None
    NR = NP + 16 * NS
    STOP = os.environ.get('KDEV_STOP', '')
    nc = bass.Bass("TRN2", target_bir_lowering=False)

    def din(name, shape, dt=F32):
        return nc.dram_tensor(name, list(shape), dt, kind="ExternalInput").ap()

    def dout(name, shape, dt=F32):
        return nc.dram_tensor(name, list(shape), dt, kind="ExternalOutput").ap()

    xp = din("xp", [NP * T, D]); xs_in = din("xs", [NS * 128, D])
    cp = din("cp", [NP, D]); cs = din("cs", [NS * 16, D])
    pt = din("pt", [1, NS * 256], I32)
    ckf = din("ckf", [n_phys * 128, 512]); cvf = din("cvf", [n_phys * 128, 512])
    ckd = din("ckd", [n_phys * 128, 512]); cvd = din("cvd", [n_phys * 128, 512]); clf = din("clf", [n_phys, 1024])
    w_ada = din("w_ada", [D, 6 * D]); b_ada = din("b_ada", [1, 6 * D])
    w_in = din("w_in", [D, DIN]); b_forget = din("b_forget", [1, 8])
    lq1 = din("lq1", [1, 64]); lk1 = din("lk1", [1, 64]); lq2 = din("lq2", [1, 64]); lk2 = din("lk2", [1, 64])
    subln = din("subln", [1, 128])
    w_ba = din("w_ba", [512, D]); w_bb = din("w_bb", [512, D]); w_out = din("w_out", [D, D])
    ln1g = din("ln1g", [1, D]); ln1b = din("ln1b", [1, D])
    w_fg = din("w_fg", [D, DFF]); w_fu = din("w_fu", [D, DFF]); w_fd = din("w_fd", [DFF, D])
    ln2g = din("ln2g", [1, D]); ln2b = din("ln2b", [1, D])
    yp = dout("yp", [NP * T, D]); ys = dout("ys", [NS * 128, D])
    kfp = dout("kfp", [NP * T, 512]); vfp = dout("vfp", [NP * T, 512]); lfp = dout("lfp", [NP * T, 8])
    kdp = dout("kdp", [NP * T, 512]); vdp = dout("vdp", [NP * T, 512])
    kfs = dout("kfs", [NS * 128, 512]); vfs = dout("vfs", [NS * 128, 512]); lfs = dout("lfs", [NS * 128, 8])
    kds = dout("kds", [NS * 128, 512]); vds = dout("vds", [NS * 128, 512])
    osc = nc.dram_tensor("osc", [NS * 128, 1024], BF16, kind="Internal").ap()

    es = ExitStack()
    with es:
        def sb(name, shape, dt=F32):
            return es.enter_context(nc.sbuf_tensor(name, list(shape), dt))

        cst = sb("cst", [128, 8, 128])
        bfb = cst[:, 0, 0:8]; gsub = cst[:, 1, :]
        wfa_t = sb("wfa", [128, 256], BF16)
        wfa = wfa_t[:, 0:64].rearrange("p (k n) -> p k n", n=8)
        lng = sb("lng", [128, 4, D])
        xs = sb("xs_sb", [128, 4, D])
        wflat = [sb(f"wring{i}", [128, KC * 512], BF16) for i in range(3)]
        stage = [sb(f"stage{i}", [128, 512]) for i in range(4)]
        ident_f = sb("ident_f", [128, 128]); ident_b = sb("ident_b", [128, 128], BF16)
        tri_f = sb("tri_f", [128, 128]); tri_b = sb("tri_b", [128, 128], BF16)
        elast_f = sb("elast_f", [128, 128]); ones_f = sb("ones_f", [128, 128])
        ones_b = sb("ones_b", [128, 128], BF16)
        posS = sb("posS", [128, 16, 4])
        modT = sb("modT", [128, 48, NR])
        lamc = sb("lamc", [128, 4])
        small = sb("small", [128, 64])
        cT = sb("cT", [128, KC, NR], BF16)
        KTf = sb("KTf", [128, 4, T + 128], BF16); KTd = sb("KTd", [128, 4, T + 128], BF16)
        Vf = sb("Vf", [128, 17, 8, 65], BF16); Vd = sb("Vd", [128, 17, 4, 129], BF16)
        c_all = sb("c_all", [128, 17, 8]); lfacc = sb("lfacc", [128, 8]); lf_t = sb("lf_t", [128, 4, 8])
        fa_t = sb("fa_t", [128, 8]); rc_bc = sb("rc_bc", [128, 8])
        biasF = sb("biasF", [128, 16, 8]); biasD = sb("biasD", [128, 16, 4])
        hT = sb("hT", [128, KC, 512], BF16)
        big = sb("big", [128, FC * 512], BF16)
        oAB = sb("oAB", [128, 4, D], BF16)
        oT = sb("oT", [128, KC, 512], BF16)
        QTf = oT[:, 0:4, :]; QTd = oT[:, 4:8, :]
        tmpA = [sb(f"tmpA{i}", [128, 512]) for i in range(2)]
        tmpB = [sb(f"tmpB{i}", [128, 512]) for i in range(2)]
        dtmp = sb("dtmp", [128, 2, 128]); d1buf = sb("d1buf", [128, 4, 128])
        stat = sb("stat", [128, 16])
        bnst = sb("bnst", [128, 12])
        wring = [w[:, :].rearrange("p (k n) -> p k n", n=512) for w in wflat]
        wdring = [w[:, 0:FC * 128].rearrange("p (k n) -> p k n", n=128) for w in wflat]
        modrow = stage[0][0:NR, :]; badab = stage[1][0:NR, :]
        mtm = big[:, 0:4096].bitcast(F32).rearrange("p (j n) -> p j n", n=512)
        psum = [es.enter_context(nc.psum_tensor(f"ps{i}", [128, 512], F32)) for i in range(8)]
        print("sbuf bytes remaining/partition:", nc.sbuf_bytes_remaining)
        es.enter_context(nc.Block())

        PE = Eng(nc, "pe", nc.tensor); ACT = Eng(nc, "act", nc.scalar); DVE = Eng(nc, "dve", nc.vector)
        POOL = Eng(nc, "pool", nc.gpsimd); SP = Eng(nc, "sp", nc.sync)
        _bufs = {}

        def B(name):
            if name not in _bufs:
                _bufs[name] = Buf()
            return _bufs[name]
        _ds = {}

        def DS(name):
            if name not in _ds:
                _ds[name] = DSem(nc, "d_" + name)
            return _ds[name]

        ps_i = [0]

        def next_ps():
            i = ps_i[0] % 8
            ps_i[0] += 1
            return psum[i], B(f"ps{i}")

        st_i = [0]

        def next_stage():
            i = st_i[0] % 3
            st_i[0] += 1
            return stage[i], B(f"stage{i}"), DS(f"stage{i}")

        wr_i = [0]

        wcache = {}

        def load_block(key, fills, nflat):
            i = wr_i[0] % 3
            wr_i[0] += 1
            sbuf_ = B(f"wring{i}")
            if key is not None and key in wcache:
                scr, scb = wcache[key]
                POOL.dma(DS(f"wring{i}"), wflat[i][:, 0:nflat], scr, reads=[scb], writes=[sbuf_])
                return i, sbuf_
            POOL._wait(Eng._deps([], [sbuf_]))
            for dst_fn, src in fills:
                POOL.dma(DS(f"wring{i}"), dst_fn(i), src)
            sbuf_.w = (DS(f"wring{i}").sem, DS(f"wring{i}").count)
            sbuf_.r = {}
            if key is not None and WCACHE:
                scr = nc.dram_tensor("wc_" + key, [128, nflat], BF16, kind="Internal").ap()
                scb = Buf()
                SP.dma(DS(f"wst{i}"), scr, wflat[i][:, 0:nflat], reads=[sbuf_], writes=[scb])
                wcache[key] = (scr, scb)
            return i, sbuf_

        def load_w(src, kc, ncols):
            src3, key = src
            i, b_ = load_block(key, [(lambda i: wring[i][:, 0:kc, 0:ncols], src3)], kc * 512)
            return wring[i], b_

        def wsrc(w, r0, nr, c0, ncols, cache=True):
            ap = w[r0:r0 + nr, c0:c0 + ncols].rearrange("(k p) n -> p k n", p=128)
            return ap, (f"{w.tensor.name}_{r0}_{c0}_{ncols}" if cache else None)

        ev_i = [0]

        def evac_eng():
            ev_i[0] += 1
            return ACT if ev_i[0] % 2 else DVE

        def copy_op(E, out, in_, scale=None):
            if E is ACT:
                if scale is None:
                    return lambda e: e.copy(out=out, in_=in_)
                return lambda e: e.mul(out=out, in_=in_, mul=scale)
            if scale is None:
                return lambda e: e.tensor_copy(out=out, in_=in_)
            return lambda e: e.tensor_scalar_mul(out=out, in0=in_, scalar1=scale)

        def mm(out, pairs):
            n = len(pairs)
            return [(lambda e, l=l, r=r, i=i: e.matmul(out, lhsT=l, rhs=r, start=(i == 0), stop=(i == n - 1)))
                    for i, (l, r) in enumerate(pairs)]

        POOL.op(lambda e: e.memset(ones_f[:], 1.0), writes=[B("ones_f")])
        POOL.op(lambda e: e.memset(ident_f[:], 0.0), writes=[B("ident_f")])
        POOL.op(lambda e: e.affine_select(out=ident_f[:], in_=ident_f[:], pattern=[[-1, 128]], compare_op=ALU.not_equal,
                                          fill=1.0, base=0, channel_multiplier=1), reads=[B("ident_f")], writes=[B("ident_f")])
        POOL.op(lambda e: e.affine_select(out=tri_f[:], in_=ones_f[:], pattern=[[1, 128]], compare_op=ALU.is_ge,
                                          fill=0.0, base=0, channel_multiplier=-1), reads=[B("ones_f")], writes=[B("tri_f")])
        POOL.op(lambda e: e.affine_select(out=elast_f[:], in_=ones_f[:], pattern=[[0, 128]], compare_op=ALU.is_equal,
                                          fill=0.0, base=-127, channel_multiplier=1), reads=[B("ones_f")], writes=[B("elast_f")])
        DVE.op(lambda e: e.tensor_copy(out=ident_b[:], in_=ident_f[:]), reads=[B("ident_f")], writes=[B("ident_b")])
        DVE.op(lambda e: e.tensor_copy(out=tri_b[:], in_=tri_f[:]), reads=[B("tri_f")], writes=[B("tri_b")])
        DVE.op(lambda e: e.tensor_copy(out=ones_b[:], in_=ones_f[:]), reads=[B("ones_f")], writes=[B("ones_b")])
        POOL.op(lambda e: e.iota(posS[:, :, 0], pattern=[[128, 16]], base=0, channel_multiplier=1,
                                 allow_small_or_imprecise_dtypes=True), writes=[B("posS")])
        for h in (1, 2, 3):
            DVE.op(lambda e, h=h: e.tensor_scalar_mul(out=posS[:, :, h], in0=posS[:, :, 0], scalar1=SLOPES[h]),
                   reads=[B("posS")], writes=[B("posS")])
        DVE.op(lambda e: e.tensor_scalar_mul(out=posS[:, :, 0], in0=posS[:, :, 0], scalar1=SLOPES[0]),
               reads=[B("posS")], writes=[B("posS")])
        POOL.op(lambda e: e.memset(Vf[:, :, :, 64:65], 1.0), writes=[B("Vf")])
        POOL.op(lambda e: e.memset(Vd[:, :, :, 128:129], 1.0), writes=[B("Vd")])
        POOL.op(lambda e: e.memset(lfacc[:], 0.0), writes=[B("lfacc")])
        SP.dma(DS("c0"), bfb, b_forget.partition_broadcast(128), writes=[B("bfb")])
        for i, v in enumerate((ln1g, ln1b, ln2g, ln2b)):
            SP.dma(DS("c0"), lng[:, i, :], v.partition_broadcast(128), writes=[B("lng")])
        SP.dma(DS("c0"), gsub, subln.partition_broadcast(128), writes=[B("gsub")])
        for i, v in enumerate((lq1, lk1, lq2, lk2)):
            SP.dma(DS("c0"), cst[:, 2 + i, 0:64], v.partition_broadcast(128), writes=[B("cstl")])
        for nm in ("bfb", "lng", "gsub", "cstl"):
            B(nm).w = (DS("c0").sem, DS("c0").count)
        DVE.op(lambda e: e.tensor_scalar_mul(out=gsub, in0=gsub, scalar1=1.0 - LAMBDA_INIT), reads=[B("gsub")], writes=[B("gsub")])
        DVE.op(lambda e: e.tensor_tensor(out=stage[2][:, 256:320], in0=cst[:, 2, 0:64], in1=cst[:, 3, 0:64], op=ALU.mult),
               reads=[B("cstl")], writes=[B("stage2")])
        DVE.op(lambda e: e.tensor_tensor(out=stage[2][:, 320:384], in0=cst[:, 4, 0:64], in1=cst[:, 5, 0:64], op=ALU.mult),
               reads=[B("cstl")], writes=[B("stage2")])
        DVE.op(lambda e: e.reduce_sum(out=small[:, 0:2], in_=stage[2][:, 256:384].rearrange("p (a b) -> p a b", b=64),
                                      axis=mybir.AxisListType.X), reads=[B("stage2")], writes=[B("small")])
        ACT.op(lambda e: e.activation(out=small[:, 2:4], in_=small[:, 0:2], func=AF.Exp), reads=[B("small")], writes=[B("small")])
        DVE.op(lambda e: e.tensor_tensor(out=small[:, 4:5], in0=small[:, 3:4], in1=small[:, 2:3], op=ALU.subtract),
               reads=[B("small")], writes=[B("small")])
        DVE.op(lambda e: e.tensor_scalar_add(out=lamc[:, 0:1], in0=small[:, 4:5], scalar1=-LAMBDA_INIT), reads=[B("small")], writes=[B("lamc")])

        SP.dma(DS("c1"), xs[0:NP, 0, :], cp, writes=[B("xs0")])
        SP.dma(DS("c1"), xs[NP:NR, 0, :], cs, writes=[B("xs0")])
        for k in range(KC):
            p_, pb_ = next_ps()
            PE.op(lambda e, k=k, p_=p_: e.transpose(out=p_[:, 0:NR], in_=xs[0:NR, 0, k * 128:(k + 1) * 128], identity=ident_f[0:NR, 0:NR]),
                  reads=[B("xs0"), B("ident_f")], writes=[pb_])
            DVE.op(lambda e, k=k, p_=p_: e.tensor_copy(out=cT[:, k, :], in_=p_[:, 0:NR]), reads=[pb_], writes=[B("cT")])
        for blk in range(12):
            wt, wb = load_w(wsrc(w_ada, 0, D, blk * 512, 512, cache=False), KC, 512)
            SP.dma(DS("bada"), badab, b_ada[0:1, blk * 512:(blk + 1) * 512].partition_broadcast(NR), writes=[B("stage1")])
            p_, pb_ = next_ps()
            PE.group(mm(p_[0:NR, :], [(cT[:, k, :], wt[:, k, :]) for k in range(KC)]), reads=[B("cT"), wb], writes=[pb_])
            DVE.op(lambda e, p_=p_: e.tensor_tensor(out=modrow, in0=p_[0:NR, :], in1=badab, op=ALU.add),
                   reads=[pb_, B("stage1")], writes=[B("stage0")])
            if blk in (2, 3, 8, 9):
                DVE.op(lambda e: e.tensor_scalar_add(out=modrow, in0=modrow, scalar1=1.0), reads=[B("stage0")], writes=[B("stage0")])
            p2, pb2 = next_ps()
            for q in range(4):
                PE.op(lambda e, q=q, p2=p2: e.transpose(out=p2[:, q * NR:(q + 1) * NR], in_=modrow[:, q * 128:(q + 1) * 128],
                                                        identity=ident_f[0:NR, 0:NR]), reads=[B("stage0"), B("ident_f")], writes=[pb2])
            DVE.op(lambda e, blk=blk, p2=p2: e.tensor_copy(out=modT[:, blk * 4:(blk + 1) * 4, :],
                                                          in_=p2[:, 0:4 * NR].rearrange("p (q r) -> p q r", r=NR)),
                   reads=[pb2], writes=[B("modT")])
        SH1, SC1, G1, SH2, SC2, G2 = 0, 8, 16, 24, 32, 40

        POOL.dma(DS("wfa"), wfa, wsrc(w_in, 0, D, C_FA, 8)[0], writes=[B("wfa")])

        def phase_c(NT, SAMPLE, mcol, mbc, y_rows, tiles, hbufs):
            ntok = NT * 128
            v3 = lambda ap: ap.rearrange("p (s q) -> p s q", q=8)

            def tm_block(wt, wb, j, ncols=512):
                p_, pb_ = next_ps()
                PE.group(mm(p_[:, 0:ncols], [(hT[:, k, j * 128:(j + 1) * 128], wt[:, k, 0:ncols]) for k in range(KC)]),
                         reads=hbufs + [wb], writes=[pb_])
                return p_, pb_
            oTb = [B(f"oT{k}") for k in range(KC)]

            def transpose8(src_fn, src_buf, j):
                for half in range(2):
                    p_, pb_ = next_ps()
                    PE.group([(lambda e, q=q, p_=p_: e.matmul(p_[:, q * 128:(q + 1) * 128], lhsT=src_fn(4 * half + q), rhs=ident_b[:],
                                                              start=True, stop=True)) for q in range(4)],
                             reads=[src_buf, B("ident_b")], writes=[pb_])
                    E = evac_eng()
                    E.op(copy_op(E, oT[:, 4 * half:4 * half + 4, j * 128:(j + 1) * 128], p_[:, :].rearrange("p (q t) -> p q t", t=128)),
                         reads=[pb_], writes=oTb[4 * half:4 * half + 4])

            for j in range(NT):
                transpose8(lambda q, j=j: oAB[:, j, q * 128:(q + 1) * 128], B(f"oAB{j}"), j)
            for n in range(2):
                i3, wab_b = load_block(f"wab_{n}", [(lambda i: wring[i][:, 0:4, :], wsrc(w_ba, 0, 512, n * 512, 512)[0]),
                                                     (lambda i: wring[i][:, 4:8, :], wsrc(w_bb, 0, 512, n * 512, 512)[0])], KC * 512)
                wab_t = wring[i3]
                for br in range(2):
                    wga, wgab = load_w(wsrc(w_in, 0, D, (C_GA if br == 0 else C_GB) + n * 512, 512), KC, 512)
                    for j in range(NT):
                        mtb = [B(f"big{2 * j}"), B(f"big{2 * j + 1}")]
                        pm, pmb = next_ps()
                        PE.group(mm(pm[:, :], [(oT[:, 4 * br + q, j * 128:(j + 1) * 128], wab_t[:, 4 * br + q, :]) for q in range(4)]),
                                 reads=oTb + [wab_b], writes=[pmb])
                        pg, pgb = tm_block(wga, wgab, j)
                        ACT.op(lambda e, pg=pg, j=j: e.activation(out=tmpA[j % 2][:], in_=pg[:, :], func=AF.Sigmoid),
                               reads=[pgb], writes=[B(f"tmpA{j % 2}")])
                        if br == 0:
                            DVE.op(lambda e, pm=pm, j=j: e.tensor_tensor(out=mtm[:, j, :], in0=pm[:, :], in1=tmpA[j % 2][:], op=ALU.mult),
                                   reads=[pmb, B(f"tmpA{j % 2}")], writes=mtb)
                        else:
                            DVE.op(lambda e, pm=pm, j=j: e.tensor_tensor(out=tmpB[j % 2][:], in0=pm[:, :], in1=tmpA[j % 2][:], op=ALU.mult),
                                   reads=[pmb, B(f"tmpA{j % 2}")], writes=[B(f"tmpB{j % 2}")])
                            DVE.op(lambda e, j=j, n=n: e.tensor_tensor(out=oAB[:, j, n * 512:(n + 1) * 512], in0=mtm[:, j, :], in1=tmpB[j % 2][:], op=ALU.add),
                                   reads=mtb + [B(f"tmpB{j % 2}")], writes=[B(f"oAB{j}")])
            for j in range(NT):
                transpose8(lambda q, j=j: oAB[:, j, q * 128:(q + 1) * 128], B(f"oAB{j}"), j)

            def lin_back(get_w, nk, rhs_fn, rhs_bufs, gidx):
                for nchunk in range(8):
                    wt, wb, wsel = get_w(nchunk)
                    p_, pb_ = next_ps()
                    PE.group(mm(p_[:, 0:ntok], [(wsel(wt, k), rhs_fn(k)) for k in range(nk)]), reads=rhs_bufs + [wb], writes=[pb_])
                    tb = tmpA[nchunk % 2]
                    tbb = B(f"tmpA{nchunk % 2}")
                    if not SAMPLE:
                        ACT.op(lambda e, p_=p_, tb=tb, nchunk=nchunk: e.activation(out=tb[:, 0:ntok], in_=p_[:, 0:ntok], func=AF.Identity,
                                                                                    scale=mcol(gidx + nchunk)),
                               reads=[pb_, B("modT")], writes=[tbb])
                    else:
                        DVE.op(lambda e, p_=p_, tb=tb, nchunk=nchunk: e.tensor_tensor(out=v3(tb[:, 0:128]), in0=v3(p_[:, 0:128]),
                                                                                       in1=mbc(gidx + nchunk), op=ALU.mult),
                               reads=[pb_, B("modT")], writes=[tbb])
                    p2, pb2 = next_ps()
                    for j in range(NT):
                        PE.op(lambda e, j=j, p2=p2, tb=tb: e.transpose(out=p2[:, j * 128:(j + 1) * 128], in_=tb[:, j * 128:(j + 1) * 128],
                                                                       identity=ident_f[:]), reads=[tbb, B("ident_f")], writes=[pb2])
                    for j in range(NT):
                        DVE.op(lambda e, j=j, p2=p2, nchunk=nchunk: e.scalar_tensor_tensor(
                            out=xs[:, j, nchunk * 128:(nchunk + 1) * 128], in0=xs[:, j, nchunk * 128:(nchunk + 1) * 128], scalar=ALPHA,
                            in1=p2[:, j * 128:(j + 1) * 128], op0=ALU.mult, op1=ALU.add), reads=[pb2, B(f"xs{j}")], writes=[B(f"xs{j}")])

            def layer_norm(j, gi):
                xb = B(f"xs{j}")
                for s_ in range(2):
                    DVE.op(lambda e, s_=s_: e.bn_stats(out=bnst[:, s_ * 6:(s_ + 1) * 6], in_=xs[:, j, s_ * 512:(s_ + 1) * 512]), reads=[xb], writes=[B("bnst")])
                DVE.op(lambda e: e.bn_aggr(out=stat[:, 13:15], in_=bnst[:]), reads=[B("bnst")], writes=[B("lnstat")])
                ACT.op(lambda e: e.activation(out=stat[:, 15:16], in_=stat[:, 14:15], func=AF.Sqrt, bias=LN_EPS, scale=1.0),
                       reads=[B("lnstat")], writes=[B("lnstat2")])
                DVE.op(lambda e: e.reciprocal(out=stat[:, 15:16], in_=stat[:, 15:16]), reads=[B("lnstat2")], writes=[B("lnstat2")])
                DVE.op(lambda e: e.tensor_scalar(out=xs[:, j, :], in0=xs[:, j, :], scalar1=stat[:, 13:14], scalar2=stat[:, 15:16],
                                                 op0=ALU.subtract, op1=ALU.mult), reads=[xb, B("lnstat"), B("lnstat2")], writes=[xb])
                DVE.op(lambda e: e.tensor_tensor(out=xs[:, j, :], in0=xs[:, j, :], in1=lng[:, gi, :], op=ALU.mult),
                       reads=[xb, B("lng")], writes=[xb])
                DVE.op(lambda e: e.tensor_tensor(out=xs[:, j, :], in0=xs[:, j, :], in1=lng[:, gi + 1, :], op=ALU.add),
                       reads=[xb, B("lng")], writes=[xb])

            if STOP == 'C3':
                return
            cur = [None]

            def get_wout(nchunk):
                if nchunk % 4 == 0:
                    cur[0] = load_w(wsrc(w_out, 0, D, nchunk * 128, 512), KC, 512)
                wt, wb = cur[0]
                c0 = (nchunk % 4) * 128
                return wt, wb, (lambda wt, k: wt[:, k, c0:c0 + 128])
            lin_back(get_wout, KC, lambda k: oT[:, k, 0:ntok], oTb, G1)
            for j in range(NT):
                layer_norm(j, 0)
            for k in range(KC):
                p_, pb_ = next_ps()
                for j in range(NT):
                    PE.op(lambda e, j=j, k=k, p_=p_: e.transpose(out=p_[:, j * 128:(j + 1) * 128], in_=xs[:, j, k * 128:(k + 1) * 128],
                                                                 identity=ident_f[:]), reads=[B(f"xs{j}"), B("ident_f")], writes=[pb_])
                if not SAMPLE:
                    ACT.op(lambda e, k=k, p_=p_: e.activation(out=oT[:, k, 0:ntok], in_=p_[:, 0:ntok], func=AF.Identity,
                                                              bias=mcol(SH2 + k), scale=mcol(SC2 + k)),
                           reads=[pb_, B("modT")], writes=[B(f"oT{k}")])
                else:
                    DVE.op(lambda e, k=k, p_=p_: e.tensor_tensor(out=v3(tmpA[0][:, 0:128]), in0=v3(p_[:, 0:128]), in1=mbc(SC2 + k), op=ALU.mult),
                           reads=[pb_, B("modT")], writes=[B("tmpA0")])
                    DVE.op(lambda e, k=k: e.tensor_tensor(out=v3(oT[:, k, 0:128]), in0=v3(tmpA[0][:, 0:128]), in1=mbc(SH2 + k), op=ALU.add),
                           reads=[B("tmpA0"), B("modT")], writes=[B(f"oT{k}")])
            if STOP == 'C6':
                return
            aT = big[:, :].rearrange("p (f t) -> p f t", t=512)
            for blk in range(11):
                i3, wb = load_block(f"wgu_{blk}", [(lambda i: wring[i][:, :, 0:256], wsrc(w_fg, 0, D, blk * 256, 256)[0]),
                                                    (lambda i: wring[i][:, :, 256:512], wsrc(w_fu, 0, D, blk * 256, 256)[0])], KC * 512)
                wt = wring[i3]
                for s_ in range(2):
                    fc = blk * 2 + s_
                    pg, pgb = next_ps()
                    PE.group(mm(pg[:, 0:ntok], [(wt[:, k, s_ * 128:(s_ + 1) * 128], oT[:, k, 0:ntok]) for k in range(KC)]), reads=oTb + [wb], writes=[pgb])
                    pu, pub = next_ps()
                    PE.group(mm(pu[:, 0:ntok], [(wt[:, k, 256 + s_ * 128:256 + (s_ + 1) * 128], oT[:, k, 0:ntok]) for k in range(KC)]), reads=oTb + [wb], writes=[pub])
                    tb = tmpB[fc % 2]
                    ACT.op(lambda e, pg=pg, tb=tb: e.activation(out=tb[:, 0:ntok], in_=pg[:, 0:ntok], func=AF.Silu), reads=[pgb], writes=[B(f"tmpB{fc % 2}")])
                    DVE.op(lambda e, pu=pu, tb=tb, fc=fc: e.tensor_tensor(out=aT[:, fc, 0:ntok], in0=pu[:, 0:ntok], in1=tb[:, 0:ntok], op=ALU.mult),
                           reads=[pub, B(f"tmpB{fc % 2}")], writes=[B(f"big{fc}")])
            def get_wd(nchunk):
                i3, wb = load_block(f"wd_{nchunk}", [(lambda i: wdring[i], w_fd[:, nchunk * 128:(nchunk + 1) * 128].rearrange("(k p) n -> p k n", p=128))],
                                    FC * 128)
                return wdring[i3], wb, (lambda wt, k: wt[:, k, :])
            lin_back(get_wd, FC, lambda k: aT[:, k, 0:ntok], [B(f"big{q}") for q in range(FC)], G2)
            for j, i in enumerate(tiles):
                layer_norm(j, 2)
                SP.dma(DS(f"x{j}"), y_rows(i), xs[:, j, :], reads=[B(f"xs{j}")])


        def chunk(c, pi):
            NT = 4
            ntok = 512
            tiles = [4 * c + j for j in range(4)]
            tok0 = 512 * c
            R0 = pi * T
            mcol = lambda idx: modT[:, idx, pi:pi + 1]
            if c == 0:
                DVE.op(lambda e: e.memset(lfacc[:], 0.0), writes=[B("lfacc")])
            for j, i in enumerate(tiles):
                SP.dma(DS(f"x{j}"), xs[:, j, :], xp[R0 + i * 128:R0 + (i + 1) * 128, :], writes=[B(f"xs{j}")])
            for k in range(KC):
                p_, pb_ = next_ps()
                for j in range(4):
                    PE.op(lambda e, j=j, k=k, p_=p_: e.transpose(out=p_[:, j * 128:(j + 1) * 128], in_=xs[:, j, k * 128:(k + 1) * 128],
                                                                 identity=ident_f[:]), reads=[B(f"xs{j}"), B("ident_f")], writes=[pb_])
                ACT.op(lambda e, k=k, p_=p_: e.activation(out=hT[:, k, :], in_=p_[:, :], func=AF.Identity,
                                                          bias=mcol(SH1 + k), scale=mcol(SC1 + k)),
                       reads=[pb_, B("modT")], writes=[B(f"hT{k}")])
            hbufs = [B(f"hT{k}") for k in range(KC)]
            if STOP == 'A1':
                return

            def fm_block(wt, wb, col0, dst, dbuf, scale):
                p_, pb_ = next_ps()
                PE.group(mm(p_[:, :], [(wt[:, k, col0:col0 + 128], hT[:, k, :]) for k in range(KC)]), reads=hbufs + [wb], writes=[pb_])
                E = evac_eng()
                E.op(copy_op(E, dst, p_[:, :], scale), reads=[pb_], writes=[dbuf])

            def tm_block(wt, wb, j, ncols=512):
                p_, pb_ = next_ps()
                PE.group(mm(p_[:, 0:ncols], [(hT[:, k, j * 128:(j + 1) * 128], wt[:, k, 0:ncols]) for k in range(KC)]),
                         reads=hbufs + [wb], writes=[pb_])
                return p_, pb_

            wt, wb = load_w(wsrc(w_in, 0, D, C_QA, 512), KC, 512)
            for hp in range(4):
                fm_block(wt, wb, hp * 128, QTf[:, hp, :], B(f"oT{hp}"), 0.125)
            wt, wb = load_w(wsrc(w_in, 0, D, C_KA, 512), KC, 512)
            for hp in range(4):
                fm_block(wt, wb, hp * 128, KTf[:, hp, tok0:tok0 + 512], B(f"KTf{hp}"), None)
            for j, i in enumerate(tiles):
                p_, pb_ = tm_block(wt, wb, j)
                st, sbuf_, sds = next_stage()
                E = evac_eng()
                E.op(copy_op(E, st[:], p_[:, :]), reads=[pb_], writes=[sbuf_])
                SP.dma(sds, kfp[R0 + i * 128:R0 + (i + 1) * 128, :], st[:], reads=[sbuf_])
            if STOP == 'A2':
                return
            wt, wb = load_w(wsrc(w_in, 0, D, C_VA, 512), KC, 512)
            for j, i in enumerate(tiles):
                p_, pb_ = tm_block(wt, wb, j)
                st, sbuf_, sds = next_stage()
                ACT.op(copy_op(ACT, st[:], p_[:, :]), reads=[pb_], writes=[sbuf_])
                DVE.op(lambda e, i=i, st=st: e.tensor_copy(out=Vf[:, i, :, 0:64], in_=st[:, :].rearrange("p (h d) -> p h d", d=64)),
                       reads=[sbuf_], writes=[B("Vf")])
                SP.dma(sds, vfp[R0 + i * 128:R0 + (i + 1) * 128, :], st[:], reads=[sbuf_])
            if STOP == 'A3':
                return
            for j, i in enumerate(tiles):
                p_, pb_ = next_ps()
                PE.group(mm(p_[:, 0:8], [(hT[:, k, j * 128:(j + 1) * 128], wfa[:, k, :]) for k in range(KC)]),
                         reads=hbufs + [B("wfa")], writes=[pb_])
                DVE.op(lambda e, p_=p_: e.tensor_tensor(out=fa_t[:], in0=p_[:, 0:8], in1=bfb, op=ALU.add),
                       reads=[pb_, B("bfb")], writes=[B("fa_t")])
                ACT.op(lambda e: e.activation(out=fa_t[:], in_=fa_t[:], func=AF.Exp, scale=-1.0), reads=[B("fa_t")], writes=[B("fa_t")])
                ACT.op(lambda e: e.activation(out=fa_t[:], in_=fa_t[:], func=AF.Ln, bias=1.0, scale=1.0), reads=[B("fa_t")], writes=[B("fa_t")])
                DVE.op(lambda e, j=j: e.tensor_scalar_mul(out=lf_t[:, j, :], in0=fa_t[:], scalar1=-1.0), reads=[B("fa_t")], writes=[B(f"lf{j}")])
                SP.dma(DS(f"lf{j}"), lfp[R0 + i * 128:R0 + (i + 1) * 128, :], lf_t[:, j, :], reads=[B(f"lf{j}")])
                p2, pb2 = next_ps()
                PE.group(mm(p2[:, 0:8], [(tri_f[:], lf_t[:, j, :]), (ones_f[:], lfacc[:])]),
                         reads=[B(f"lf{j}"), B("lfacc"), B("tri_f"), B("ones_f")], writes=[pb2])
                DVE.op(lambda e, i=i, p2=p2: e.tensor_copy(out=c_all[:, i, :], in_=p2[:, 0:8]), reads=[pb2], writes=[B("c_all")])
                DVE.op(lambda e, j=j: e.tensor_tensor(out=lfacc[:], in0=lfacc[:], in1=lf_t[:, j, :], op=ALU.add),
                       reads=[B("lfacc"), B(f"lf{j}")], writes=[B("lfacc")])
            if STOP == 'A4':
                return
            wt, wb = load_w(wsrc(w_in, 0, D, C_QB, 512), KC, 512)
            for h in range(4):
                fm_block(wt, wb, h * 128, QTd[:, h, :], B(f"oT{4 + h}"), 0.125)
            wt, wb = load_w(wsrc(w_in, 0, D, C_KB, 512), KC, 512)
            for h in range(4):
                fm_block(wt, wb, h * 128, KTd[:, h, tok0:tok0 + 512], B(f"KTd{h}"), None)
            for j, i in enumerate(tiles):
                p_, pb_ = tm_block(wt, wb, j)
                st, sbuf_, sds = next_stage()
                E = evac_eng()
                E.op(copy_op(E, st[:], p_[:, :]), reads=[pb_], writes=[sbuf_])
                SP.dma(sds, kdp[R0 + i * 128:R0 + (i + 1) * 128, :], st[:], reads=[sbuf_])
            wt, wb = load_w(wsrc(w_in, 0, D, C_VB, 512), KC, 512)
            for j, i in enumerate(tiles):
                p_, pb_ = tm_block(wt, wb, j)
                st, sbuf_, sds = next_stage()
                ACT.op(copy_op(ACT, st[:], p_[:, :]), reads=[pb_], writes=[sbuf_])
                DVE.op(lambda e, i=i, st=st: e.tensor_copy(out=Vd[:, i, :, 0:128], in_=st[:, :].rearrange("p (h d) -> p h d", d=128)),
                       reads=[sbuf_], writes=[B("Vd")])
                SP.dma(sds, vdp[R0 + i * 128:R0 + (i + 1) * 128, :], st[:], reads=[sbuf_])

            if STOP == 'A':
                return
            nkb = 4 * c + 4
            p_, pb_ = next_ps()
            PE.op(lambda e, p_=p_: e.matmul(p_[:, 0:8], lhsT=elast_f[:], rhs=c_all[:, 4 * c + 1, :], start=True, stop=True),
                  reads=[B("elast_f"), B("c_all")], writes=[pb_])
            DVE.op(lambda e, p_=p_: e.tensor_copy(out=rc_bc[:], in_=p_[:, 0:8]), reads=[pb_], writes=[B("rc_bc")])
            DVE.op(lambda e: e.tensor_tensor(out=biasF[:, 0:nkb, :], in0=rc_bc[:].unsqueeze(1).to_broadcast([128, nkb, 8]),
                                             in1=c_all[:, 0:nkb, :], op=ALU.subtract),
                   reads=[B("rc_bc"), B("c_all")], writes=[B("biasF")])
            for h in range(4):
                DVE.op(lambda e, h=h: e.tensor_scalar_add(out=biasD[:, 0:nkb, h], in0=posS[:, 0:nkb, h],
                                                          scalar1=-SLOPES[h] * (512 * c + 256)),
                       reads=[B("posS")], writes=[B("biasD")])
            PT = big[:, 0:16 * 512].rearrange("p (k q) -> p k q", q=512)

            def scores(KT, ktb, QT, qtb, kslot, r0, bias_ap, bias_buf):
                for kb in range(nkb):
                    j0 = max(0, kb - 4 * c)
                    nq = (4 - j0) * 128
                    p_, pb_ = next_ps()
                    PE.op(lambda e, p_=p_, kb=kb, j0=j0, nq=nq: e.matmul(
                        p_[:, 0:nq], lhsT=KT[r0:r0 + 64, kslot, kb * 128:(kb + 1) * 128], rhs=QT[r0:r0 + 64, kslot, j0 * 128:512],
                        start=True, stop=True), reads=[ktb, qtb], writes=[pb_])
                    ACT.op(lambda e, p_=p_, kb=kb, j0=j0, nq=nq: e.activation(
                        out=PT[:, kb, j0 * 128:512], in_=p_[:, 0:nq], func=AF.Exp, bias=bias_ap(kb), scale=1.0),
                        reads=[pb_, bias_buf], writes=[B(f"big{kb}")])
                    if kb >= 4 * c:
                        DVE.op(lambda e, kb=kb, j0=j0: e.tensor_tensor(out=PT[:, kb, j0 * 128:(j0 + 1) * 128],
                                                                       in0=PT[:, kb, j0 * 128:(j0 + 1) * 128], in1=tri_b[:], op=ALU.mult),
                               reads=[B(f"big{kb}"), B("tri_b")], writes=[B(f"big{kb}")])

            def pv(j, V, vbuf, h, ncol):
                i = 4 * c + j
                p_, pb_ = next_ps()
                PE.group(mm(p_[:, 0:ncol], [(PT[:, kb, j * 128:(j + 1) * 128], V[:, kb, h, :]) for kb in range(i + 1)]),
                         reads=[B(f"big{kb}") for kb in range(i + 1)] + [vbuf], writes=[pb_])
                return p_, pb_

            for h in range(8):
                hp, r0 = h // 2, (h % 2) * 64
                scores(KTf, B(f"KTf{hp}"), QTf, B(f"oT{hp}"), hp, r0, lambda kb, h=h: biasF[:, kb, h:h + 1], B("biasF"))
                for j in range(4):
                    p_, pb_ = pv(j, Vf, B("Vf"), h, 65)
                    DVE.op(lambda e, p_=p_: e.reciprocal(out=stat[:, 0:1], in_=p_[:, 64:65]), reads=[pb_], writes=[B("stat")])
                    DVE.op(lambda e, p_=p_, j=j, h=h: e.tensor_scalar_mul(out=oAB[:, j, h * 64:(h + 1) * 64], in0=p_[:, 0:64], scalar1=stat[:, 0:1]),
                           reads=[pb_, B("stat")], writes=[B(f"oAB{j}")])
            for h in range(4):
                for m in range(2):
                    scores(KTd, B(f"KTd{h}"), QTd, B(f"oT{4 + h}"), h, m * 64, lambda kb, h=h: biasD[:, kb, h:h + 1], B("biasD"))
                    for j in range(4):
                        p_, pb_ = pv(j, Vd, B("Vd"), h, 129)
                        if m == 0:
                            DVE.op(lambda e, p_=p_, j=j: e.reciprocal(out=stat[:, 4 + j:5 + j], in_=p_[:, 128:129]), reads=[pb_], writes=[B("statd")])
                            DVE.op(lambda e, p_=p_, j=j: e.tensor_scalar_mul(out=d1buf[:, j, :], in0=p_[:, 0:128], scalar1=stat[:, 4 + j:5 + j]),
                                   reads=[pb_, B("statd")], writes=[B(f"d1_{j}")])
                        else:
                            DVE.op(lambda e, p_=p_: e.reciprocal(out=stat[:, 8:9], in_=p_[:, 128:129]), reads=[pb_], writes=[B("stat2")])
                            DVE.op(lambda e: e.tensor_tensor(out=stat[:, 9:10], in0=stat[:, 8:9], in1=lamc[:, 0:1], op=ALU.mult),
                                   reads=[B("stat2"), B("lamc")], writes=[B("stat2")])
                            DVE.op(lambda e, p_=p_, j=j: e.scalar_tensor_tensor(out=dtmp[:, 0, :], in0=p_[:, 0:128], scalar=stat[:, 9:10],
                                                                                 in1=d1buf[:, j, :], op0=ALU.mult, op1=ALU.add),
                                   reads=[pb_, B("stat2"), B(f"d1_{j}")], writes=[B("dtmp0")])
                            ACT.op(lambda e: e.activation(out=dtmp[:, 1, :], in_=dtmp[:, 0, :], func=AF.Square),
                                   reads=[B("dtmp0")], writes=[B("dtmp1")])
                            DVE.op(lambda e: e.reduce_sum(out=stat[:, 10:11], in_=dtmp[:, 1, :], axis=mybir.AxisListType.X),
                                   reads=[B("dtmp1")], writes=[B("stat3")])
                            ACT.op(lambda e: e.activation(out=stat[:, 11:12], in_=stat[:, 10:11], func=AF.Sqrt, bias=LN_EPS, scale=1.0 / 128),
                                   reads=[B("stat3")], writes=[B("stat3")])
                            DVE.op(lambda e: e.reciprocal(out=stat[:, 12:13], in_=stat[:, 11:12]), reads=[B("stat3")], writes=[B("stat3")])
                            DVE.op(lambda e, j=j, h=h: e.scalar_tensor_tensor(out=oAB[:, j, 512 + h * 128:512 + (h + 1) * 128], in0=dtmp[:, 0, :],
                                                                               scalar=stat[:, 12:13], in1=gsub, op0=ALU.mult, op1=ALU.mult),
                                   reads=[B("dtmp0"), B("stat3"), B("gsub")], writes=[B(f"oAB{j}")])

            if STOP == 'B':
                return
            phase_c(4, False, mcol, None, lambda i: yp[R0 + i * 128:R0 + (i + 1) * 128, :], tiles, hbufs)

        idx_all = stage[3][:, :].bitcast(I32)
        SP.dma(DS("pts"), idx_all[:, 0:NS * 256], pt.partition_broadcast(128), writes=[B("stage3")])
        DVE.op(lambda e: e.tensor_copy(out=stage[2][:, 0:NS * 256], in_=idx_all[:, 0:NS * 256]), reads=[B("stage3")], writes=[B("stage2")])
        POOL.op(lambda e: e.iota(small[:, 20:21], pattern=[[0, 1]], base=0, channel_multiplier=1, allow_small_or_imprecise_dtypes=True),
                writes=[B("small20")])
        DVE.op(lambda e: e.tensor_scalar(out=stage[2][:, 0:NS * 256], in0=stage[2][:, 0:NS * 256], scalar1=128.0, scalar2=small[:, 20:21],
                                         op0=ALU.mult, op1=ALU.add), reads=[B("stage2"), B("small20")], writes=[B("stage2")])
        DVE.op(lambda e: e.tensor_copy(out=idx_all[:, 0:NS * 256], in_=stage[2][:, 0:NS * 256]), reads=[B("stage2")], writes=[B("stage3")])
        ptT = cst[:, 6, :].bitcast(I32)
        with nc.allow_non_contiguous_dma(reason="tiny page-table transpose"):
            SP.dma(DS("ptT"), ptT[:, 0:NS * 2], pt.rearrange("o (c p) -> p (o c)", p=128), writes=[B("ptT")])

        def gather(dsem, out, src2d, idx_col):
            ins = nc.gpsimd.indirect_dma_start(out=out, out_offset=None, in_=src2d,
                                               in_offset=bass.IndirectOffsetOnAxis(ap=idx_col, axis=0))
            dsem.count += 16
            ins.then_inc(dsem.sem, 16)
        maskS_f = sb("maskS_f", [128, 128]); maskS_b = sb("maskS_b", [128, 128], BF16); MB_f = sb("MB_f", [128, 128])
        alibS = sb("alibS", [128, 16, 8]); alibN = sb("alibN", [128, 16]); negcs = sb("negcs", [128, 8])
        m3 = maskS_f[:, :].rearrange("p (b q) -> p b q", q=8)
        POOL.op(lambda e: e.affine_select(out=m3, in_=ones_f[:, :].rearrange("p (b q) -> p b q", q=8), pattern=[[-8, 16], [0, 8]],
                                          compare_op=ALU.is_ge, fill=0.0, base=0, channel_multiplier=1), reads=[B("ones_f")], writes=[B("maskS")])
        POOL.op(lambda e: e.affine_select(out=m3, in_=m3, pattern=[[8, 16], [1, 8]], compare_op=ALU.is_ge, fill=0.0, base=0,
                                          channel_multiplier=-1), reads=[B("maskS")], writes=[B("maskS")])
        DVE.op(lambda e: e.tensor_copy(out=maskS_b[:], in_=maskS_f[:]), reads=[B("maskS")], writes=[B("maskSb")])
        mb3 = MB_f[:, :].rearrange("p (b w) -> p b w", w=16)
        POOL.op(lambda e: e.affine_select(out=mb3, in_=ones_f[:, :].rearrange("p (b w) -> p b w", w=16), pattern=[[-16, 8], [-1, 16]],
                                          compare_op=ALU.is_gt, fill=0.0, base=0, channel_multiplier=1), reads=[B("ones_f")], writes=[B("MB")])
        POOL.op(lambda e: e.affine_select(out=mb3, in_=mb3, pattern=[[16, 8], [0, 16]], compare_op=ALU.is_ge, fill=0.0, base=15,
                                          channel_multiplier=-1), reads=[B("MB")], writes=[B("MB")])
        for hm in range(8):
            DVE.op(lambda e, hm=hm: e.tensor_scalar_add(out=alibS[:, :, hm], in0=posS[:, :, hm // 2], scalar1=-2048.0 * SLOPES[hm // 2]),
                   reads=[B("posS")], writes=[B("alibS")])
        DVE.op(lambda e: e.reduce_sum(out=alibN[:, 8:9], in_=maskS_f[:, :], axis=mybir.AxisListType.X), reads=[B("maskS")], writes=[B("alibN")])
        DVE.op(lambda e: e.tensor_scalar(out=alibN[:, 9:10], in0=alibN[:, 8:9], scalar1=-1.0, scalar2=8.0, op0=ALU.mult, op1=ALU.add),
               reads=[B("alibN")], writes=[B("alibN")])
        for hm in range(8):
            DVE.op(lambda e, hm=hm: e.tensor_scalar_mul(out=alibN[:, hm:hm + 1], in0=alibN[:, 9:10], scalar1=SLOPES[hm // 2]),
                   reads=[B("alibN")], writes=[B("alibN")])
        QBD = oAB[:, 1:3, :].rearrange("p a (k s c) -> p (a k) s c", s=16, c=16)
        biasAll = xs[:, 1:3, :].rearrange("p a (h b g) -> p (a h) b g", b=16, g=16)
        Lst = xs[:, 3, :]
        Pf = big[:, 0:2048].bitcast(F32).rearrange("p (r h) -> p h r", h=8)
        KTs = big[:, 0:4096].rearrange("p (k n) -> p k n", n=512)
        PTs = big[:, 4096:4096 + 17 * 128].rearrange("p (k n) -> p k n", n=128)
        PTb = [B(f"big{q}") for q in range(8, 13)]
        KTsb = [B(f"big{q}") for q in range(8)]
        osb = tmpB[0][:, :].bitcast(BF16)
        ckf3 = ckf.rearrange("(n p) c -> n p c", p=128); cvf3 = cvf.rearrange("(n p) c -> n p c", p=128)
        ckd3 = ckd.rearrange("(n p) c -> n p c", p=128); cvd3 = cvd.rearrange("(n p) c -> n p c", p=128)
        v3 = lambda ap: ap.rearrange("p (s q) -> p s q", q=8)
        clf3 = clf.rearrange("n (a c) -> n a c", a=1)

        def sample_tile(ts):
            r0 = NP + 16 * ts
            hbufs = [B(f"hT{k}") for k in range(KC)]
            mbc = lambda idx: modT[:, idx, r0:r0 + 16].unsqueeze(2).to_broadcast([128, 16, 8])
            rows = slice(ts * 128, (ts + 1) * 128)
            SP.dma(DS("x0"), xs[:, 0, :], xs_in[rows, :], writes=[B("xs0")])
            for k in range(KC):
                p_, pb_ = next_ps()
                PE.op(lambda e, k=k, p_=p_: e.transpose(out=p_[:, 0:128], in_=xs[:, 0, k * 128:(k + 1) * 128], identity=ident_f[:]),
                      reads=[B("xs0"), B("ident_f")], writes=[pb_])
                DVE.op(lambda e, k=k, p_=p_: e.tensor_tensor(out=v3(tmpA[0][:, 0:128]), in0=v3(p_[:, 0:128]), in1=mbc(SC1 + k), op=ALU.mult),
                       reads=[pb_, B("modT")], writes=[B("tmpA0")])
                DVE.op(lambda e, k=k: e.tensor_tensor(out=v3(hT[:, k, 0:128]), in0=v3(tmpA[0][:, 0:128]), in1=mbc(SH1 + k), op=ALU.add),
                       reads=[B("tmpA0"), B("modT")], writes=[B(f"hT{k}")])
            if ts == 0:
                DVE.op(lambda e: e.memset(oAB[:, 1:3, :], 0.0), writes=[B("oAB1"), B("oAB2")])
            qb = [B("oAB1"), B("oAB2")]

            def fm(wt, wb, col0):
                p_, pb_ = next_ps()
                PE.group(mm(p_[:, 0:128], [(wt[:, k, col0:col0 + 128], hT[:, k, 0:128]) for k in range(KC)]), reads=hbufs + [wb], writes=[pb_])
                return p_, pb_

            def tmj(wt, wb, ncols=512):
                p_, pb_ = next_ps()
                PE.group(mm(p_[:, 0:ncols], [(hT[:, k, 0:128], wt[:, k, 0:ncols]) for k in range(KC)]), reads=hbufs + [wb], writes=[pb_])
                return p_, pb_

            def q_block(col, blk0):
                wt, wb = load_w(wsrc(w_in, 0, D, col, 512), KC, 512)
                for q in range(4):
                    p_, pb_ = fm(wt, wb, q * 128)
                    DVE.op(lambda e, p_=p_, q=q: e.tensor_scalar_mul(out=QBD[0:64, blk0 + q, :, 0:8], in0=v3(p_[0:64, 0:128]), scalar1=0.125),
                           reads=[pb_], writes=qb)
                    DVE.op(lambda e, p_=p_, q=q: e.tensor_scalar_mul(out=QBD[64:128, blk0 + q, :, 8:16], in0=v3(p_[64:128, 0:128]), scalar1=0.125),
                           reads=[pb_], writes=qb)

            def k_block(col, KT, nm, oap):
                wt, wb = load_w(wsrc(w_in, 0, D, col, 512), KC, 512)
                for q in range(4):
                    p_, pb_ = fm(wt, wb, q * 128)
                    E = evac_eng()
                    E.op(copy_op(E, KT[:, q, T:T + 128], p_[:, 0:128]), reads=[pb_], writes=[B(f"{nm}{q}")])
                p_, pb_ = tmj(wt, wb)
                st, sbuf_, sds = next_stage()
                E = evac_eng()
                E.op(copy_op(E, st[:], p_[:, :]), reads=[pb_], writes=[sbuf_])
                SP.dma(sds, oap[rows, :], st[:], reads=[sbuf_])

            def v_block(col, V, vb, hd, oap):
                wt, wb = load_w(wsrc(w_in, 0, D, col, 512), KC, 512)
                p_, pb_ = tmj(wt, wb)
                st, sbuf_, sds = next_stage()
                ACT.op(copy_op(ACT, st[:], p_[:, :]), reads=[pb_], writes=[sbuf_])
                DVE.op(lambda e, st=st: e.tensor_copy(out=V[:, 16, :, 0:hd], in_=st[:, :].rearrange("p (h d) -> p h d", d=hd)),
                       reads=[sbuf_], writes=[vb])
                SP.dma(sds, oap[rows, :], st[:], reads=[sbuf_])

            q_block(C_QA, 0)
            k_block(C_KA, KTf, "KTf", kfs)
            v_block(C_VA, Vf, B("Vf"), 64, vfs)
            p_, pb_ = next_ps()
            PE.group(mm(p_[:, 0:8], [(hT[:, k, 0:128], wfa[:, k, :]) for k in range(KC)]), reads=hbufs + [B("wfa")], writes=[pb_])
            DVE.op(lambda e: e.tensor_tensor(out=fa_t[:], in0=p_[:, 0:8], in1=bfb, op=ALU.add), reads=[pb_, B("bfb")], writes=[B("fa_t")])
            ACT.op(lambda e: e.activation(out=fa_t[:], in_=fa_t[:], func=AF.Exp, scale=-1.0), reads=[B("fa_t")], writes=[B("fa_t")])
            ACT.op(lambda e: e.activation(out=fa_t[:], in_=fa_t[:], func=AF.Ln, bias=1.0, scale=1.0), reads=[B("fa_t")], writes=[B("fa_t")])
            DVE.op(lambda e: e.tensor_scalar_mul(out=lf_t[:, 0, :], in0=fa_t[:], scalar1=-1.0), reads=[B("fa_t")], writes=[B("lf0")])
            SP.dma(DS("lf0"), lfs[rows, :], lf_t[:, 0, :], reads=[B("lf0")])
            p2, pb2 = next_ps()
            PE.op(lambda e: e.matmul(p2[:, 0:8], lhsT=maskS_f[:], rhs=lf_t[:, 0, :], start=True, stop=True),
                  reads=[B("maskS"), B("lf0")], writes=[pb2])
            DVE.op(lambda e: e.tensor_scalar_mul(out=negcs[:], in0=p2[:, 0:8], scalar1=-1.0), reads=[pb2], writes=[B("negcs")])
            q_block(C_QB, 4)
            k_block(C_KB, KTd, "KTd", kds)
            v_block(C_VB, Vd, B("Vd"), 128, vds)
            if STOP == 'SA':
                return

            for hf in range(2):
                dsl = DS("lg")
                POOL._wait(Eng._deps([B("ptT")], [B("xs3")]))
                gather(dsl, Lst[:, :], clf, ptT[:, ts * 2 + hf:ts * 2 + hf + 1])
                B("xs3").w = (dsl.sem, dsl.count)
                B("xs3").r = {}
                L3 = Lst.rearrange("p (r h) -> p h r", h=8)
                for h in range(8):
                    DVE.op(lambda e, h=h: e.tensor_tensor_scan(out=Pf[:, h, :], data0=ones_f[:, :], data1=L3[:, h, :], initial=0.0,
                                                               op0=ALU.mult, op1=ALU.add), reads=[B("xs3"), B("ones_f")], writes=[B(f"big{h // 2}")])
                pfb = [B(f"big{q}") for q in range(4)]
                pE, pEb = next_ps()
                PE.op(lambda e, pE=pE: e.matmul(pE[:, 0:8], lhsT=MB_f[:], rhs=Pf[:, :, 127], start=True, stop=True),
                      reads=pfb + [B("MB")], writes=[pEb])
                DVE.op(lambda e, pE=pE: e.tensor_tensor(out=small[:, 8:16], in0=pE[:, 0:8], in1=Pf[:, :, 127], op=ALU.add),
                       reads=pfb + [pEb], writes=[B("smallTE")])
                DVE.op(lambda e: e.tensor_tensor(out=Pf, in0=small[:, 8:16].unsqueeze(2).to_broadcast([128, 8, 128]), in1=Pf, op=ALU.subtract),
                       reads=pfb + [B("smallTE")], writes=pfb)
                for hh in range(2):
                    p_, pb_ = next_ps()
                    for q in range(4):
                        h = hh * 4 + q
                        PE.op(lambda e, p_=p_, q=q, h=h: e.transpose(out=p_[:, q * 128:(q + 1) * 128], in_=Pf[:, h, :], identity=ident_f[:]),
                              reads=pfb + [B("ident_f")], writes=[pb_])
                    E = evac_eng()
                    E.op(copy_op(E, biasAll[:, hh * 4:hh * 4 + 4, hf * 8:hf * 8 + 8, :],
                                 p_[:, :].rearrange("p (q b g) -> p q b g", b=8, g=16)), reads=[pb_], writes=[B("xs1"), B("xs2")])
            if STOP == 'SB':
                return

            dso = DS("osc")
            for bl in range(16):
                for pg in range(4):
                    iK = wr_i[0] % 3; wr_i[0] += 1
                    iV = wr_i[0] % 3; wr_i[0] += 1
                    sK, sKb, sV, sVb = wring[iK], B(f"wring{iK}"), wring[iV], B(f"wring{iV}")
                    POOL._wait(Eng._deps([B("stage3")], [sKb, sVb]))
                    for j in range(4):
                        col = ts * 256 + bl * 16 + pg * 4 + j
                        ic = idx_all[:, col:col + 1]
                        gather(DS(f"wring{iK}"), sK[:, j, :], ckf, ic)
                        gather(DS(f"wring{iK}"), sK[:, 4 + j, :], ckd, ic)
                        gather(DS(f"wring{iV}"), sV[:, j, :], cvf, ic)
                        gather(DS(f"wring{iV}"), sV[:, 4 + j, :], cvd, ic)
                    sKb.w = (DS(f"wring{iK}").sem, DS(f"wring{iK}").count); sKb.r = {}
                    sVb.w = (DS(f"wring{iV}").sem, DS(f"wring{iV}").count); sVb.r = {}
                    for j in range(4):
                        g = pg * 4 + j
                        DVE.op(lambda e, j=j, g=g, sV=sV: e.tensor_copy(out=Vf[:, g, :, 0:64], in_=sV[:, j, :].rearrange("p (h d) -> p h d", d=64)),
                               reads=[sVb], writes=[B("Vf")])
                        DVE.op(lambda e, j=j, g=g, sV=sV: e.tensor_copy(out=Vd[:, g, :, 0:128], in_=sV[:, 4 + j, :].rearrange("p (h d) -> p h d", d=128)),
                               reads=[sVb], writes=[B("Vd")])
                    for j in range(4):
                        for hh in range(2):
                            p_, pb_ = next_ps()
                            PE.group([(lambda e, p_=p_, q=q, j=j, hh=hh, sK=sK: e.matmul(p_[:, q * 128:(q + 1) * 128], lhsT=sK[:, 4 * hh + j, q * 128:(q + 1) * 128],
                                                                                             rhs=ident_b[:], start=True, stop=True)) for q in range(4)],
                                     reads=[sKb, B("ident_b")], writes=[pb_])
                            E = evac_eng()
                            E.op(copy_op(E, KTs[:, 4 * hh:4 * hh + 4, j * 128:(j + 1) * 128], p_[:, :].rearrange("p (q n) -> p q n", n=128)),
                                 reads=[pb_], writes=KTsb[4 * hh:4 * hh + 4])
                    pS, pSb = next_ps()
                    PE.group([(lambda e, pS=pS, j=j, blk=blk: e.matmul(pS[:, j * 128 + blk * 16:j * 128 + blk * 16 + 16], lhsT=KTs[:, blk, j * 128:(j + 1) * 128],
                                                                        rhs=QBD[:, blk, bl, :], start=True, stop=True)) for j in range(4) for blk in range(8)],
                             reads=KTsb + qb, writes=[pSb])
                    tv = tmpA[0][:, :].rearrange("p (j c) -> p j c", c=128)
                    sv = pS[:, :].rearrange("p (j c) -> p j c", c=128)
                    hq = lambda ap: ap.rearrange("p j (h q) -> p j h q", q=8)
                    DVE.op(lambda e, pg=pg: e.tensor_tensor(out=hq(tv[:, :, 0:64]), in0=hq(sv[:, :, 0:64]),
                                                            in1=biasAll[:, :, bl, pg * 4:pg * 4 + 4].rearrange("p h g -> p g h").unsqueeze(3).to_broadcast([128, 4, 8, 8]),
                                                            op=ALU.add), reads=[pSb, B("xs1"), B("xs2")], writes=[B("tmpA0")])
                    DVE.op(lambda e, pg=pg: e.tensor_tensor(out=hq(tv[:, :, 64:128]), in0=hq(sv[:, :, 64:128]),
                                                            in1=alibS[:, pg * 4:pg * 4 + 4, :].unsqueeze(3).to_broadcast([128, 4, 8, 8]),
                                                            op=ALU.add), reads=[pSb, B("alibS")], writes=[B("tmpA0")])
                    ACT.op(lambda e, pg=pg: e.activation(out=PTs[:, pg * 4:pg * 4 + 4, :], in_=tv, func=AF.Exp), reads=[B("tmpA0")], writes=PTb)
                pS, pSb = next_ps()
                PE.group([(lambda e, pS=pS, blk=blk: e.matmul(pS[:, blk * 16:blk * 16 + 16], lhsT=(KTf if blk < 4 else KTd)[:, blk % 4, T:T + 128],
                                                               rhs=QBD[:, blk, bl, :], start=True, stop=True)) for blk in range(8)],
                         reads=[B(f"KTf{q}") for q in range(4)] + [B(f"KTd{q}") for q in range(4)] + qb, writes=[pSb])
                t1 = tmpA[1][:, 0:128]
                h2 = lambda ap: ap.rearrange("p (h q) -> p h q", q=8)
                DVE.op(lambda e: e.tensor_tensor(out=h2(t1[:, 0:64]), in0=h2(pS[:, 0:64]), in1=negcs[:, :].unsqueeze(2).to_broadcast([128, 8, 8]), op=ALU.add),
                       reads=[pSb, B("negcs")], writes=[B("tmpA1")])
                DVE.op(lambda e: e.tensor_tensor(out=h2(t1[:, 64:128]), in0=h2(pS[:, 64:128]), in1=alibN[:, 0:8].unsqueeze(2).to_broadcast([128, 8, 8]), op=ALU.add),
                       reads=[pSb, B("alibN")], writes=[B("tmpA1")])
                ACT.op(lambda e: e.activation(out=tmpA[1][:, 128:256], in_=t1, func=AF.Exp), reads=[B("tmpA1")], writes=[B("tmpA1")])
                DVE.op(lambda e: e.tensor_tensor(out=h2(PTs[:, 16, :]), in0=h2(tmpA[1][:, 128:256]),
                                                 in1=maskS_f[:, bl * 8:bl * 8 + 8].unsqueeze(1).to_broadcast([128, 16, 8]), op=ALU.mult),
                       reads=[B("tmpA1"), B("maskS")], writes=PTb)
                banks = [(list(range(0, 4)), 65), (list(range(4, 8)), 65), ([8, 9, 10], 129), ([11, 12, 13], 129), ([14, 15], 129)]
                res = {}
                for hms, ncol in banks:
                    p_, pb_ = next_ps()
                    fns = []
                    for n_, hm in enumerate(hms):
                        h = hm if hm < 8 else (hm - 8) // 2
                        V = Vf if hm < 8 else Vd
                        fns += mm(p_[0:8, n_ * ncol:(n_ + 1) * ncol], [(PTs[:, kb, hm * 8:(hm + 1) * 8], V[:, kb, h, :]) for kb in range(17)])
                        res[hm] = (p_, pb_, n_ * ncol)
                    PE.group(fns, reads=PTb + [B("Vf"), B("Vd")], writes=[pb_])
                for hm in range(16):
                    p_, pb_, off = res[hm]
                    if hm < 8:
                        DVE.op(lambda e, p_=p_, off=off: e.reciprocal(out=stat[0:8, 0:1], in_=p_[0:8, off + 64:off + 65]), reads=[pb_], writes=[B("stat")])
                        DVE.op(lambda e, p_=p_, off=off, hm=hm: e.tensor_scalar_mul(out=osb[0:8, hm * 64:(hm + 1) * 64], in0=p_[0:8, off:off + 64],
                                                                                     scalar1=stat[0:8, 0:1]), reads=[pb_, B("stat")], writes=[B("tmpB0")])
                    else:
                        h, m = (hm - 8) // 2, (hm - 8) % 2
                        DVE.op(lambda e, p_=p_, off=off: e.reciprocal(out=stat[0:8, 8:9], in_=p_[0:8, off + 128:off + 129]), reads=[pb_], writes=[B("stat2")])
                        if m == 0:
                            DVE.op(lambda e, p_=p_, off=off, h=h: e.tensor_scalar_mul(out=d1buf[0:8, h, :], in0=p_[0:8, off:off + 128], scalar1=stat[0:8, 8:9]),
                                   reads=[pb_, B("stat2")], writes=[B("d1_0")])
                        else:
                            DVE.op(lambda e: e.tensor_tensor(out=stat[0:8, 9:10], in0=stat[0:8, 8:9], in1=lamc[0:8, 0:1], op=ALU.mult),
                                   reads=[B("stat2"), B("lamc")], writes=[B("stat2")])
                            DVE.op(lambda e, p_=p_, off=off, h=h: e.scalar_tensor_tensor(out=d1buf[0:8, h, :], in0=p_[0:8, off:off + 128], scalar=stat[0:8, 9:10],
                                                                                          in1=d1buf[0:8, h, :], op0=ALU.mult, op1=ALU.add),
                                   reads=[pb_, B("stat2"), B("d1_0")], writes=[B("d1_0")])
                sq = tmpA[1][0:8, 0:512].rearrange("p (h e) -> p h e", e=128)
                ACT.op(lambda e: e.activation(out=sq, in_=d1buf[0:8, :, :], func=AF.Square), reads=[B("d1_0")], writes=[B("tmpA1")])
                DVE.op(lambda e: e.reduce_sum(out=stat[0:8, 10:14], in_=sq, axis=mybir.AxisListType.X), reads=[B("tmpA1")], writes=[B("stat3")])
                ACT.op(lambda e: e.activation(out=stat[0:8, 10:14], in_=stat[0:8, 10:14], func=AF.Sqrt, bias=LN_EPS, scale=1.0 / 128),
                       reads=[B("stat3")], writes=[B("stat3")])
                DVE.op(lambda e: e.reciprocal(out=stat[0:8, 10:14], in_=stat[0:8, 10:14]), reads=[B("stat3")], writes=[B("stat3")])
                DVE.op(lambda e: e.tensor_tensor(out=d1buf[0:8, :, :], in0=d1buf[0:8, :, :], in1=stat[0:8, 10:14].unsqueeze(2).to_broadcast([8, 4, 128]), op=ALU.mult),
                       reads=[B("d1_0"), B("stat3")], writes=[B("d1_0")])
                DVE.op(lambda e: e.tensor_tensor(out=osb[0:8, 512:1024].rearrange("p (h e) -> p h e", e=128), in0=d1buf[0:8, :, :],
                                                 in1=gsub[0:8, :].unsqueeze(1).to_broadcast([8, 4, 128]), op=ALU.mult),
                       reads=[B("d1_0"), B("gsub")], writes=[B("tmpB0")])
                SP.dma(dso, osc[ts * 128 + bl * 8:ts * 128 + bl * 8 + 8, :], osb[0:8, :], reads=[B("tmpB0")], writes=[B("osc")])
            B("osc").w = (dso.sem, dso.count)
            SP.dma(dso, oAB[:, 0, :], osc[rows, :], reads=[B("osc")], writes=[B("oAB0")])
            if STOP == 'SC':
                return
            phase_c(1, True, None, mbc, lambda i: ys[rows, :], [0], hbufs)

        nch = int(os.environ.get("KDEV_NCH", NCH))
        if STOP == 'setup':
            nch = 0
        for pi in range(NP):
            for c in range(nch):
                chunk(c, pi)
        if STOP != 'setup' and os.environ.get('KDEV_NOSAMPLE') is None:
            for ts in range(NS):
                sample_tile(ts)

        SP._wait({d.sem: d.count for d in _ds.values() if d.count})
        SP._wait({E.sem: E.count for E in (PE, ACT, DVE, POOL)})
    return nc


_INPUT_KEYS = None


def kernel(**inputs):
    ncores = int(os.environ.get("KDEV_CORES", 8))
    NP = int(os.environ.get("KDEV_NP", 8 // ncores))
    NS = int(os.environ.get("KDEV_NS", 8 // ncores))
    n_phys = int(inputs["cache_k_fox"].shape[1])
    nc = build_program(NP=NP, NS=NS, n_phys=n_phys)
    f = lambda a: np.ascontiguousarray(np.asarray(a))
    pools = {
        "ckf": f(inputs["cache_k_fox"]).reshape(n_phys * 128, 512), "cvf": f(inputs["cache_v_fox"]).reshape(n_phys * 128, 512),
        "ckd": f(inputs["cache_k_diff"]).reshape(n_phys * 128, 512), "cvd": f(inputs["cache_v_diff"]).reshape(n_phys * 128, 512),
        "clf": f(inputs["cache_logf_fox"]).reshape(n_phys, 1024),
    }
    shared = {
        "w_ada": f(inputs["w_ada"][0]), "b_ada": f(inputs["b_ada"]), "w_in": f(inputs["w_in"][0]),
        "b_forget": f(inputs["b_forget"]), "lq1": f(inputs["lambda_q1"]), "lk1": f(inputs["lambda_k1"]),
        "lq2": f(inputs["lambda_q2"]), "lk2": f(inputs["lambda_k2"]), "subln": f(inputs["subln_gain"]),
        "w_ba": f(inputs["w_branch_a"][0]), "w_bb": f(inputs["w_branch_b"][0]), "w_out": f(inputs["w_out"][0]),
        "ln1g": f(inputs["ln1_gain"]), "ln1b": f(inputs["ln1_bias"]),
        "w_fg": f(inputs["w_ffn_gate"][0]), "w_fu": f(inputs["w_ffn_up"][0]), "w_fd": f(inputs["w_ffn_down"][0]),
        "ln2g": f(inputs["ln2_gain"]), "ln2b": f(inputs["ln2_bias"]),
    }
    in_maps = []
    for b in range(ncores):
        sq = slice(16 * NS * b, 16 * NS * (b + 1))
        m = {
            "xp": f(inputs["x_prompt"][NP * b:NP * (b + 1)]).reshape(NP * T, D),
            "xs": f(inputs["x_sample"][sq]).reshape(NS * 128, D),
            "cp": f(inputs["c_prompt"][NP * b:NP * (b + 1)]), "cs": f(inputs["c_sample"][sq]),
            "pt": f(inputs["page_table"][sq]).reshape(1, NS * 256).astype(np.int32),
        }
        m.update(pools)
        m.update(shared)
        in_maps.append(m)
    res = run_bass_kernel_spmd(nc, in_maps, core_ids=list(range(ncores)))
    R = res.results
    nb = len(R)

    def cat(name, shape):
        return np.stack([R[b][name] for b in range(nb)], 0).reshape(shape)
    npq, nsq = nb * NP, nb * NS * 16
    outs = (cat("yp", (npq, T, D)), cat("ys", (nsq, 8, D)),
            cat("kfp", (1, npq, T, 8, 64)), cat("vfp", (1, npq, T, 8, 64)), cat("lfp", (1, npq, T, 8)),
            cat("kdp", (1, npq, T, 4, 2, 64)), cat("vdp", (1, npq, T, 4, 128)),
            cat("kfs", (1, nsq, 8, 8, 64)), cat("vfs", (1, nsq, 8, 8, 64)), cat("lfs", (1, nsq, 8, 8)),
            cat("kds", (1, nsq, 8, 4, 2, 64)), cat("vds", (1, nsq, 8, 4, 128)))
    return tuple(np.ascontiguousarray(o.astype(np.float32)) for o in outs)
```

```python
import os
from contextlib import ExitStack
import numpy as np
import concourse.bass as bass
import concourse.mybir as mybir
from concourse.bass_utils import run_bass_kernel_spmd

F32 = mybir.dt.float32
BF16 = mybir.dt.bfloat16
I32 = mybir.dt.int32
AF = mybir.ActivationFunctionType
ALU = mybir.AluOpType

D = 1024
KC = 8
T = 2048
NCH = 4
DFF = 2816
FC = 22
DIN = 5128
ALPHA = 2.0 ** 0.25
LN_EPS = 1e-5
LAMBDA_INIT = 0.2
SLOPES = [2.0 ** (-8.0 * (h + 1) / 4) for h in range(4)]
C_QA, C_KA, C_VA, C_FA, C_QB, C_KB, C_VB, C_GA, C_GB = 0, 512, 1024, 1536, 1544, 2056, 2568, 3080, 4104


class Buf:
    __slots__ = ("w", "r")

    def __init__(self):
        self.w = None
        self.r = {}


class DSem:
    def __init__(self, nc, name):
        self.sem = nc.alloc_semaphore(name)
        self.count = 0


class Eng:
    def __init__(self, nc, name, eng):
        self.eng = eng
        self.sem = nc.alloc_semaphore("sem_" + name)
        self.count = 0
        self.waited = {}

    def _wait(self, deps):
        for sem, val in deps.items():
            if self.waited.get(sem, 0) < val:
                self.eng.wait_ge(sem, val)
                self.waited[sem] = val

    @staticmethod
    def _deps(reads, writes):
        deps = {}
        for b in reads:
            if b.w is not None and deps.get(b.w[0], 0) < b.w[1]:
                deps[b.w[0]] = b.w[1]
        for b in writes:
            if b.w is not None and deps.get(b.w[0], 0) < b.w[1]:
                deps[b.w[0]] = b.w[1]
            for s, v in b.r.items():
                if deps.get(s, 0) < v:
                    deps[s] = v
        return deps

    @staticmethod
    def _mark(tok, reads, writes):
        s, v = tok
        for b in reads:
            if b.r.get(s, 0) < v:
                b.r[s] = v
        for b in writes:
            b.w = tok
            b.r = {}

    def op(self, fn, reads=(), writes=()):
        return self.group([fn], reads, writes)

    def group(self, fns, reads=(), writes=()):
        self._wait(self._deps(reads, writes))
        ins = None
        for fn in fns:
            ins = fn(self.eng)
        self.count += 1
        ins.then_inc(self.sem, 1)
        tok = (self.sem, self.count)
        self._mark(tok, reads, writes)
        return tok

    def dma(self, dsem, out, in_, reads=(), writes=()):
        self._wait(self._deps(reads, writes))
        ins = self.eng.dma_start(out=out, in_=in_)
        dsem.count += 16
        ins.then_inc(dsem.sem, 16)
        tok = (dsem.sem, dsem.count)
        self._mark(tok, reads, writes)
        return tok


def build_program(NP=2, NS=2, n_phys=2560, do_sample=True):
    WCACHE = os.environ.get('KDEV_NOWCACHE') is None
    NR = NP + 16 * NS
    STOP = os.environ.get('KDEV_STOP', '')
    nc = bass.Bass("TRN2", target_bir_lowering=False)

    def din(name, shape, dt=F32):
        return nc.dram_tensor(name, list(shape), dt, kind="ExternalInput").ap()

    def dout(name, shape, dt=F32):
        return nc.dram_tensor(name, list(shape), dt, kind="ExternalOutput").ap()

    xp = din("xp", [NP * T, D]); xs_in = din("xs", [NS * 128, D])
    cp = din("cp", [NP, D]); cs = din("cs", [NS * 16, D])
    pt = din("pt", [1, NS * 256], I32)
    cpool = din("cpool", [n_phys * 128, 2048])
    clf = din("clf", [n_phys, 1024])
    w_ada = din("w_ada", [D, 6 * D]); b_ada = din("b_ada", [1, 6 * D])
    w_in = din("w_in", [D, DIN]); b_forget = din("b_forget", [1, 8])
    lq1 = din("lq1", [1, 64]); lk1 = din("lk1", [1, 64]); lq2 = din("lq2", [1, 64]); lk2 = din("lk2", [1, 64])
    subln = din("subln", [1, 128])
    w_ba = din("w_ba", [512, D]); w_bb = din("w_bb", [512, D]); w_out = din("w_out", [D, D])
    ln1g = din("ln1g", [1, D]); ln1b = din("ln1b", [1, D])
    w_fg = din("w_fg", [D, DFF]); w_fu = din("w_fu", [D, DFF]); w_fd = din("w_fd", [DFF, D])
    ln2g = din("ln2g", [1, D]); ln2b = din("ln2b", [1, D])
    yp = dout("yp", [NP * T, D]); ys = dout("ys", [NS * 128, D])
    kfp = dout("kfp", [NP * T, 512]); vfp = dout("vfp", [NP * T, 512]); lfp = dout("lfp", [NP * T, 8])
    kdp = dout("kdp", [NP * T, 512]); vdp = dout("vdp", [NP * T, 512])
    kfs = dout("kfs", [NS * 128, 512]); vfs = dout("vfs", [NS * 128, 512]); lfs = dout("lfs", [NS * 128, 8])
    kds = dout("kds", [NS * 128, 512]); vds = dout("vds", [NS * 128, 512])
    osc = nc.dram_tensor("osc", [NS * 128, 1024], BF16, kind="Internal").ap()

    es = ExitStack()
    with es:
        def sb(name, shape, dt=F32):
            return es.enter_context(nc.sbuf_tensor(name, list(shape), dt))

        cst = sb("cst", [128, 8, 128])
        bfb = cst[:, 0, 0:8]; gsub = cst[:, 1, :]
        wfa_t = sb("wfa", [128, 256], BF16)
        wfa = wfa_t[:, 0:64].rearrange("p (k n) -> p k n", n=8)
        lng = sb("lng", [128, 4, D])
        xs = sb("xs_sb", [128, 4, D])
        wflat = [sb(f"wring{i}", [128, KC * 512], BF16) for i in range(3)]
        stage = [sb(f"stage{i}", [128, 512]) for i in range(4)]
        ident_f = sb("ident_f", [128, 128]); ident_b = sb("ident_b", [128, 128], BF16)
        tri_f = sb("tri_f", [128, 128]); tri_b = sb("tri_b", [128, 128], BF16)
        elast_f = sb("elast_f", [128, 128]); ones_f = sb("ones_f", [128, 128])
        ones_b = sb("ones_b", [128, 128], BF16)
        posS = sb("posS", [128, 16, 4])
        modT = sb("modT", [128, 48, NR])
        lamc = sb("lamc", [128, 4])
        small = sb("small", [128, 64])
        cT = sb("cT", [128, KC, NR], BF16)
        KTf = sb("KTf", [128, 4, T + 128], BF16); KTd = sb("KTd", [128, 4, T + 128], BF16)
        Vf = sb("Vf", [128, 17, 8, 65], BF16); Vd = sb("Vd", [128, 17, 4, 129], BF16)
        c_all = sb("c_all", [128, 17, 8]); lfacc = sb("lfacc", [128, 8]); lf_t = sb("lf_t", [128, 4, 8])
        fa_t = sb("fa_t", [128, 8]); rc_bc = sb("rc_bc", [128, 8])
        biasF = sb("biasF", [128, 16, 8]); biasD = sb("biasD", [128, 16, 4])
        hT = sb("hT", [128, KC, 512], BF16)
        big = sb("big", [128, FC * 512], BF16)
        oAB = sb("oAB", [128, 4, D], BF16)
        oT = sb("oT", [128, KC, 512], BF16)
        QTf = oT[:, 0:4, :]; QTd = oT[:, 4:8, :]
        tmpA = [sb(f"tmpA{i}", [128, 512]) for i in range(2)]
        tmpB = [sb(f"tmpB{i}", [128, 512]) for i in range(2)]
        dtmp = sb("dtmp", [128, 2, 128]); d1buf = sb("d1buf", [128, 4, 128])
        stat = sb("stat", [128, 16])
        bnst = sb("bnst", [128, 12])
        wring = [w[:, :].rearrange("p (k n) -> p k n", n=512) for w in wflat]
        wdring = [w[:, 0:FC * 128].rearrange("p (k n) -> p k n", n=128) for w in wflat]
        modrow = stage[0][0:NR, :]; badab = stage[1][0:NR, :]
        mtm = big[:, 0:4096].bitcast(F32).rearrange("p (j n) -> p j n", n=512)
        psum = [es.enter_context(nc.psum_tensor(f"ps{i}", [128, 512], F32)) for i in range(8)]
        print("sbuf bytes remaining/partition:", nc.sbuf_bytes_remaining)
        es.enter_context(nc.Block())

        PE = Eng(nc, "pe", nc.tensor); ACT = Eng(nc, "act", nc.scalar); DVE = Eng(nc, "dve", nc.vector)
        POOL = Eng(nc, "pool", nc.gpsimd); SP = Eng(nc, "sp", nc.sync)
        _bufs = {}

        def B(name):
            if name not in _bufs:
                _bufs[name] = Buf()
            return _bufs[name]
        _ds = {}

        def DS(name):
            if name not in _ds:
                _ds[name] = DSem(nc, "d_" + name)
            return _ds[name]

        ps_i = [0]

        def next_ps():
            i = ps_i[0] % 8
            ps_i[0] += 1
            return psum[i], B(f"ps{i}")

        st_i = [0]

        def next_stage():
            i = st_i[0] % 3
            st_i[0] += 1
            return stage[i], B(f"stage{i}"), DS(f"stage{i}")

        wr_i = [0]

        wcache = {}

        def load_block(key, fills, nflat):
            i = wr_i[0] % 3
            wr_i[0] += 1
            sbuf_ = B(f"wring{i}")
            if key is not None and key in wcache:
                scr, scb = wcache[key]
                POOL.dma(DS(f"wring{i}"), wflat[i][:, 0:nflat], scr, reads=[scb], writes=[sbuf_])
                return i, sbuf_
            POOL._wait(Eng._deps([], [sbuf_]))
            for dst_fn, src in fills:
                POOL.dma(DS(f"wring{i}"), dst_fn(i), src)
            sbuf_.w = (DS(f"wring{i}").sem, DS(f"wring{i}").count)
            sbuf_.r = {}
            if key is not None and WCACHE:
                scr = nc.dram_tensor("wc_" + key, [128, nflat], BF16, kind="Internal").ap()
                scb = Buf()
                SP.dma(DS(f"wst{i}"), scr, wflat[i][:, 0:nflat], reads=[sbuf_], writes=[scb])
                wcache[key] = (scr, scb)
            return i, sbuf_

        def load_w(src, kc, ncols):
            src3, key = src
            i, b_ = load_block(key, [(lambda i: wring[i][:, 0:kc, 0:ncols], src3)], kc * 512)
            return wring[i], b_

        def wsrc(w, r0, nr, c0, ncols, cache=True):
            ap = w[r0:r0 + nr, c0:c0 + ncols].rearrange("(k p) n -> p k n", p=128)
            return ap, (f"{w.tensor.name}_{r0}_{c0}_{ncols}" if cache else None)

        ev_i = [0]

        def evac_eng():
            ev_i[0] += 1
            return ACT if ev_i[0] % 2 else DVE

        def copy_op(E, out, in_, scale=None):
            if E is ACT:
                if scale is None:
                    return lambda e: e.copy(out=out, in_=in_)
                return lambda e: e.mul(out=out, in_=in_, mul=scale)
            if scale is None:
                return lambda e: e.tensor_copy(out=out, in_=in_)
            return lambda e: e.tensor_scalar_mul(out=out, in0=in_, scalar1=scale)

        def mm(out, pairs):
            n = len(pairs)
            return [(lambda e, l=l, r=r, i=i: e.matmul(out, lhsT=l, rhs=r, start=(i == 0), stop=(i == n - 1)))
                    for i, (l, r) in enumerate(pairs)]

        POOL.op(lambda e: e.memset(ones_f[:], 1.0), writes=[B("ones_f")])
        POOL.op(lambda e: e.memset(ident_f[:], 0.0), writes=[B("ident_f")])
        POOL.op(lambda e: e.affine_select(out=ident_f[:], in_=ident_f[:], pattern=[[-1, 128]], compare_op=ALU.not_equal,
                                          fill=1.0, base=0, channel_multiplier=1), reads=[B("ident_f")], writes=[B("ident_f")])
        POOL.op(lambda e: e.affine_select(out=tri_f[:], in_=ones_f[:], pattern=[[1, 128]], compare_op=ALU.is_ge,
                                          fill=0.0, base=0, channel_multiplier=-1), reads=[B("ones_f")], writes=[B("tri_f")])
        POOL.op(lambda e: e.affine_select(out=elast_f[:], in_=ones_f[:], pattern=[[0, 128]], compare_op=ALU.is_equal,
                                          fill=0.0, base=-127, channel_multiplier=1), reads=[B("ones_f")], writes=[B("elast_f")])
        DVE.op(lambda e: e.tensor_copy(out=ident_b[:], in_=ident_f[:]), reads=[B("ident_f")], writes=[B("ident_b")])
        DVE.op(lambda e: e.tensor_copy(out=tri_b[:], in_=tri_f[:]), reads=[B("tri_f")], writes=[B("tri_b")])
        DVE.op(lambda e: e.tensor_copy(out=ones_b[:], in_=ones_f[:]), reads=[B("ones_f")], writes=[B("ones_b")])
        POOL.op(lambda e: e.iota(posS[:, :, 0], pattern=[[128, 16]], base=0, channel_multiplier=1,
                                 allow_small_or_imprecise_dtypes=True), writes=[B("posS")])
        for h in (1, 2, 3):
            DVE.op(lambda e, h=h: e.tensor_scalar_mul(out=posS[:, :, h], in0=posS[:, :, 0], scalar1=SLOPES[h]),
                   reads=[B("posS")], writes=[B("posS")])
        DVE.op(lambda e: e.tensor_scalar_mul(out=posS[:, :, 0], in0=posS[:, :, 0], scalar1=SLOPES[0]),
               reads=[B("posS")], writes=[B("posS")])
        POOL.op(lambda e: e.memset(Vf[:, :, :, 64:65], 1.0), writes=[B("Vf")])
        POOL.op(lambda e: e.memset(Vd[:, :, :, 128:129], 1.0), writes=[B("Vd")])
        POOL.op(lambda e: e.memset(lfacc[:], 0.0), writes=[B("lfacc")])
        SP.dma(DS("c0"), bfb, b_forget.partition_broadcast(128), writes=[B("bfb")])
        for i, v in enumerate((ln1g, ln1b, ln2g, ln2b)):
            SP.dma(DS("c0"), lng[:, i, :], v.partition_broadcast(128), writes=[B("lng")])
        SP.dma(DS("c0"), gsub, subln.partition_broadcast(128), writes=[B("gsub")])
        for i, v in enumerate((lq1, lk1, lq2, lk2)):
            SP.dma(DS("c0"), cst[:, 2 + i, 0:64], v.partition_broadcast(128), writes=[B("cstl")])
        for nm in ("bfb", "lng", "gsub", "cstl"):
            B(nm).w = (DS("c0").sem, DS("c0").count)
        DVE.op(lambda e: e.tensor_scalar_mul(out=gsub, in0=gsub, scalar1=1.0 - LAMBDA_INIT), reads=[B("gsub")], writes=[B("gsub")])
        DVE.op(lambda e: e.tensor_tensor(out=stage[2][:, 256:320], in0=cst[:, 2, 0:64], in1=cst[:, 3, 0:64], op=ALU.mult),
               reads=[B("cstl")], writes=[B("stage2")])
        DVE.op(lambda e: e.tensor_tensor(out=stage[2][:, 320:384], in0=cst[:, 4, 0:64], in1=cst[:, 5, 0:64], op=ALU.mult),
               reads=[B("cstl")], writes=[B("stage2")])
        DVE.op(lambda e: e.reduce_sum(out=small[:, 0:2], in_=stage[2][:, 256:384].rearrange("p (a b) -> p a b", b=64),
                                      axis=mybir.AxisListType.X), reads=[B("stage2")], writes=[B("small")])
        ACT.op(lambda e: e.activation(out=small[:, 2:4], in_=small[:, 0:2], func=AF.Exp), reads=[B("small")], writes=[B("small")])
        DVE.op(lambda e: e.tensor_tensor(out=small[:, 4:5], in0=small[:, 3:4], in1=small[:, 2:3], op=ALU.subtract),
               reads=[B("small")], writes=[B("small")])
        DVE.op(lambda e: e.tensor_scalar_add(out=lamc[:, 0:1], in0=small[:, 4:5], scalar1=-LAMBDA_INIT), reads=[B("small")], writes=[B("lamc")])

        SP.dma(DS("c1"), xs[0:NP, 0, :], cp, writes=[B("xs0")])
        SP.dma(DS("c1"), xs[NP:NR, 0, :], cs, writes=[B("xs0")])
        for k in range(KC):
            p_, pb_ = next_ps()
            PE.op(lambda e, k=k, p_=p_: e.transpose(out=p_[:, 0:NR], in_=xs[0:NR, 0, k * 128:(k + 1) * 128], identity=ident_f[0:NR, 0:NR]),
                  reads=[B("xs0"), B("ident_f")], writes=[pb_])
            DVE.op(lambda e, k=k, p_=p_: e.tensor_copy(out=cT[:, k, :], in_=p_[:, 0:NR]), reads=[pb_], writes=[B("cT")])
        for blk in range(12):
            wt, wb = load_w(wsrc(w_ada, 0, D, blk * 512, 512, cache=False), KC, 512)
            SP.dma(DS("bada"), badab, b_ada[0:1, blk * 512:(blk + 1) * 512].partition_broadcast(NR), writes=[B("stage1")])
            p_, pb_ = next_ps()
            PE.group(mm(p_[0:NR, :], [(cT[:, k, :], wt[:, k, :]) for k in range(KC)]), reads=[B("cT"), wb], writes=[pb_])
            DVE.op(lambda e, p_=p_: e.tensor_tensor(out=modrow, in0=p_[0:NR, :], in1=badab, op=ALU.add),
                   reads=[pb_, B("stage1")], writes=[B("stage0")])
            if blk in (2, 3, 8, 9):
                DVE.op(lambda e: e.tensor_scalar_add(out=modrow, in0=modrow, scalar1=1.0), reads=[B("stage0")], writes=[B("stage0")])
            p2, pb2 = next_ps()
            for q in range(4):
                PE.op(lambda e, q=q, p2=p2: e.transpose(out=p2[:, q * NR:(q + 1) * NR], in_=modrow[:, q * 128:(q + 1) * 128],
                                                        identity=ident_f[0:NR, 0:NR]), reads=[B("stage0"), B("ident_f")], writes=[pb2])
            DVE.op(lambda e, blk=blk, p2=p2: e.tensor_copy(out=modT[:, blk * 4:(blk + 1) * 4, :],
                                                          in_=p2[:, 0:4 * NR].rearrange("p (q r) -> p q r", r=NR)),
                   reads=[pb2], writes=[B("modT")])
        SH1, SC1, G1, SH2, SC2, G2 = 0, 8, 16, 24, 32, 40

        POOL.dma(DS("wfa"), wfa, wsrc(w_in, 0, D, C_FA, 8)[0], writes=[B("wfa")])

        def phase_c(NT, SAMPLE, mcol, mbc, y_rows, tiles, hbufs):
            ntok = NT * 128
            v3 = lambda ap: ap.rearrange("p (s q) -> p s q", q=8)

            def tm_block(wt, wb, j, ncols=512):
                p_, pb_ = next_ps()
                PE.group(mm(p_[:, 0:ncols], [(hT[:, k, j * 128:(j + 1) * 128], wt[:, k, 0:ncols]) for k in range(KC)]),
                         reads=hbufs + [wb], writes=[pb_])
                return p_, pb_
            oTb = [B(f"oT{k}") for k in range(KC)]

            def transpose8(src_fn, src_buf, j):
                for half in range(2):
                    p_, pb_ = next_ps()
                    PE.group([(lambda e, q=q, p_=p_: e.matmul(p_[:, q * 128:(q + 1) * 128], lhsT=src_fn(4 * half + q), rhs=ident_b[:],
                                                              start=True, stop=True)) for q in range(4)],
                             reads=[src_buf, B("ident_b")], writes=[pb_])
                    E = evac_eng()
                    E.op(copy_op(E, oT[:, 4 * half:4 * half + 4, j * 128:(j + 1) * 128], p_[:, :].rearrange("p (q t) -> p q t", t=128)),
                         reads=[pb_], writes=oTb[4 * half:4 * half + 4])

            for j in range(NT):
                transpose8(lambda q, j=j: oAB[:, j, q * 128:(q + 1) * 128], B(f"oAB{j}"), j)
            for n in range(2):
                i3, wab_b = load_block(f"wab_{n}", [(lambda i: wring[i][:, 0:4, :], wsrc(w_ba, 0, 512, n * 512, 512)[0]),
                                                     (lambda i: wring[i][:, 4:8, :], wsrc(w_bb, 0, 512, n * 512, 512)[0])], KC * 512)
                wab_t = wring[i3]
                for br in range(2):
                    wga, wgab = load_w(wsrc(w_in, 0, D, (C_GA if br == 0 else C_GB) + n * 512, 512), KC, 512)
                    for j in range(NT):
                        mtb = [B(f"big{2 * j}"), B(f"big{2 * j + 1}")]
                        pm, pmb = next_ps()
                        PE.group(mm(pm[:, :], [(oT[:, 4 * br + q, j * 128:(j + 1) * 128], wab_t[:, 4 * br + q, :]) for q in range(4)]),
                                 reads=oTb + [wab_b], writes=[pmb])
                        pg, pgb = tm_block(wga, wgab, j)
                        ACT.op(lambda e, pg=pg, j=j: e.activation(out=tmpA[j % 2][:], in_=pg[:, :], func=AF.Sigmoid),
                               reads=[pgb], writes=[B(f"tmpA{j % 2}")])
                        if br == 0:
                            DVE.op(lambda e, pm=pm, j=j: e.tensor_tensor(out=mtm[:, j, :], in0=pm[:, :], in1=tmpA[j % 2][:], op=ALU.mult),
                                   reads=[pmb, B(f"tmpA{j % 2}")], writes=mtb)
                        else:
                            DVE.op(lambda e, pm=pm, j=j: e.tensor_tensor(out=tmpB[j % 2][:], in0=pm[:, :], in1=tmpA[j % 2][:], op=ALU.mult),
                                   reads=[pmb, B(f"tmpA{j % 2}")], writes=[B(f"tmpB{j % 2}")])
                            DVE.op(lambda e, j=j, n=n: e.tensor_tensor(out=oAB[:, j, n * 512:(n + 1) * 512], in0=mtm[:, j, :], in1=tmpB[j % 2][:], op=ALU.add),
                                   reads=mtb + [B(f"tmpB{j % 2}")], writes=[B(f"oAB{j}")])
            for j in range(NT):
                transpose8(lambda q, j=j: oAB[:, j, q * 128:(q + 1) * 128], B(f"oAB{j}"), j)

            def lin_back(get_w, nk, rhs_fn, rhs_bufs, gidx):
                for nchunk in range(8):
                    wt, wb, wsel = get_w(nchunk)
                    p_, pb_ = next_ps()
                    PE.group(mm(p_[:, 0:ntok], [(wsel(wt, k), rhs_fn(k)) for k in range(nk)]), reads=rhs_bufs + [wb], writes=[pb_])
                    tb = tmpA[nchunk % 2]
                    tbb = B(f"tmpA{nchunk % 2}")
                    if not SAMPLE:
                        ACT.op(lambda e, p_=p_, tb=tb, nchunk=nchunk: e.activation(out=tb[:, 0:ntok], in_=p_[:, 0:ntok], func=AF.Identity,
                                                                                    scale=mcol(gidx + nchunk)),
                               reads=[pb_, B("modT")], writes=[tbb])
                    else:
                        DVE.op(lambda e, p_=p_, tb=tb, nchunk=nchunk: e.tensor_tensor(out=v3(tb[:, 0:128]), in0=v3(p_[:, 0:128]),
                                                                                       in1=mbc(gidx + nchunk), op=ALU.mult),
                               reads=[pb_, B("modT")], writes=[tbb])
                    p2, pb2 = next_ps()
                    for j in range(NT):
                        PE.op(lambda e, j=j, p2=p2, tb=tb: e.transpose(out=p2[:, j * 128:(j + 1) * 128], in_=tb[:, j * 128:(j + 1) * 128],
                                                                       identity=ident_f[:]), reads=[tbb, B("ident_f")], writes=[pb2])
                    for j in range(NT):
                        DVE.op(lambda e, j=j, p2=p2, nchunk=nchunk: e.scalar_tensor_tensor(
                            out=xs[:, j, nchunk * 128:(nchunk + 1) * 128], in0=xs[:, j, nchunk * 128:(nchunk + 1) * 128], scalar=ALPHA,
                            in1=p2[:, j * 128:(j + 1) * 128], op0=ALU.mult, op1=ALU.add), reads=[pb2, B(f"xs{j}")], writes=[B(f"xs{j}")])

            def layer_norm(j, gi):
                xb = B(f"xs{j}")
                for s_ in range(2):
                    DVE.op(lambda e, s_=s_: e.bn_stats(out=bnst[:, s_ * 6:(s_ + 1) * 6], in_=xs[:, j, s_ * 512:(s_ + 1) * 512]), reads=[xb], writes=[B("bnst")])
                DVE.op(lambda e: e.bn_aggr(out=stat[:, 13:15], in_=bnst[:]), reads=[B("bnst")], writes=[B("lnstat")])
                ACT.op(lambda e: e.activation(out=stat[:, 15:16], in_=stat[:, 14:15], func=AF.Sqrt, bias=LN_EPS, scale=1.0),
                       reads=[B("lnstat")], writes=[B("lnstat2")])
                DVE.op(lambda e: e.reciprocal(out=stat[:, 15:16], in_=stat[:, 15:16]), reads=[B("lnstat2")], writes=[B("lnstat2")])
                DVE.op(lambda e: e.tensor_scalar(out=xs[:, j, :], in0=xs[:, j, :], scalar1=stat[:, 13:14], scalar2=stat[:, 15:16],
                                                 op0=ALU.subtract, op1=ALU.mult), reads=[xb, B("lnstat"), B("lnstat2")], writes=[xb])
                DVE.op(lambda e: e.tensor_tensor(out=xs[:, j, :], in0=xs[:, j, :], in1=lng[:, gi, :], op=ALU.mult),
                       reads=[xb, B("lng")], writes=[xb])
                DVE.op(lambda e: e.tensor_tensor(out=xs[:, j, :], in0=xs[:, j, :], in1=lng[:, gi + 1, :], op=ALU.add),
                       reads=[xb, B("lng")], writes=[xb])

            if STOP == 'C3':
                return
            cur = [None]

            def get_wout(nchunk):
                if nchunk % 4 == 0:
                    cur[0] = load_w(wsrc(w_out, 0, D, nchunk * 128, 512), KC, 512)
                wt, wb = cur[0]
                c0 = (nchunk % 4) * 128
                return wt, wb, (lambda wt, k: wt[:, k, c0:c0 + 128])
            lin_back(get_wout, KC, lambda k: oT[:, k, 0:ntok], oTb, G1)
            for j in range(NT):
                layer_norm(j, 0)
            for k in range(KC):
                p_, pb_ = next_ps()
                for j in range(NT):
                    PE.op(lambda e, j=j, k=k, p_=p_: e.transpose(out=p_[:, j * 128:(j + 1) * 128], in_=xs[:, j, k * 128:(k + 1) * 128],
                                                                 identity=ident_f[:]), reads=[B(f"xs{j}"), B("ident_f")], writes=[pb_])
                if not SAMPLE:
                    ACT.op(lambda e, k=k, p_=p_: e.activation(out=oT[:, k, 0:ntok], in_=p_[:, 0:ntok], func=AF.Identity,
                                                              bias=mcol(SH2 + k), scale=mcol(SC2 + k)),
                           reads=[pb_, B("modT")], writes=[B(f"oT{k}")])
                else:
                    DVE.op(lambda e, k=k, p_=p_: e.tensor_tensor(out=v3(tmpA[0][:, 0:128]), in0=v3(p_[:, 0:128]), in1=mbc(SC2 + k), op=ALU.mult),
                           reads=[pb_, B("modT")], writes=[B("tmpA0")])
                    DVE.op(lambda e, k=k: e.tensor_tensor(out=v3(oT[:, k, 0:128]), in0=v3(tmpA[0][:, 0:128]), in1=mbc(SH2 + k), op=ALU.add),
                           reads=[B("tmpA0"), B("modT")], writes=[B(f"oT{k}")])
            if STOP == 'C6':
                return
            aT = big[:, :].rearrange("p (f t) -> p f t", t=512)
            for blk in range(11):
                i3, wb = load_block(f"wgu_{blk}", [(lambda i: wring[i][:, :, 0:256], wsrc(w_fg, 0, D, blk * 256, 256)[0]),
                                                    (lambda i: wring[i][:, :, 256:512], wsrc(w_fu, 0, D, blk * 256, 256)[0])], KC * 512)
                wt = wring[i3]
                for s_ in range(2):
                    fc = blk * 2 + s_
                    pg, pgb = next_ps()
                    PE.group(mm(pg[:, 0:ntok], [(wt[:, k, s_ * 128:(s_ + 1) * 128], oT[:, k, 0:ntok]) for k in range(KC)]), reads=oTb + [wb], writes=[pgb])
                    pu, pub = next_ps()
                    PE.group(mm(pu[:, 0:ntok], [(wt[:, k, 256 + s_ * 128:256 + (s_ + 1) * 128], oT[:, k, 0:ntok]) for k in range(KC)]), reads=oTb + [wb], writes=[pub])
                    tb = tmpB[fc % 2]
                    ACT.op(lambda e, pg=pg, tb=tb: e.activation(out=tb[:, 0:ntok], in_=pg[:, 0:ntok], func=AF.Silu), reads=[pgb], writes=[B(f"tmpB{fc % 2}")])
                    DVE.op(lambda e, pu=pu, tb=tb, fc=fc: e.tensor_tensor(out=aT[:, fc, 0:ntok], in0=pu[:, 0:ntok], in1=tb[:, 0:ntok], op=ALU.mult),
                           reads=[pub, B(f"tmpB{fc % 2}")], writes=[B(f"big{fc}")])
            def get_wd(nchunk):
                i3, wb = load_block(f"wd_{nchunk}", [(lambda i: wdring[i], w_fd[:, nchunk * 128:(nchunk + 1) * 128].rearrange("(k p) n -> p k n", p=128))],
                                    FC * 128)
                return wdring[i3], wb, (lambda wt, k: wt[:, k, :])
            lin_back(get_wd, FC, lambda k: aT[:, k, 0:ntok], [B(f"big{q}") for q in range(FC)], G2)
            for j, i in enumerate(tiles):
                layer_norm(j, 2)
                SP.dma(DS(f"x{j}"), y_rows(i), xs[:, j, :], reads=[B(f"xs{j}")])


        def chunk(c, pi):
            NT = 4
            ntok = 512
            tiles = [4 * c + j for j in range(4)]
            tok0 = 512 * c
            R0 = pi * T
            mcol = lambda idx: modT[:, idx, pi:pi + 1]
            if c == 0:
                DVE.op(lambda e: e.memset(lfacc[:], 0.0), writes=[B("lfacc")])
            for j, i in enumerate(tiles):
                SP.dma(DS(f"x{j}"), xs[:, j, :], xp[R0 + i * 128:R0 + (i + 1) * 128, :], writes=[B(f"xs{j}")])
            for k in range(KC):
                p_, pb_ = next_ps()
                for j in range(4):
                    PE.op(lambda e, j=j, k=k, p_=p_: e.transpose(out=p_[:, j * 128:(j + 1) * 128], in_=xs[:, j, k * 128:(k + 1) * 128],
                                                                 identity=ident_f[:]), reads=[B(f"xs{j}"), B("ident_f")], writes=[pb_])
                ACT.op(lambda e, k=k, p_=p_: e.activation(out=hT[:, k, :], in_=p_[:, :], func=AF.Identity,
                                                          bias=mcol(SH1 + k), scale=mcol(SC1 + k)),
                       reads=[pb_, B("modT")], writes=[B(f"hT{k}")])
            hbufs = [B(f"hT{k}") for k in range(KC)]
            if STOP == 'A1':
                return

            def fm_block(wt, wb, col0, dst, dbuf, scale):
                p_, pb_ = next_ps()
                PE.group(mm(p_[:, :], [(wt[:, k, col0:col0 + 128], hT[:, k, :]) for k in range(KC)]), reads=hbufs + [wb], writes=[pb_])
                E = evac_eng()
                E.op(copy_op(E, dst, p_[:, :], scale), reads=[pb_], writes=[dbuf])

            def tm_block(wt, wb, j, ncols=512):
                p_, pb_ = next_ps()
                PE.group(mm(p_[:, 0:ncols], [(hT[:, k, j * 128:(j + 1) * 128], wt[:, k, 0:ncols]) for k in range(KC)]),
                         reads=hbufs + [wb], writes=[pb_])
                return p_, pb_

            wt, wb = load_w(wsrc(w_in, 0, D, C_QA, 512), KC, 512)
            for hp in range(4):
                fm_block(wt, wb, hp * 128, QTf[:, hp, :], B(f"oT{hp}"), 0.125)
            wt, wb = load_w(wsrc(w_in, 0, D, C_KA, 512), KC, 512)
            for hp in range(4):
                fm_block(wt, wb, hp * 128, KTf[:, hp, tok0:tok0 + 512], B(f"KTf{hp}"), None)
            for j, i in enumerate(tiles):
                p_, pb_ = tm_block(wt, wb, j)
                st, sbuf_, sds = next_stage()
                E = evac_eng()
                E.op(copy_op(E, st[:], p_[:, :]), reads=[pb_], writes=[sbuf_])
                SP.dma(sds, kfp[R0 + i * 128:R0 + (i + 1) * 128, :], st[:], reads=[sbuf_])
            if STOP == 'A2':
                return
            wt, wb = load_w(wsrc(w_in, 0, D, C_VA, 512), KC, 512)
            for j, i in enumerate(tiles):
                p_, pb_ = tm_block(wt, wb, j)
                st, sbuf_, sds = next_stage()
                ACT.op(copy_op(ACT, st[:], p_[:, :]), reads=[pb_], writes=[sbuf_])
                DVE.op(lambda e, i=i, st=st: e.tensor_copy(out=Vf[:, i, :, 0:64], in_=st[:, :].rearrange("p (h d) -> p h d", d=64)),
                       reads=[sbuf_], writes=[B("Vf")])
                SP.dma(sds, vfp[R0 + i * 128:R0 + (i + 1) * 128, :], st[:], reads=[sbuf_])
            if STOP == 'A3':
                return
            for j, i in enumerate(tiles):
                p_, pb_ = next_ps()
                PE.group(mm(p_[:, 0:8], [(hT[:, k, j * 128:(j + 1) * 128], wfa[:, k, :]) for k in range(KC)]),
                         reads=hbufs + [B("wfa")], writes=[pb_])
                DVE.op(lambda e, p_=p_: e.tensor_tensor(out=fa_t[:], in0=p_[:, 0:8], in1=bfb, op=ALU.add),
                       reads=[pb_, B("bfb")], writes=[B("fa_t")])
                ACT.op(lambda e: e.activation(out=fa_t[:], in_=fa_t[:], func=AF.Exp, scale=-1.0), reads=[B("fa_t")], writes=[B("fa_t")])
                ACT.op(lambda e: e.activation(out=fa_t[:], in_=fa_t[:], func=AF.Ln, bias=1.0, scale=1.0), reads=[B("fa_t")], writes=[B("fa_t")])
                DVE.op(lambda e, j=j: e.tensor_scalar_mul(out=lf_t[:, j, :], in0=fa_t[:], scalar1=-1.0), reads=[B("fa_t")], writes=[B(f"lf{j}")])
                SP.dma(DS(f"lf{j}"), lfp[R0 + i * 128:R0 + (i + 1) * 128, :], lf_t[:, j, :], reads=[B(f"lf{j}")])
                p2, pb2 = next_ps()
                PE.group(mm(p2[:, 0:8], [(tri_f[:], lf_t[:, j, :]), (ones_f[:], lfacc[:])]),
                         reads=[B(f"lf{j}"), B("lfacc"), B("tri_f"), B("ones_f")], writes=[pb2])
                DVE.op(lambda e, i=i, p2=p2: e.tensor_copy(out=c_all[:, i, :], in_=p2[:, 0:8]), reads=[pb2], writes=[B("c_all")])
                DVE.op(lambda e, j=j: e.tensor_tensor(out=lfacc[:], in0=lfacc[:], in1=lf_t[:, j, :], op=ALU.add),
                       reads=[B("lfacc"), B(f"lf{j}")], writes=[B("lfacc")])
            if STOP == 'A4':
                return
            wt, wb = load_w(wsrc(w_in, 0, D, C_QB, 512), KC, 512)
            for h in range(4):
                fm_block(wt, wb, h * 128, QTd[:, h, :], B(f"oT{4 + h}"), 0.125)
            wt, wb = load_w(wsrc(w_in, 0, D, C_KB, 512), KC, 512)
            for h in range(4):
                fm_block(wt, wb, h * 128, KTd[:, h, tok0:tok0 + 512], B(f"KTd{h}"), None)
            for j, i in enumerate(tiles):
                p_, pb_ = tm_block(wt, wb, j)
                st, sbuf_, sds = next_stage()
                E = evac_eng()
                E.op(copy_op(E, st[:], p_[:, :]), reads=[pb_], writes=[sbuf_])
                SP.dma(sds, kdp[R0 + i * 128:R0 + (i + 1) * 128, :], st[:], reads=[sbuf_])
            wt, wb = load_w(wsrc(w_in, 0, D, C_VB, 512), KC, 512)
            for j, i in enumerate(tiles):
                p_, pb_ = tm_block(wt, wb, j)
                st, sbuf_, sds = next_stage()
                ACT.op(copy_op(ACT, st[:], p_[:, :]), reads=[pb_], writes=[sbuf_])
                DVE.op(lambda e, i=i, st=st: e.tensor_copy(out=Vd[:, i, :, 0:128], in_=st[:, :].rearrange("p (h d) -> p h d", d=128)),
                       reads=[sbuf_], writes=[B("Vd")])
                SP.dma(sds, vdp[R0 + i * 128:R0 + (i + 1) * 128, :], st[:], reads=[sbuf_])

            if STOP == 'A':
                return
            nkb = 4 * c + 4
            p_, pb_ = next_ps()
            PE.op(lambda e, p_=p_: e.matmul(p_[:, 0:8], lhsT=elast_f[:], rhs=c_all[:, 4 * c + 1, :], start=True, stop=True),
                  reads=[B("elast_f"), B("c_all")], writes=[pb_])
            DVE.op(lambda e, p_=p_: e.tensor_copy(out=rc_bc[:], in_=p_[:, 0:8]), reads=[pb_], writes=[B("rc_bc")])
            DVE.op(lambda e: e.tensor_tensor(out=biasF[:, 0:nkb, :], in0=rc_bc[:].unsqueeze(1).to_broadcast([128, nkb, 8]),
                                             in1=c_all[:, 0:nkb, :], op=ALU.subtract),
                   reads=[B("rc_bc"), B("c_all")], writes=[B("biasF")])
            for h in range(4):
                DVE.op(lambda e, h=h: e.tensor_scalar_add(out=biasD[:, 0:nkb, h], in0=posS[:, 0:nkb, h],
                                                          scalar1=-SLOPES[h] * (512 * c + 256)),
                       reads=[B("posS")], writes=[B("biasD")])
            PT = big[:, 0:16 * 512].rearrange("p (k q) -> p k q", q=512)

            def scores(KT, ktb, QT, qtb, kslot, r0, bias_ap, bias_buf):
                for kb in range(nkb):
                    j0 = max(0, kb - 4 * c)
                    nq = (4 - j0) * 128
                    p_, pb_ = next_ps()
                    PE.op(lambda e, p_=p_, kb=kb, j0=j0, nq=nq: e.matmul(
                        p_[:, 0:nq], lhsT=KT[r0:r0 + 64, kslot, kb * 128:(kb + 1) * 128], rhs=QT[r0:r0 + 64, kslot, j0 * 128:512],
                        start=True, stop=True), reads=[ktb, qtb], writes=[pb_])
                    ACT.op(lambda e, p_=p_, kb=kb, j0=j0, nq=nq: e.activation(
                        out=PT[:, kb, j0 * 128:512], in_=p_[:, 0:nq], func=AF.Exp, bias=bias_ap(kb), scale=1.0),
                        reads=[pb_, bias_buf], writes=[B(f"big{kb}")])
                    if kb >= 4 * c:
                        DVE.op(lambda e, kb=kb, j0=j0: e.tensor_tensor(out=PT[:, kb, j0 * 128:(j0 + 1) * 128],
                                                                       in0=PT[:, kb, j0 * 128:(j0 + 1) * 128], in1=tri_b[:], op=ALU.mult),
                               reads=[B(f"big{kb}"), B("tri_b")], writes=[B(f"big{kb}")])

            def pv(j, V, vbuf, h, ncol):
                i = 4 * c + j
                p_, pb_ = next_ps()
                PE.group(mm(p_[:, 0:ncol], [(PT[:, kb, j * 128:(j + 1) * 128], V[:, kb, h, :]) for kb in range(i + 1)]),
                         reads=[B(f"big{kb}") for kb in range(i + 1)] + [vbuf], writes=[pb_])
                return p_, pb_

            for h in range(8):
                hp, r0 = h // 2, (h % 2) * 64
                scores(KTf, B(f"KTf{hp}"), QTf, B(f"oT{hp}"), hp, r0, lambda kb, h=h: biasF[:, kb, h:h + 1], B("biasF"))
                for j in range(4):
                    p_, pb_ = pv(j, Vf, B("Vf"), h, 65)
                    DVE.op(lambda e, p_=p_: e.reciprocal(out=stat[:, 0:1], in_=p_[:, 64:65]), reads=[pb_], writes=[B("stat")])
                    DVE.op(lambda e, p_=p_, j=j, h=h: e.tensor_scalar_mul(out=oAB[:, j, h * 64:(h + 1) * 64], in0=p_[:, 0:64], scalar1=stat[:, 0:1]),
                           reads=[pb_, B("stat")], writes=[B(f"oAB{j}")])
            for h in range(4):
                for m in range(2):
                    scores(KTd, B(f"KTd{h}"), QTd, B(f"oT{4 + h}"), h, m * 64, lambda kb, h=h: biasD[:, kb, h:h + 1], B("biasD"))
                    for j in range(4):
                        p_, pb_ = pv(j, Vd, B("Vd"), h, 129)
                        if m == 0:
                            DVE.op(lambda e, p_=p_, j=j: e.reciprocal(out=stat[:, 4 + j:5 + j], in_=p_[:, 128:129]), reads=[pb_], writes=[B("statd")])
                            DVE.op(lambda e, p_=p_, j=j: e.tensor_scalar_mul(out=d1buf[:, j, :], in0=p_[:, 0:128], scalar1=stat[:, 4 + j:5 + j]),
                                   reads=[pb_, B("statd")], writes=[B(f"d1_{j}")])
                        else:
                            DVE.op(lambda e, p_=p_: e.reciprocal(out=stat[:, 8:9], in_=p_[:, 128:129]), reads=[pb_], writes=[B("stat2")])
                            DVE.op(lambda e: e.tensor_tensor(out=stat[:, 9:10], in0=stat[:, 8:9], in1=lamc[:, 0:1], op=ALU.mult),
                                   reads=[B("stat2"), B("lamc")], writes=[B("stat2")])
                            DVE.op(lambda e, p_=p_, j=j: e.scalar_tensor_tensor(out=dtmp[:, 0, :], in0=p_[:, 0:128], scalar=stat[:, 9:10],
                                                                                 in1=d1buf[:, j, :], op0=ALU.mult, op1=ALU.add),
                                   reads=[pb_, B("stat2"), B(f"d1_{j}")], writes=[B("dtmp0")])
                            ACT.op(lambda e: e.activation(out=dtmp[:, 1, :], in_=dtmp[:, 0, :], func=AF.Square),
                                   reads=[B("dtmp0")], writes=[B("dtmp1")])
                            DVE.op(lambda e: e.reduce_sum(out=stat[:, 10:11], in_=dtmp[:, 1, :], axis=mybir.AxisListType.X),
                                   reads=[B("dtmp1")], writes=[B("stat3")])
                            ACT.op(lambda e: e.activation(out=stat[:, 11:12], in_=stat[:, 10:11], func=AF.Sqrt, bias=LN_EPS, scale=1.0 / 128),
                                   reads=[B("stat3")], writes=[B("stat3")])
                            DVE.op(lambda e: e.reciprocal(out=stat[:, 12:13], in_=stat[:, 11:12]), reads=[B("stat3")], writes=[B("stat3")])
                            DVE.op(lambda e, j=j, h=h: e.scalar_tensor_tensor(out=oAB[:, j, 512 + h * 128:512 + (h + 1) * 128], in0=dtmp[:, 0, :],
                                                                               scalar=stat[:, 12:13], in1=gsub, op0=ALU.mult, op1=ALU.mult),
                                   reads=[B("dtmp0"), B("stat3"), B("gsub")], writes=[B(f"oAB{j}")])

            if STOP == 'B':
                return
            phase_c(4, False, mcol, None, lambda i: yp[R0 + i * 128:R0 + (i + 1) * 128, :], tiles, hbufs)

        idx_all = stage[3][:, :].bitcast(I32)
        SP.dma(DS("pts"), idx_all[:, 0:NS * 256], pt.partition_broadcast(128), writes=[B("stage3")])
        DVE.op(lambda e: e.tensor_copy(out=stage[2][:, 0:NS * 256], in_=idx_all[:, 0:NS * 256]), reads=[B("stage3")], writes=[B("stage2")])
        POOL.op(lambda e: e.iota(small[:, 20:21], pattern=[[0, 1]], base=0, channel_multiplier=1, allow_small_or_imprecise_dtypes=True),
                writes=[B("small20")])
        DVE.op(lambda e: e.tensor_scalar(out=stage[2][:, 0:NS * 256], in0=stage[2][:, 0:NS * 256], scalar1=128.0, scalar2=small[:, 20:21],
                                         op0=ALU.mult, op1=ALU.add), reads=[B("stage2"), B("small20")], writes=[B("stage2")])
        DVE.op(lambda e: e.tensor_copy(out=idx_all[:, 0:NS * 256], in_=stage[2][:, 0:NS * 256]), reads=[B("stage2")], writes=[B("stage3")])
        ptT = cst[:, 6, :].bitcast(I32)
        with nc.allow_non_contiguous_dma(reason="tiny page-table transpose"):
            SP.dma(DS("ptT"), ptT[:, 0:NS * 2], pt.rearrange("o (c p) -> p (o c)", p=128), writes=[B("ptT")])

        def gather(dsem, out, src2d, idx_col):
            ins = nc.gpsimd.indirect_dma_start(out=out, out_offset=None, in_=src2d,
                                               in_offset=bass.IndirectOffsetOnAxis(ap=idx_col, axis=0))
            dsem.count += 16
            ins.then_inc(dsem.sem, 16)
        maskS_f = sb("maskS_f", [128, 128]); maskS_b = sb("maskS_b", [128, 128], BF16); MB_f = sb("MB_f", [128, 128])
        alibS = sb("alibS", [128, 16, 8]); alibN = sb("alibN", [128, 16]); negcs = sb("negcs", [128, 8])
        m3 = maskS_f[:, :].rearrange("p (b q) -> p b q", q=8)
        POOL.op(lambda e: e.affine_select(out=m3, in_=ones_f[:, :].rearrange("p (b q) -> p b q", q=8), pattern=[[-8, 16], [0, 8]],
                                          compare_op=ALU.is_ge, fill=0.0, base=0, channel_multiplier=1), reads=[B("ones_f")], writes=[B("maskS")])
        POOL.op(lambda e: e.affine_select(out=m3, in_=m3, pattern=[[8, 16], [1, 8]], compare_op=ALU.is_ge, fill=0.0, base=0,
                                          channel_multiplier=-1), reads=[B("maskS")], writes=[B("maskS")])
        DVE.op(lambda e: e.tensor_copy(out=maskS_b[:], in_=maskS_f[:]), reads=[B("maskS")], writes=[B("maskSb")])
        mb3 = MB_f[:, :].rearrange("p (b w) -> p b w", w=16)
        POOL.op(lambda e: e.affine_select(out=mb3, in_=ones_f[:, :].rearrange("p (b w) -> p b w", w=16), pattern=[[-16, 8], [-1, 16]],
                                          compare_op=ALU.is_gt, fill=0.0, base=0, channel_multiplier=1), reads=[B("ones_f")], writes=[B("MB")])
        POOL.op(lambda e: e.affine_select(out=mb3, in_=mb3, pattern=[[16, 8], [0, 16]], compare_op=ALU.is_ge, fill=0.0, base=15,
                                          channel_multiplier=-1), reads=[B("MB")], writes=[B("MB")])
        for hm in range(8):
            DVE.op(lambda e, hm=hm: e.tensor_scalar_add(out=alibS[:, :, hm], in0=posS[:, :, hm // 2], scalar1=-2048.0 * SLOPES[hm // 2]),
                   reads=[B("posS")], writes=[B("alibS")])
        DVE.op(lambda e: e.reduce_sum(out=alibN[:, 8:9], in_=maskS_f[:, :], axis=mybir.AxisListType.X), reads=[B("maskS")], writes=[B("alibN")])
        DVE.op(lambda e: e.tensor_scalar(out=alibN[:, 9:10], in0=alibN[:, 8:9], scalar1=-1.0, scalar2=8.0, op0=ALU.mult, op1=ALU.add),
               reads=[B("alibN")], writes=[B("alibN")])
        for hm in range(8):
            DVE.op(lambda e, hm=hm: e.tensor_scalar_mul(out=alibN[:, hm:hm + 1], in0=alibN[:, 9:10], scalar1=SLOPES[hm // 2]),
                   reads=[B("alibN")], writes=[B("alibN")])
        QBD = oAB[:, 1:3, :].rearrange("p a (k s c) -> p (a k) s c", s=16, c=16)
        biasAll = xs[:, 1:3, :].rearrange("p a (h b g) -> p (a h) b g", b=16, g=16)
        Lst = xs[:, 3, :]
        Pf = big[:, 0:2048].bitcast(F32).rearrange("p (r h) -> p h r", h=8)
        KTs = big[:, 0:4096].rearrange("p (k n) -> p k n", n=512)
        PTs = big[:, 4096:4096 + 17 * 128].rearrange("p (k n) -> p k n", n=128)
        PTb = [B(f"big{q}") for q in range(8, 13)]
        KTsb = [B(f"big{q}") for q in range(8)]
        osb = tmpB[0][:, :].bitcast(BF16)
        v3 = lambda ap: ap.rearrange("p (s q) -> p s q", q=8)
        clf3 = clf.rearrange("n (a c) -> n a c", a=1)

        def sample_tile(ts):
            r0 = NP + 16 * ts
            hbufs = [B(f"hT{k}") for k in range(KC)]
            mbc = lambda idx: modT[:, idx, r0:r0 + 16].unsqueeze(2).to_broadcast([128, 16, 8])
            rows = slice(ts * 128, (ts + 1) * 128)
            SP.dma(DS("x0"), xs[:, 0, :], xs_in[rows, :], writes=[B("xs0")])
            for k in range(KC):
                p_, pb_ = next_ps()
                PE.op(lambda e, k=k, p_=p_: e.transpose(out=p_[:, 0:128], in_=xs[:, 0, k * 128:(k + 1) * 128], identity=ident_f[:]),
                      reads=[B("xs0"), B("ident_f")], writes=[pb_])
                DVE.op(lambda e, k=k, p_=p_: e.tensor_tensor(out=v3(tmpA[0][:, 0:128]), in0=v3(p_[:, 0:128]), in1=mbc(SC1 + k), op=ALU.mult),
                       reads=[pb_, B("modT")], writes=[B("tmpA0")])
                DVE.op(lambda e, k=k: e.tensor_tensor(out=v3(hT[:, k, 0:128]), in0=v3(tmpA[0][:, 0:128]), in1=mbc(SH1 + k), op=ALU.add),
                       reads=[B("tmpA0"), B("modT")], writes=[B(f"hT{k}")])
            if ts == 0:
                DVE.op(lambda e: e.memset(oAB[:, 1:3, :], 0.0), writes=[B("oAB1"), B("oAB2")])
            qb = [B("oAB1"), B("oAB2")]

            def fm(wt, wb, col0):
                p_, pb_ = next_ps()
                PE.group(mm(p_[:, 0:128], [(wt[:, k, col0:col0 + 128], hT[:, k, 0:128]) for k in range(KC)]), reads=hbufs + [wb], writes=[pb_])
                return p_, pb_

            def tmj(wt, wb, ncols=512):
                p_, pb_ = next_ps()
                PE.group(mm(p_[:, 0:ncols], [(hT[:, k, 0:128], wt[:, k, 0:ncols]) for k in range(KC)]), reads=hbufs + [wb], writes=[pb_])
                return p_, pb_

            def q_block(col, blk0):
                wt, wb = load_w(wsrc(w_in, 0, D, col, 512), KC, 512)
                for q in range(4):
                    p_, pb_ = fm(wt, wb, q * 128)
                    DVE.op(lambda e, p_=p_, q=q: e.tensor_scalar_mul(out=QBD[0:64, blk0 + q, :, 0:8], in0=v3(p_[0:64, 0:128]), scalar1=0.125),
                           reads=[pb_], writes=qb)
                    DVE.op(lambda e, p_=p_, q=q: e.tensor_scalar_mul(out=QBD[64:128, blk0 + q, :, 8:16], in0=v3(p_[64:128, 0:128]), scalar1=0.125),
                           reads=[pb_], writes=qb)

            def k_block(col, KT, nm, oap):
                wt, wb = load_w(wsrc(w_in, 0, D, col, 512), KC, 512)
                for q in range(4):
                    p_, pb_ = fm(wt, wb, q * 128)
                    E = evac_eng()
                    E.op(copy_op(E, KT[:, q, T:T + 128], p_[:, 0:128]), reads=[pb_], writes=[B(f"{nm}{q}")])
                p_, pb_ = tmj(wt, wb)
                st, sbuf_, sds = next_stage()
                E = evac_eng()
                E.op(copy_op(E, st[:], p_[:, :]), reads=[pb_], writes=[sbuf_])
                SP.dma(sds, oap[rows, :], st[:], reads=[sbuf_])

            def v_block(col, V, vb, hd, oap):
                wt, wb = load_w(wsrc(w_in, 0, D, col, 512), KC, 512)
                p_, pb_ = tmj(wt, wb)
                st, sbuf_, sds = next_stage()
                ACT.op(copy_op(ACT, st[:], p_[:, :]), reads=[pb_], writes=[sbuf_])
                DVE.op(lambda e, st=st: e.tensor_copy(out=V[:, 16, :, 0:hd], in_=st[:, :].rearrange("p (h d) -> p h d", d=hd)),
                       reads=[sbuf_], writes=[vb])
                SP.dma(sds, oap[rows, :], st[:], reads=[sbuf_])

            q_block(C_QA, 0)
            k_block(C_KA, KTf, "KTf", kfs)
            v_block(C_VA, Vf, B("Vf"), 64, vfs)
            p_, pb_ = next_ps()
            PE.group(mm(p_[:, 0:8], [(hT[:, k, 0:128], wfa[:, k, :]) for k in range(KC)]), reads=hbufs + [B("wfa")], writes=[pb_])
            DVE.op(lambda e: e.tensor_tensor(out=fa_t[:], in0=p_[:, 0:8], in1=bfb, op=ALU.add), reads=[pb_, B("bfb")], writes=[B("fa_t")])
            ACT.op(lambda e: e.activation(out=fa_t[:], in_=fa_t[:], func=AF.Exp, scale=-1.0), reads=[B("fa_t")], writes=[B("fa_t")])
            ACT.op(lambda e: e.activation(out=fa_t[:], in_=fa_t[:], func=AF.Ln, bias=1.0, scale=1.0), reads=[B("fa_t")], writes=[B("fa_t")])
            DVE.op(lambda e: e.tensor_scalar_mul(out=lf_t[:, 0, :], in0=fa_t[:], scalar1=-1.0), reads=[B("fa_t")], writes=[B("lf0")])
            SP.dma(DS("lf0"), lfs[rows, :], lf_t[:, 0, :], reads=[B("lf0")])
            p2, pb2 = next_ps()
            PE.op(lambda e: e.matmul(p2[:, 0:8], lhsT=maskS_f[:], rhs=lf_t[:, 0, :], start=True, stop=True),
                  reads=[B("maskS"), B("lf0")], writes=[pb2])
            DVE.op(lambda e: e.tensor_scalar_mul(out=negcs[:], in0=p2[:, 0:8], scalar1=-1.0), reads=[pb2], writes=[B("negcs")])
            q_block(C_QB, 4)
            k_block(C_KB, KTd, "KTd", kds)
            v_block(C_VB, Vd, B("Vd"), 128, vds)
            if STOP == 'SA':
                return

            for hf in range(2):
                dsl = DS("lg")
                POOL._wait(Eng._deps([B("ptT")], [B("xs3")]))
                gather(dsl, Lst[:, :], clf, ptT[:, ts * 2 + hf:ts * 2 + hf + 1])
                B("xs3").w = (dsl.sem, dsl.count)
                B("xs3").r = {}
                L3 = Lst.rearrange("p (r h) -> p h r", h=8)
                for h in range(8):
                    DVE.op(lambda e, h=h: e.tensor_tensor_scan(out=Pf[:, h, :], data0=ones_f[:, :], data1=L3[:, h, :], initial=0.0,
                                                               op0=ALU.mult, op1=ALU.add), reads=[B("xs3"), B("ones_f")], writes=[B(f"big{h // 2}")])
                pfb = [B(f"big{q}") for q in range(4)]
                pE, pEb = next_ps()
                PE.op(lambda e, pE=pE: e.matmul(pE[:, 0:8], lhsT=MB_f[:], rhs=Pf[:, :, 127], start=True, stop=True),
                      reads=pfb + [B("MB")], writes=[pEb])
                DVE.op(lambda e, pE=pE: e.tensor_tensor(out=small[:, 8:16], in0=pE[:, 0:8], in1=Pf[:, :, 127], op=ALU.add),
                       reads=pfb + [pEb], writes=[B("smallTE")])
                DVE.op(lambda e: e.tensor_tensor(out=Pf, in0=small[:, 8:16].unsqueeze(2).to_broadcast([128, 8, 128]), in1=Pf, op=ALU.subtract),
                       reads=pfb + [B("smallTE")], writes=pfb)
                for hh in range(2):
                    p_, pb_ = next_ps()
                    for q in range(4):
                        h = hh * 4 + q
                        PE.op(lambda e, p_=p_, q=q, h=h: e.transpose(out=p_[:, q * 128:(q + 1) * 128], in_=Pf[:, h, :], identity=ident_f[:]),
                              reads=pfb + [B("ident_f")], writes=[pb_])
                    E = evac_eng()
                    E.op(copy_op(E, biasAll[:, hh * 4:hh * 4 + 4, hf * 8:hf * 8 + 8, :],
                                 p_[:, :].rearrange("p (q b g) -> p q b g", b=8, g=16)), reads=[pb_], writes=[B("xs1"), B("xs2")])
            if STOP == 'SB':
                return

            dso = DS("osc")
            for bl in range(16):
                for pg in range(4):
                    iA = wr_i[0] % 3; wr_i[0] += 1
                    iB = wr_i[0] % 3; wr_i[0] += 1
                    sl = [wflat[iA], wflat[iB]]
                    slb = [B(f"wring{iA}"), B(f"wring{iB}")]
                    sld = [DS(f"wring{iA}"), DS(f"wring{iB}")]
                    POOL._wait(Eng._deps([B("stage3")], slb))
                    for j in range(4):
                        col = ts * 256 + bl * 16 + pg * 4 + j
                        gather(sld[j // 2], sl[j // 2][:, (j % 2) * 2048:(j % 2 + 1) * 2048], cpool, idx_all[:, col:col + 1])
                    for q in range(2):
                        slb[q].w = (sld[q].sem, sld[q].count); slb[q].r = {}
                    pgv = lambda j, t: sl[j // 2][:, (j % 2) * 2048 + t * 512:(j % 2) * 2048 + (t + 1) * 512]
                    for j in range(4):
                        g = pg * 4 + j
                        DVE.op(lambda e, j=j, g=g: e.tensor_copy(out=Vf[:, g, :, 0:64], in_=pgv(j, 1).rearrange("p (h d) -> p h d", d=64)),
                               reads=[slb[j // 2]], writes=[B("Vf")])
                        DVE.op(lambda e, j=j, g=g: e.tensor_copy(out=Vd[:, g, :, 0:128], in_=pgv(j, 3).rearrange("p (h d) -> p h d", d=128)),
                               reads=[slb[j // 2]], writes=[B("Vd")])
                    for j in range(4):
                        for hh in range(2):
                            p_, pb_ = next_ps()
                            PE.group([(lambda e, p_=p_, q=q, j=j, hh=hh: e.matmul(p_[:, q * 128:(q + 1) * 128], lhsT=pgv(j, 2 * hh)[:, q * 128:(q + 1) * 128],
                                                                                      rhs=ident_b[:], start=True, stop=True)) for q in range(4)],
                                     reads=[slb[j // 2], B("ident_b")], writes=[pb_])
                            E = evac_eng()
                            E.op(copy_op(E, KTs[:, 4 * hh:4 * hh + 4, j * 128:(j + 1) * 128], p_[:, :].rearrange("p (q n) -> p q n", n=128)),
                                 reads=[pb_], writes=KTsb[4 * hh:4 * hh + 4])
                    pS, pSb = next_ps()
                    PE.group([(lambda e, pS=pS, j=j, blk=blk: e.matmul(pS[:, j * 128 + blk * 16:j * 128 + blk * 16 + 16], lhsT=KTs[:, blk, j * 128:(j + 1) * 128],
                                                                        rhs=QBD[:, blk, bl, :], start=True, stop=True)) for j in range(4) for blk in range(8)],
                             reads=KTsb + qb, writes=[pSb])
                    tv = tmpA[0][:, :].rearrange("p (j c) -> p j c", c=128)
                    sv = pS[:, :].rearrange("p (j c) -> p j c", c=128)
                    hq = lambda ap: ap.rearrange("p j (h q) -> p j h q", q=8)
                    DVE.op(lambda e, pg=pg: e.tensor_tensor(out=hq(tv[:, :, 0:64]), in0=hq(sv[:, :, 0:64]),
                                                            in1=biasAll[:, :, bl, pg * 4:pg * 4 + 4].rearrange("p h g -> p g h").unsqueeze(3).to_broadcast([128, 4, 8, 8]),
                                                            op=ALU.add), reads=[pSb, B("xs1"), B("xs2")], writes=[B("tmpA0")])
                    DVE.op(lambda e, pg=pg: e.tensor_tensor(out=hq(tv[:, :, 64:128]), in0=hq(sv[:, :, 64:128]),
                                                            in1=alibS[:, pg * 4:pg * 4 + 4, :].unsqueeze(3).to_broadcast([128, 4, 8, 8]),
                                                            op=ALU.add), reads=[pSb, B("alibS")], writes=[B("tmpA0")])
                    ACT.op(lambda e, pg=pg: e.activation(out=PTs[:, pg * 4:pg * 4 + 4, :], in_=tv, func=AF.Exp), reads=[B("tmpA0")], writes=PTb)
                pS, pSb = next_ps()
                PE.group([(lambda e, pS=pS, blk=blk: e.matmul(pS[:, blk * 16:blk * 16 + 16], lhsT=(KTf if blk < 4 else KTd)[:, blk % 4, T:T + 128],
                                                               rhs=QBD[:, blk, bl, :], start=True, stop=True)) for blk in range(8)],
                         reads=[B(f"KTf{q}") for q in range(4)] + [B(f"KTd{q}") for q in range(4)] + qb, writes=[pSb])
                t1 = tmpA[1][:, 0:128]
                h2 = lambda ap: ap.rearrange("p (h q) -> p h q", q=8)
                DVE.op(lambda e: e.tensor_tensor(out=h2(t1[:, 0:64]), in0=h2(pS[:, 0:64]), in1=negcs[:, :].unsqueeze(2).to_broadcast([128, 8, 8]), op=ALU.add),
                       reads=[pSb, B("negcs")], writes=[B("tmpA1")])
                DVE.op(lambda e: e.tensor_tensor(out=h2(t1[:, 64:128]), in0=h2(pS[:, 64:128]), in1=alibN[:, 0:8].unsqueeze(2).to_broadcast([128, 8, 8]), op=ALU.add),
                       reads=[pSb, B("alibN")], writes=[B("tmpA1")])
                ACT.op(lambda e: e.activation(out=tmpA[1][:, 128:256], in_=t1, func=AF.Exp), reads=[B("tmpA1")], writes=[B("tmpA1")])
                DVE.op(lambda e: e.tensor_tensor(out=h2(PTs[:, 16, :]), in0=h2(tmpA[1][:, 128:256]),
                                                 in1=maskS_f[:, bl * 8:bl * 8 + 8].unsqueeze(1).to_broadcast([128, 16, 8]), op=ALU.mult),
                       reads=[B("tmpA1"), B("maskS")], writes=PTb)
                banks = [(list(range(0, 4)), 65), (list(range(4, 8)), 65), ([8, 9, 10], 129), ([11, 12, 13], 129), ([14, 15], 129)]
                res = {}
                for hms, ncol in banks:
                    p_, pb_ = next_ps()
                    fns = []
                    for n_, hm in enumerate(hms):
                        h = hm if hm < 8 else (hm - 8) // 2
                        V = Vf if hm < 8 else Vd
                        fns += mm(p_[0:8, n_ * ncol:(n_ + 1) * ncol], [(PTs[:, kb, hm * 8:(hm + 1) * 8], V[:, kb, h, :]) for kb in range(17)])
                        res[hm] = (p_, pb_, n_ * ncol)
                    PE.group(fns, reads=PTb + [B("Vf"), B("Vd")], writes=[pb_])
                for hm in range(16):
                    p_, pb_, off = res[hm]
                    if hm < 8:
                        DVE.op(lambda e, p_=p_, off=off: e.reciprocal(out=stat[0:8, 0:1], in_=p_[0:8, off + 64:off + 65]), reads=[pb_], writes=[B("stat")])
                        DVE.op(lambda e, p_=p_, off=off, hm=hm: e.tensor_scalar_mul(out=osb[0:8, hm * 64:(hm + 1) * 64], in0=p_[0:8, off:off + 64],
                                                                                     scalar1=stat[0:8, 0:1]), reads=[pb_, B("stat")], writes=[B("tmpB0")])
                    else:
                        h, m = (hm - 8) // 2, (hm - 8) % 2
                        DVE.op(lambda e, p_=p_, off=off: e.reciprocal(out=stat[0:8, 8:9], in_=p_[0:8, off + 128:off + 129]), reads=[pb_], writes=[B("stat2")])
                        if m == 0:
                            DVE.op(lambda e, p_=p_, off=off, h=h: e.tensor_scalar_mul(out=d1buf[0:8, h, :], in0=p_[0:8, off:off + 128], scalar1=stat[0:8, 8:9]),
                                   reads=[pb_, B("stat2")], writes=[B("d1_0")])
                        else:
                            DVE.op(lambda e: e.tensor_tensor(out=stat[0:8, 9:10], in0=stat[0:8, 8:9], in1=lamc[0:8, 0:1], op=ALU.mult),
                                   reads=[B("stat2"), B("lamc")], writes=[B("stat2")])
                            DVE.op(lambda e, p_=p_, off=off, h=h: e.scalar_tensor_tensor(out=d1buf[0:8, h, :], in0=p_[0:8, off:off + 128], scalar=stat[0:8, 9:10],
                                                                                          in1=d1buf[0:8, h, :], op0=ALU.mult, op1=ALU.add),
                                   reads=[pb_, B("stat2"), B("d1_0")], writes=[B("d1_0")])
                sq = tmpA[1][0:8, 0:512].rearrange("p (h e) -> p h e", e=128)
                ACT.op(lambda e: e.activation(out=sq, in_=d1buf[0:8, :, :], func=AF.Square), reads=[B("d1_0")], writes=[B("tmpA1")])
                DVE.op(lambda e: e.reduce_sum(out=stat[0:8, 10:14], in_=sq, axis=mybir.AxisListType.X), reads=[B("tmpA1")], writes=[B("stat3")])
                ACT.op(lambda e: e.activation(out=stat[0:8, 10:14], in_=stat[0:8, 10:14], func=AF.Sqrt, bias=LN_EPS, scale=1.0 / 128),
                       reads=[B("stat3")], writes=[B("stat3")])
                DVE.op(lambda e: e.reciprocal(out=stat[0:8, 10:14], in_=stat[0:8, 10:14]), reads=[B("stat3")], writes=[B("stat3")])
                DVE.op(lambda e: e.tensor_tensor(out=d1buf[0:8, :, :], in0=d1buf[0:8, :, :], in1=stat[0:8, 10:14].unsqueeze(2).to_broadcast([8, 4, 128]), op=ALU.mult),
                       reads=[B("d1_0"), B("stat3")], writes=[B("d1_0")])
                DVE.op(lambda e: e.tensor_tensor(out=osb[0:8, 512:1024].rearrange("p (h e) -> p h e", e=128), in0=d1buf[0:8, :, :],
                                                 in1=gsub[0:8, :].unsqueeze(1).to_broadcast([8, 4, 128]), op=ALU.mult),
                       reads=[B("d1_0"), B("gsub")], writes=[B("tmpB0")])
                SP.dma(dso, osc[ts * 128 + bl * 8:ts * 128 + bl * 8 + 8, :], osb[0:8, :], reads=[B("tmpB0")], writes=[B("osc")])
            B("osc").w = (dso.sem, dso.count)
            SP.dma(dso, oAB[:, 0, :], osc[rows, :], reads=[B("osc")], writes=[B("oAB0")])
            if STOP == 'SC':
                return
            phase_c(1, True, None, mbc, lambda i: ys[rows, :], [0], hbufs)

        nch = int(os.environ.get("KDEV_NCH", NCH))
        if STOP == 'setup':
            nch = 0
        for pi in range(NP):
            for c in range(nch):
                chunk(c, pi)
        if STOP != 'setup' and os.environ.get('KDEV_NOSAMPLE') is None:
            for ts in range(NS):
                sample_tile(ts)

        SP._wait({d.sem: d.count for d in _ds.values() if d.count})
        SP._wait({E.sem: E.count for E in (PE, ACT, DVE, POOL)})
    return nc


_INPUT_KEYS = None


def kernel(**inputs):
    ncores = int(os.environ.get("KDEV_CORES", 8))
    NP = int(os.environ.get("KDEV_NP", 8 // ncores))
    NS = int(os.environ.get("KDEV_NS", 8 // ncores))
    n_phys = int(inputs["cache_k_fox"].shape[1])
    nc = build_program(NP=NP, NS=NS, n_phys=n_phys)
    f = lambda a: np.ascontiguousarray(np.asarray(a))
    rr = lambda k: np.asarray(inputs[k]).reshape(n_phys * 128, 512)
    pools = {
        "cpool": np.concatenate([rr("cache_k_fox"), rr("cache_v_fox"), rr("cache_k_diff"), rr("cache_v_diff")], axis=1),
        "clf": f(inputs["cache_logf_fox"]).reshape(n_phys, 1024),
    }
    shared = {
        "w_ada": f(inputs["w_ada"][0]), "b_ada": f(inputs["b_ada"]), "w_in": f(inputs["w_in"][0]),
        "b_forget": f(inputs["b_forget"]), "lq1": f(inputs["lambda_q1"]), "lk1": f(inputs["lambda_k1"]),
        "lq2": f(inputs["lambda_q2"]), "lk2": f(inputs["lambda_k2"]), "subln": f(inputs["subln_gain"]),
        "w_ba": f(inputs["w_branch_a"][0]), "w_bb": f(inputs["w_branch_b"][0]), "w_out": f(inputs["w_out"][0]),
        "ln1g": f(inputs["ln1_gain"]), "ln1b": f(inputs["ln1_bias"]),
        "w_fg": f(inputs["w_ffn_gate"][0]), "w_fu": f(inputs["w_ffn_up"][0]), "w_fd": f(inputs["w_ffn_down"][0]),
        "ln2g": f(inputs["ln2_gain"]), "ln2b": f(inputs["ln2_bias"]),
    }
    in_maps = []
    for b in range(ncores):
        sq = slice(16 * NS * b, 16 * NS * (b + 1))
        m = {
            "xp": f(inputs["x_prompt"][NP * b:NP * (b + 1)]).reshape(NP * T, D),
            "xs": f(inputs["x_sample"][sq]).reshape(NS * 128, D),
            "cp": f(inputs["c_prompt"][NP * b:NP * (b + 1)]), "cs": f(inputs["c_sample"][sq]),
            "pt": f(inputs["page_table"][sq]).reshape(1, NS * 256).astype(np.int32),
        }
        m.update(pools)
        m.update(shared)
        in_maps.append(m)
    res = run_bass_kernel_spmd(nc, in_maps, core_ids=list(range(ncores)))
    R = res.results
    nb = len(R)

    def cat(name, shape):
        return np.stack([R[b][name] for b in range(nb)], 0).reshape(shape)
    npq, nsq = nb * NP, nb * NS * 16
    outs = (cat("yp", (npq, T, D)), cat("ys", (nsq, 8, D)),
            cat("kfp", (1, npq, T, 8, 64)), cat("vfp", (1, npq, T, 8, 64)), cat("lfp", (1, npq, T, 8)),
            cat("kdp", (1, npq, T, 4, 2, 64)), cat("vdp", (1, npq, T, 4, 128)),
            cat("kfs", (1, nsq, 8, 8, 64)), cat("vfs", (1, nsq, 8, 8, 64)), cat("lfs", (1, nsq, 8, 8)),
            cat("kds", (1, nsq, 8, 4, 2, 64)), cat("vds", (1, nsq, 8, 4, 128)))
    return tuple(np.ascontiguousarray(o.astype(np.float32)) for o in outs)
```

```python
import os
from contextlib import ExitStack
import numpy as np
import concourse.bass as bass
import concourse.mybir as mybir
from concourse.bass_utils import run_bass_kernel_spmd

F32 = mybir.dt.float32
BF16 = mybir.dt.bfloat16
I32 = mybir.dt.int32
AF = mybir.ActivationFunctionType
ALU = mybir.AluOpType

D = 1024
KC = 8
T = 2048
NCH = 4
DFF = 2816
FC = 22
DIN = 5128
ALPHA = 2.0 ** 0.25
LN_EPS = 1e-5
LAMBDA_INIT = 0.2
SLOPES = [2.0 ** (-8.0 * (h + 1) / 4) for h in range(4)]
C_QA, C_KA, C_VA, C_FA, C_QB, C_KB, C_VB, C_GA, C_GB = 0, 512, 1024, 1536, 1544, 2056, 2568, 3080, 4104


class Buf:
    __slots__ = ("w", "r")

    def __init__(self):
        self.w = None
        self.r = {}


class DSem:
    def __init__(self, nc, name):
        self.sem = nc.alloc_semaphore(name)
        self.count = 0


class Eng:
    def __init__(self, nc, name, eng):
        self.eng = eng
        self.sem = nc.alloc_semaphore("sem_" + name)
        self.count = 0
        self.waited = {}

    def _wait(self, deps):
        for sem, val in deps.items():
            if self.waited.get(sem, 0) < val:
                self.eng.wait_ge(sem, val)
                self.waited[sem] = val

    @staticmethod
    def _deps(reads, writes):
        deps = {}
        for b in reads:
            if b.w is not None and deps.get(b.w[0], 0) < b.w[1]:
                deps[b.w[0]] = b.w[1]
        for b in writes:
            if b.w is not None and deps.get(b.w[0], 0) < b.w[1]:
                deps[b.w[0]] = b.w[1]
            for s, v in b.r.items():
                if deps.get(s, 0) < v:
                    deps[s] = v
        return deps

    @staticmethod
    def _mark(tok, reads, writes):
        s, v = tok
        for b in reads:
            if b.r.get(s, 0) < v:
                b.r[s] = v
        for b in writes:
            b.w = tok
            b.r = {}

    def op(self, fn, reads=(), writes=()):
        return self.group([fn], reads, writes)

    def group(self, fns, reads=(), writes=()):
        self._wait(self._deps(reads, writes))
        ins = None
        for fn in fns:
            ins = fn(self.eng)
        self.count += 1
        ins.then_inc(self.sem, 1)
        tok = (self.sem, self.count)
        self._mark(tok, reads, writes)
        return tok

    def dma(self, dsem, out, in_, reads=(), writes=()):
        self._wait(self._deps(reads, writes))
        ins = self.eng.dma_start(out=out, in_=in_)
        dsem.count += 16
        ins.then_inc(dsem.sem, 16)
        tok = (dsem.sem, dsem.count)
        self._mark(tok, reads, writes)
        return tok


def build_program(NP=2, NS=2, n_phys=2560, do_sample=True):
    WCACHE = os.environ.get('KDEV_NOWCACHE') is None
    NR = NP + 16 * NS
    STOP = os.environ.get('KDEV_STOP', '')
    nc = bass.Bass("TRN2", target_bir_lowering=False)

    def din(name, shape, dt=F32):
        return nc.dram_tensor(name, list(shape), dt, kind="ExternalInput").ap()

    def dout(name, shape, dt=F32):
        return nc.dram_tensor(name, list(shape), dt, kind="ExternalOutput").ap()

    xp = din("xp", [NP * T, D]); xs_in = din("xs", [NS * 128, D])
    cp = din("cp", [NP, D]); cs = din("cs", [NS * 16, D])
    pt = din("pt", [1, NS * 256], I32)
    cpool = din("cpool", [n_phys * 128, 2048])
    clf = din("clf", [n_phys, 1024])
    w_ada = din("w_ada", [D, 6 * D]); b_ada = din("b_ada", [1, 6 * D])
    w_in = din("w_in", [D, DIN]); b_forget = din("b_forget", [1, 8])
    lq1 = din("lq1", [1, 64]); lk1 = din("lk1", [1, 64]); lq2 = din("lq2", [1, 64]); lk2 = din("lk2", [1, 64])
    subln = din("subln", [1, 128])
    w_ba = din("w_ba", [512, D]); w_bb = din("w_bb", [512, D]); w_out = din("w_out", [D, D])
    ln1g = din("ln1g", [1, D]); ln1b = din("ln1b", [1, D])
    w_fg = din("w_fg", [D, DFF]); w_fu = din("w_fu", [D, DFF]); w_fd = din("w_fd", [DFF, D])
    ln2g = din("ln2g", [1, D]); ln2b = din("ln2b", [1, D])
    yp = dout("yp", [NP * T, D]); ys = dout("ys", [NS * 128, D])
    kfp = dout("kfp", [NP * T, 512]); vfp = dout("vfp", [NP * T, 512]); lfp = dout("lfp", [NP * T, 8])
    kdp = dout("kdp", [NP * T, 512]); vdp = dout("vdp", [NP * T, 512])
    kfs = dout("kfs", [NS * 128, 512]); vfs = dout("vfs", [NS * 128, 512]); lfs = dout("lfs", [NS * 128, 8])
    kds = dout("kds", [NS * 128, 512]); vds = dout("vds", [NS * 128, 512])
    osc = nc.dram_tensor("osc", [NS * 128, 1024], BF16, kind="Internal").ap()

    es = ExitStack()
    with es:
        def sb(name, shape, dt=F32):
            return es.enter_context(nc.sbuf_tensor(name, list(shape), dt))

        cst = sb("cst", [128, 8, 128])
        bfb = cst[:, 0, 0:8]; gsub = cst[:, 1, :]
        wfa_t = sb("wfa", [128, 256], BF16)
        wfa = wfa_t[:, 0:64].rearrange("p (k n) -> p k n", n=8)
        lng = sb("lng", [128, 4, D])
        xs = sb("xs_sb", [128, 4, D])
        wflat = [sb(f"wring{i}", [128, KC * 512], BF16) for i in range(3)]
        stage = [sb(f"stage{i}", [128, 512]) for i in range(4)]
        ident_f = sb("ident_f", [128, 128]); ident_b = sb("ident_b", [128, 128], BF16)
        tri_f = sb("tri_f", [128, 128]); tri_b = sb("tri_b", [128, 128], BF16)
        elast_f = sb("elast_f", [128, 128]); ones_f = sb("ones_f", [128, 128])
        ones_b = sb("ones_b", [128, 128], BF16)
        posS = sb("posS", [128, 16, 4])
        modT = sb("modT", [128, 48, NR])
        lamc = sb("lamc", [128, 4])
        small = sb("small", [128, 64])
        cT = sb("cT", [128, KC, NR], BF16)
        KTf = sb("KTf", [128, 4, T + 128], BF16); KTd = sb("KTd", [128, 4, T + 128], BF16)
        Vf = sb("Vf", [128, 17, 8, 65], BF16); Vd = sb("Vd", [128, 17, 4, 129], BF16)
        c_all = sb("c_all", [128, 17, 8]); lfacc = sb("lfacc", [128, 8]); lf_t = sb("lf_t", [128, 4, 8])
        fa_t = sb("fa_t", [128, 8]); rc_bc = sb("rc_bc", [128, 8])
        biasF = sb("biasF", [128, 16, 8]); biasD = sb("biasD", [128, 16, 4])
        hT = sb("hT", [128, KC, 512], BF16)
        big = sb("big", [128, FC * 512], BF16)
        oAB = sb("oAB", [128, 4, D], BF16)
        oT = sb("oT", [128, KC, 512], BF16)
        QTf = oT[:, 0:4, :]; QTd = oT[:, 4:8, :]
        tmpA = [sb(f"tmpA{i}", [128, 512]) for i in range(2)]
        tmpB = [sb(f"tmpB{i}", [128, 512]) for i in range(2)]
        dtmp = sb("dtmp", [128, 2, 128]); d1buf = sb("d1buf", [128, 4, 128])
        stat = sb("stat", [128, 16])
        bnst = sb("bnst", [128, 12])
        wring = [w[:, :].rearrange("p (k n) -> p k n", n=512) for w in wflat]
        wdring = [w[:, 0:FC * 128].rearrange("p (k n) -> p k n", n=128) for w in wflat]
        modrow = stage[0][0:NR, :]; badab = stage[1][0:NR, :]
        mtm = big[:, 0:4096].bitcast(F32).rearrange("p (j n) -> p j n", n=512)
        psum = [es.enter_context(nc.psum_tensor(f"ps{i}", [128, 512], F32)) for i in range(8)]
        print("sbuf bytes remaining/partition:", nc.sbuf_bytes_remaining)
        es.enter_context(nc.Block())

        PE = Eng(nc, "pe", nc.tensor); ACT = Eng(nc, "act", nc.scalar); DVE = Eng(nc, "dve", nc.vector)
        POOL = Eng(nc, "pool", nc.gpsimd); SP = Eng(nc, "sp", nc.sync)
        _bufs = {}

        def B(name):
            if name not in _bufs:
                _bufs[name] = Buf()
            return _bufs[name]
        _ds = {}

        def DS(name):
            if name not in _ds:
                _ds[name] = DSem(nc, "d_" + name)
            return _ds[name]

        ps_i = [0]

        def next_ps():
            i = ps_i[0] % 8
            ps_i[0] += 1
            return psum[i], B(f"ps{i}")

        st_i = [0]

        def next_stage():
            i = st_i[0] % 3
            st_i[0] += 1
            return stage[i], B(f"stage{i}"), DS(f"stage{i}")

        wr_i = [0]

        wcache = {}

        def load_block(key, fills, nflat):
            i = wr_i[0] % 3
            wr_i[0] += 1
            sbuf_ = B(f"wring{i}")
            if key is not None and key in wcache:
                scr, scb = wcache[key]
                POOL.dma(DS(f"wring{i}"), wflat[i][:, 0:nflat], scr, reads=[scb], writes=[sbuf_])
                return i, sbuf_
            POOL._wait(Eng._deps([], [sbuf_]))
            for dst_fn, src in fills:
                POOL.dma(DS(f"wring{i}"), dst_fn(i), src)
            sbuf_.w = (DS(f"wring{i}").sem, DS(f"wring{i}").count)
            sbuf_.r = {}
            if key is not None and WCACHE:
                scr = nc.dram_tensor("wc_" + key, [128, nflat], BF16, kind="Internal").ap()
                scb = Buf()
                SP.dma(DS(f"wst{i}"), scr, wflat[i][:, 0:nflat], reads=[sbuf_], writes=[scb])
                wcache[key] = (scr, scb)
            return i, sbuf_

        def load_w(src, kc, ncols):
            src3, key = src
            i, b_ = load_block(key, [(lambda i: wring[i][:, 0:kc, 0:ncols], src3)], kc * 512)
            return wring[i], b_

        def wsrc(w, r0, nr, c0, ncols, cache=True):
            ap = w[r0:r0 + nr, c0:c0 + ncols].rearrange("(k p) n -> p k n", p=128)
            return ap, (f"{w.tensor.name}_{r0}_{c0}_{ncols}" if cache else None)

        ev_i = [0]

        def evac_eng():
            ev_i[0] += 1
            return ACT if ev_i[0] % 2 else DVE

        def copy_op(E, out, in_, scale=None):
            if E is ACT:
                if scale is None:
                    return lambda e: e.copy(out=out, in_=in_)
                return lambda e: e.mul(out=out, in_=in_, mul=scale)
            if scale is None:
                return lambda e: e.tensor_copy(out=out, in_=in_)
            return lambda e: e.tensor_scalar_mul(out=out, in0=in_, scalar1=scale)

        def mm(out, pairs):
            n = len(pairs)
            return [(lambda e, l=l, r=r, i=i: e.matmul(out, lhsT=l, rhs=r, start=(i == 0), stop=(i == n - 1)))
                    for i, (l, r) in enumerate(pairs)]

        POOL.op(lambda e: e.memset(ones_f[:], 1.0), writes=[B("ones_f")])
        POOL.op(lambda e: e.memset(ident_f[:], 0.0), writes=[B("ident_f")])
        POOL.op(lambda e: e.affine_select(out=ident_f[:], in_=ident_f[:], pattern=[[-1, 128]], compare_op=ALU.not_equal,
                                          fill=1.0, base=0, channel_multiplier=1), reads=[B("ident_f")], writes=[B("ident_f")])
        POOL.op(lambda e: e.affine_select(out=tri_f[:], in_=ones_f[:], pattern=[[1, 128]], compare_op=ALU.is_ge,
                                          fill=0.0, base=0, channel_multiplier=-1), reads=[B("ones_f")], writes=[B("tri_f")])
        POOL.op(lambda e: e.affine_select(out=elast_f[:], in_=ones_f[:], pattern=[[0, 128]], compare_op=ALU.is_equal,
                                          fill=0.0, base=-127, channel_multiplier=1), reads=[B("ones_f")], writes=[B("elast_f")])
        DVE.op(lambda e: e.tensor_copy(out=ident_b[:], in_=ident_f[:]), reads=[B("ident_f")], writes=[B("ident_b")])
        DVE.op(lambda e: e.tensor_copy(out=tri_b[:], in_=tri_f[:]), reads=[B("tri_f")], writes=[B("tri_b")])
        DVE.op(lambda e: e.tensor_copy(out=ones_b[:], in_=ones_f[:]), reads=[B("ones_f")], writes=[B("ones_b")])
        POOL.op(lambda e: e.iota(posS[:, :, 0], pattern=[[128, 16]], base=0, channel_multiplier=1,
                                 allow_small_or_imprecise_dtypes=True), writes=[B("posS")])
        for h in (1, 2, 3):
            DVE.op(lambda e, h=h: e.tensor_scalar_mul(out=posS[:, :, h], in0=posS[:, :, 0], scalar1=SLOPES[h]),
                   reads=[B("posS")], writes=[B("posS")])
        DVE.op(lambda e: e.tensor_scalar_mul(out=posS[:, :, 0], in0=posS[:, :, 0], scalar1=SLOPES[0]),
               reads=[B("posS")], writes=[B("posS")])
        POOL.op(lambda e: e.memset(Vf[:, :, :, 64:65], 1.0), writes=[B("Vf")])
        POOL.op(lambda e: e.memset(Vd[:, :, :, 128:129], 1.0), writes=[B("Vd")])
        POOL.op(lambda e: e.memset(lfacc[:], 0.0), writes=[B("lfacc")])
        SP.dma(DS("c0"), bfb, b_forget.partition_broadcast(128), writes=[B("bfb")])
        for i, v in enumerate((ln1g, ln1b, ln2g, ln2b)):
            SP.dma(DS("c0"), lng[:, i, :], v.partition_broadcast(128), writes=[B("lng")])
        SP.dma(DS("c0"), gsub, subln.partition_broadcast(128), writes=[B("gsub")])
        for i, v in enumerate((lq1, lk1, lq2, lk2)):
            SP.dma(DS("c0"), cst[:, 2 + i, 0:64], v.partition_broadcast(128), writes=[B("cstl")])
        for nm in ("bfb", "lng", "gsub", "cstl"):
            B(nm).w = (DS("c0").sem, DS("c0").count)
        DVE.op(lambda e: e.tensor_scalar_mul(out=gsub, in0=gsub, scalar1=1.0 - LAMBDA_INIT), reads=[B("gsub")], writes=[B("gsub")])
        DVE.op(lambda e: e.tensor_tensor(out=stage[2][:, 256:320], in0=cst[:, 2, 0:64], in1=cst[:, 3, 0:64], op=ALU.mult),
               reads=[B("cstl")], writes=[B("stage2")])
        DVE.op(lambda e: e.tensor_tensor(out=stage[2][:, 320:384], in0=cst[:, 4, 0:64], in1=cst[:, 5, 0:64], op=ALU.mult),
               reads=[B("cstl")], writes=[B("stage2")])
        DVE.op(lambda e: e.reduce_sum(out=small[:, 0:2], in_=stage[2][:, 256:384].rearrange("p (a b) -> p a b", b=64),
                                      axis=mybir.AxisListType.X), reads=[B("stage2")], writes=[B("small")])
        ACT.op(lambda e: e.activation(out=small[:, 2:4], in_=small[:, 0:2], func=AF.Exp), reads=[B("small")], writes=[B("small")])
        DVE.op(lambda e: e.tensor_tensor(out=small[:, 4:5], in0=small[:, 3:4], in1=small[:, 2:3], op=ALU.subtract),
               reads=[B("small")], writes=[B("small")])
        DVE.op(lambda e: e.tensor_scalar_add(out=lamc[:, 0:1], in0=small[:, 4:5], scalar1=-LAMBDA_INIT), reads=[B("small")], writes=[B("lamc")])

        SP.dma(DS("c1"), xs[0:NP, 0, :], cp, writes=[B("xs0")])
        SP.dma(DS("c1"), xs[NP:NR, 0, :], cs, writes=[B("xs0")])
        for k in range(KC):
            p_, pb_ = next_ps()
            PE.op(lambda e, k=k, p_=p_: e.transpose(out=p_[:, 0:NR], in_=xs[0:NR, 0, k * 128:(k + 1) * 128], identity=ident_f[0:NR, 0:NR]),
                  reads=[B("xs0"), B("ident_f")], writes=[pb_])
            DVE.op(lambda e, k=k, p_=p_: e.tensor_copy(out=cT[:, k, :], in_=p_[:, 0:NR]), reads=[pb_], writes=[B("cT")])
        for blk in range(12):
            wt, wb = load_w(wsrc(w_ada, 0, D, blk * 512, 512, cache=False), KC, 512)
            SP.dma(DS("bada"), badab, b_ada[0:1, blk * 512:(blk + 1) * 512].partition_broadcast(NR), writes=[B("stage1")])
            p_, pb_ = next_ps()
            PE.group(mm(p_[0:NR, :], [(cT[:, k, :], wt[:, k, :]) for k in range(KC)]), reads=[B("cT"), wb], writes=[pb_])
            DVE.op(lambda e, p_=p_: e.tensor_tensor(out=modrow, in0=p_[0:NR, :], in1=badab, op=ALU.add),
                   reads=[pb_, B("stage1")], writes=[B("stage0")])
            if blk in (2, 3, 8, 9):
                DVE.op(lambda e: e.tensor_scalar_add(out=modrow, in0=modrow, scalar1=1.0), reads=[B("stage0")], writes=[B("stage0")])
            p2, pb2 = next_ps()
            for q in range(4):
                PE.op(lambda e, q=q, p2=p2: e.transpose(out=p2[:, q * NR:(q + 1) * NR], in_=modrow[:, q * 128:(q + 1) * 128],
                                                        identity=ident_f[0:NR, 0:NR]), reads=[B("stage0"), B("ident_f")], writes=[pb2])
            DVE.op(lambda e, blk=blk, p2=p2: e.tensor_copy(out=modT[:, blk * 4:(blk + 1) * 4, :],
                                                          in_=p2[:, 0:4 * NR].rearrange("p (q r) -> p q r", r=NR)),
                   reads=[pb2], writes=[B("modT")])
        SH1, SC1, G1, SH2, SC2, G2 = 0, 8, 16, 24, 32, 40

        POOL.dma(DS("wfa"), wfa, wsrc(w_in, 0, D, C_FA, 8)[0], writes=[B("wfa")])

        def phase_c(NT, SAMPLE, mcol, mbc, y_rows, tiles, hbufs):
            ntok = NT * 128
            v3 = lambda ap: ap.rearrange("p (s q) -> p s q", q=8)

            def tm_block(wt, wb, j, ncols=512):
                p_, pb_ = next_ps()
                PE.group(mm(p_[:, 0:ncols], [(hT[:, k, j * 128:(j + 1) * 128], wt[:, k, 0:ncols]) for k in range(KC)]),
                         reads=hbufs + [wb], writes=[pb_])
                return p_, pb_
            oTb = [B(f"oT{k}") for k in range(KC)]

            def transpose8(src_fn, src_buf, j):
                for half in range(2):
                    p_, pb_ = next_ps()
                    PE.group([(lambda e, q=q, p_=p_: e.matmul(p_[:, q * 128:(q + 1) * 128], lhsT=src_fn(4 * half + q), rhs=ident_b[:],
                                                              start=True, stop=True)) for q in range(4)],
                             reads=[src_buf, B("ident_b")], writes=[pb_])
                    E = evac_eng()
                    E.op(copy_op(E, oT[:, 4 * half:4 * half + 4, j * 128:(j + 1) * 128], p_[:, :].rearrange("p (q t) -> p q t", t=128)),
                         reads=[pb_], writes=oTb[4 * half:4 * half + 4])

            for j in range(NT):
                transpose8(lambda q, j=j: oAB[:, j, q * 128:(q + 1) * 128], B(f"oAB{j}"), j)
            for n in range(2):
                i3, wab_b = load_block(f"wab_{n}", [(lambda i: wring[i][:, 0:4, :], wsrc(w_ba, 0, 512, n * 512, 512)[0]),
                                                     (lambda i: wring[i][:, 4:8, :], wsrc(w_bb, 0, 512, n * 512, 512)[0])], KC * 512)
                wab_t = wring[i3]
                for br in range(2):
                    wga, wgab = load_w(wsrc(w_in, 0, D, (C_GA if br == 0 else C_GB) + n * 512, 512), KC, 512)
                    for j in range(NT):
                        mtb = [B(f"big{2 * j}"), B(f"big{2 * j + 1}")]
                        pm, pmb = next_ps()
                        PE.group(mm(pm[:, :], [(oT[:, 4 * br + q, j * 128:(j + 1) * 128], wab_t[:, 4 * br + q, :]) for q in range(4)]),
                                 reads=oTb + [wab_b], writes=[pmb])
                        pg, pgb = tm_block(wga, wgab, j)
                        ACT.op(lambda e, pg=pg, j=j: e.activation(out=tmpA[j % 2][:], in_=pg[:, :], func=AF.Sigmoid),
                               reads=[pgb], writes=[B(f"tmpA{j % 2}")])
                        if br == 0:
                            DVE.op(lambda e, pm=pm, j=j: e.tensor_tensor(out=mtm[:, j, :], in0=pm[:, :], in1=tmpA[j % 2][:], op=ALU.mult),
                                   reads=[pmb, B(f"tmpA{j % 2}")], writes=mtb)
                        else:
                            DVE.op(lambda e, pm=pm, j=j: e.tensor_tensor(out=tmpB[j % 2][:], in0=pm[:, :], in1=tmpA[j % 2][:], op=ALU.mult),
                                   reads=[pmb, B(f"tmpA{j % 2}")], writes=[B(f"tmpB{j % 2}")])
                            DVE.op(lambda e, j=j, n=n: e.tensor_tensor(out=oAB[:, j, n * 512:(n + 1) * 512], in0=mtm[:, j, :], in1=tmpB[j % 2][:], op=ALU.add),
                                   reads=mtb + [B(f"tmpB{j % 2}")], writes=[B(f"oAB{j}")])
            for j in range(NT):
                transpose8(lambda q, j=j: oAB[:, j, q * 128:(q + 1) * 128], B(f"oAB{j}"), j)

            def lin_back(get_w, nk, rhs_fn, rhs_bufs, gidx):
                for nchunk in range(8):
                    wt, wb, wsel = get_w(nchunk)
                    p_, pb_ = next_ps()
                    PE.group(mm(p_[:, 0:ntok], [(wsel(wt, k), rhs_fn(k)) for k in range(nk)]), reads=rhs_bufs + [wb], writes=[pb_])
                    tb = tmpA[nchunk % 2]
                    tbb = B(f"tmpA{nchunk % 2}")
                    if not SAMPLE:
                        ACT.op(lambda e, p_=p_, tb=tb, nchunk=nchunk: e.activation(out=tb[:, 0:ntok], in_=p_[:, 0:ntok], func=AF.Identity,
                                                                                    scale=mcol(gidx + nchunk)),
                               reads=[pb_, B("modT")], writes=[tbb])
                    else:
                        DVE.op(lambda e, p_=p_, tb=tb, nchunk=nchunk: e.tensor_tensor(out=v3(tb[:, 0:128]), in0=v3(p_[:, 0:128]),
                                                                                       in1=mbc(gidx + nchunk), op=ALU.mult),
                               reads=[pb_, B("modT")], writes=[tbb])
                    p2, pb2 = next_ps()
                    for j in range(NT):
                        PE.op(lambda e, j=j, p2=p2, tb=tb: e.transpose(out=p2[:, j * 128:(j + 1) * 128], in_=tb[:, j * 128:(j + 1) * 128],
                                                                       identity=ident_f[:]), reads=[tbb, B("ident_f")], writes=[pb2])
                    for j in range(NT):
                        DVE.op(lambda e, j=j, p2=p2, nchunk=nchunk: e.scalar_tensor_tensor(
                            out=xs[:, j, nchunk * 128:(nchunk + 1) * 128], in0=xs[:, j, nchunk * 128:(nchunk + 1) * 128], scalar=ALPHA,
                            in1=p2[:, j * 128:(j + 1) * 128], op0=ALU.mult, op1=ALU.add), reads=[pb2, B(f"xs{j}")], writes=[B(f"xs{j}")])

            def layer_norm(j, gi):
                xb = B(f"xs{j}")
                for s_ in range(2):
                    DVE.op(lambda e, s_=s_: e.bn_stats(out=bnst[:, s_ * 6:(s_ + 1) * 6], in_=xs[:, j, s_ * 512:(s_ + 1) * 512]), reads=[xb], writes=[B("bnst")])
                DVE.op(lambda e: e.bn_aggr(out=stat[:, 13:15], in_=bnst[:]), reads=[B("bnst")], writes=[B("lnstat")])
                ACT.op(lambda e: e.activation(out=stat[:, 15:16], in_=stat[:, 14:15], func=AF.Sqrt, bias=LN_EPS, scale=1.0),
                       reads=[B("lnstat")], writes=[B("lnstat2")])
                DVE.op(lambda e: e.reciprocal(out=stat[:, 15:16], in_=stat[:, 15:16]), reads=[B("lnstat2")], writes=[B("lnstat2")])
                DVE.op(lambda e: e.tensor_scalar(out=xs[:, j, :], in0=xs[:, j, :], scalar1=stat[:, 13:14], scalar2=stat[:, 15:16],
                                                 op0=ALU.subtract, op1=ALU.mult), reads=[xb, B("lnstat"), B("lnstat2")], writes=[xb])
                DVE.op(lambda e: e.tensor_tensor(out=xs[:, j, :], in0=xs[:, j, :], in1=lng[:, gi, :], op=ALU.mult),
                       reads=[xb, B("lng")], writes=[xb])
                DVE.op(lambda e: e.tensor_tensor(out=xs[:, j, :], in0=xs[:, j, :], in1=lng[:, gi + 1, :], op=ALU.add),
                       reads=[xb, B("lng")], writes=[xb])

            if STOP == 'C3':
                return
            cur = [None]

            def get_wout(nchunk):
                if nchunk % 4 == 0:
                    cur[0] = load_w(wsrc(w_out, 0, D, nchunk * 128, 512), KC, 512)
                wt, wb = cur[0]
                c0 = (nchunk % 4) * 128
                return wt, wb, (lambda wt, k: wt[:, k, c0:c0 + 128])
            lin_back(get_wout, KC, lambda k: oT[:, k, 0:ntok], oTb, G1)
            for j in range(NT):
                layer_norm(j, 0)
            for k in range(KC):
                p_, pb_ = next_ps()
                for j in range(NT):
                    PE.op(lambda e, j=j, k=k, p_=p_: e.transpose(out=p_[:, j * 128:(j + 1) * 128], in_=xs[:, j, k * 128:(k + 1) * 128],
                                                                 identity=ident_f[:]), reads=[B(f"xs{j}"), B("ident_f")], writes=[pb_])
                if not SAMPLE:
                    ACT.op(lambda e, k=k, p_=p_: e.activation(out=oT[:, k, 0:ntok], in_=p_[:, 0:ntok], func=AF.Identity,
                                                              bias=mcol(SH2 + k), scale=mcol(SC2 + k)),
                           reads=[pb_, B("modT")], writes=[B(f"oT{k}")])
                else:
                    DVE.op(lambda e, k=k, p_=p_: e.tensor_tensor(out=v3(tmpA[0][:, 0:128]), in0=v3(p_[:, 0:128]), in1=mbc(SC2 + k), op=ALU.mult),
                           reads=[pb_, B("modT")], writes=[B("tmpA0")])
                    DVE.op(lambda e, k=k: e.tensor_tensor(out=v3(oT[:, k, 0:128]), in0=v3(tmpA[0][:, 0:128]), in1=mbc(SH2 + k), op=ALU.add),
                           reads=[B("tmpA0"), B("modT")], writes=[B(f"oT{k}")])
            if STOP == 'C6':
                return
            aT = big[:, :].rearrange("p (f t) -> p f t", t=512)
            for blk in range(11):
                i3, wb = load_block(f"wgu_{blk}", [(lambda i: wring[i][:, :, 0:256], wsrc(w_fg, 0, D, blk * 256, 256)[0]),
                                                    (lambda i: wring[i][:, :, 256:512], wsrc(w_fu, 0, D, blk * 256, 256)[0])], KC * 512)
                wt = wring[i3]
                for s_ in range(2):
                    fc = blk * 2 + s_
                    pg, pgb = next_ps()
                    PE.group(mm(pg[:, 0:ntok], [(wt[:, k, s_ * 128:(s_ + 1) * 128], oT[:, k, 0:ntok]) for k in range(KC)]), reads=oTb + [wb], writes=[pgb])
                    pu, pub = next_ps()
                    PE.group(mm(pu[:, 0:ntok], [(wt[:, k, 256 + s_ * 128:256 + (s_ + 1) * 128], oT[:, k, 0:ntok]) for k in range(KC)]), reads=oTb + [wb], writes=[pub])
                    tb = tmpB[fc % 2]
                    ACT.op(lambda e, pg=pg, tb=tb: e.activation(out=tb[:, 0:ntok], in_=pg[:, 0:ntok], func=AF.Silu), reads=[pgb], writes=[B(f"tmpB{fc % 2}")])
                    DVE.op(lambda e, pu=pu, tb=tb, fc=fc: e.tensor_tensor(out=aT[:, fc, 0:ntok], in0=pu[:, 0:ntok], in1=tb[:, 0:ntok], op=ALU.mult),
                           reads=[pub, B(f"tmpB{fc % 2}")], writes=[B(f"big{fc}")])
            def get_wd(nchunk):
                i3, wb = load_block(f"wd_{nchunk}", [(lambda i: wdring[i], w_fd[:, nchunk * 128:(nchunk + 1) * 128].rearrange("(k p) n -> p k n", p=128))],
                                    FC * 128)
                return wdring[i3], wb, (lambda wt, k: wt[:, k, :])
            lin_back(get_wd, FC, lambda k: aT[:, k, 0:ntok], [B(f"big{q}") for q in range(FC)], G2)
            for j, i in enumerate(tiles):
                layer_norm(j, 2)
                SP.dma(DS(f"x{j}"), y_rows(i), xs[:, j, :], reads=[B(f"xs{j}")])


        def chunk(c, pi):
            NT = 4
            ntok = 512
            tiles = [4 * c + j for j in range(4)]
            tok0 = 512 * c
            R0 = pi * T
            mcol = lambda idx: modT[:, idx, pi:pi + 1]
            if c == 0:
                DVE.op(lambda e: e.memset(lfacc[:], 0.0), writes=[B("lfacc")])
            for j, i in enumerate(tiles):
                SP.dma(DS(f"x{j}"), xs[:, j, :], xp[R0 + i * 128:R0 + (i + 1) * 128, :], writes=[B(f"xs{j}")])
            for k in range(KC):
                p_, pb_ = next_ps()
                for j in range(4):
                    PE.op(lambda e, j=j, k=k, p_=p_: e.transpose(out=p_[:, j * 128:(j + 1) * 128], in_=xs[:, j, k * 128:(k + 1) * 128],
                                                                 identity=ident_f[:]), reads=[B(f"xs{j}"), B("ident_f")], writes=[pb_])
                ACT.op(lambda e, k=k, p_=p_: e.activation(out=hT[:, k, :], in_=p_[:, :], func=AF.Identity,
                                                          bias=mcol(SH1 + k), scale=mcol(SC1 + k)),
                       reads=[pb_, B("modT")], writes=[B(f"hT{k}")])
            hbufs = [B(f"hT{k}") for k in range(KC)]
            if STOP == 'A1':
                return

            def fm_block(wt, wb, col0, dst, dbuf, scale):
                p_, pb_ = next_ps()
                PE.group(mm(p_[:, :], [(wt[:, k, col0:col0 + 128], hT[:, k, :]) for k in range(KC)]), reads=hbufs + [wb], writes=[pb_])
                E = evac_eng()
                E.op(copy_op(E, dst, p_[:, :], scale), reads=[pb_], writes=[dbuf])

            def tm_block(wt, wb, j, ncols=512):
                p_, pb_ = next_ps()
                PE.group(mm(p_[:, 0:ncols], [(hT[:, k, j * 128:(j + 1) * 128], wt[:, k, 0:ncols]) for k in range(KC)]),
                         reads=hbufs + [wb], writes=[pb_])
                return p_, pb_

            wt, wb = load_w(wsrc(w_in, 0, D, C_QA, 512), KC, 512)
            for hp in range(4):
                fm_block(wt, wb, hp * 128, QTf[:, hp, :], B(f"oT{hp}"), 0.125)
            wt, wb = load_w(wsrc(w_in, 0, D, C_KA, 512), KC, 512)
            for hp in range(4):
                fm_block(wt, wb, hp * 128, KTf[:, hp, tok0:tok0 + 512], B(f"KTf{hp}"), None)
            for j, i in enumerate(tiles):
                p_, pb_ = tm_block(wt, wb, j)
                st, sbuf_, sds = next_stage()
                E = evac_eng()
                E.op(copy_op(E, st[:], p_[:, :]), reads=[pb_], writes=[sbuf_])
                SP.dma(sds, kfp[R0 + i * 128:R0 + (i + 1) * 128, :], st[:], reads=[sbuf_])
            if STOP == 'A2':
                return
            wt, wb = load_w(wsrc(w_in, 0, D, C_VA, 512), KC, 512)
            for j, i in enumerate(tiles):
                p_, pb_ = tm_block(wt, wb, j)
                st, sbuf_, sds = next_stage()
                ACT.op(copy_op(ACT, st[:], p_[:, :]), reads=[pb_], writes=[sbuf_])
                DVE.op(lambda e, i=i, st=st: e.tensor_copy(out=Vf[:, i, :, 0:64], in_=st[:, :].rearrange("p (h d) -> p h d", d=64)),
                       reads=[sbuf_], writes=[B("Vf")])
                SP.dma(sds, vfp[R0 + i * 128:R0 + (i + 1) * 128, :], st[:], reads=[sbuf_])
            if STOP == 'A3':
                return
            for j, i in enumerate(tiles):
                p_, pb_ = next_ps()
                PE.group(mm(p_[:, 0:8], [(hT[:, k, j * 128:(j + 1) * 128], wfa[:, k, :]) for k in range(KC)]),
                         reads=hbufs + [B("wfa")], writes=[pb_])
                DVE.op(lambda e, p_=p_: e.tensor_tensor(out=fa_t[:], in0=p_[:, 0:8], in1=bfb, op=ALU.add),
                       reads=[pb_, B("bfb")], writes=[B("fa_t")])
                ACT.op(lambda e: e.activation(out=fa_t[:], in_=fa_t[:], func=AF.Exp, scale=-1.0), reads=[B("fa_t")], writes=[B("fa_t")])
                ACT.op(lambda e: e.activation(out=fa_t[:], in_=fa_t[:], func=AF.Ln, bias=1.0, scale=1.0), reads=[B("fa_t")], writes=[B("fa_t")])
                DVE.op(lambda e, j=j: e.tensor_scalar_mul(out=lf_t[:, j, :], in0=fa_t[:], scalar1=-1.0), reads=[B("fa_t")], writes=[B(f"lf{j}")])
                SP.dma(DS(f"lf{j}"), lfp[R0 + i * 128:R0 + (i + 1) * 128, :], lf_t[:, j, :], reads=[B(f"lf{j}")])
                p2, pb2 = next_ps()
                PE.group(mm(p2[:, 0:8], [(tri_f[:], lf_t[:, j, :]), (ones_f[:], lfacc[:])]),
                         reads=[B(f"lf{j}"), B("lfacc"), B("tri_f"), B("ones_f")], writes=[pb2])
                DVE.op(lambda e, i=i, p2=p2: e.tensor_copy(out=c_all[:, i, :], in_=p2[:, 0:8]), reads=[pb2], writes=[B("c_all")])
                DVE.op(lambda e, j=j: e.tensor_tensor(out=lfacc[:], in0=lfacc[:], in1=lf_t[:, j, :], op=ALU.add),
                       reads=[B("lfacc"), B(f"lf{j}")], writes=[B("lfacc")])
            if STOP == 'A4':
                return
            wt, wb = load_w(wsrc(w_in, 0, D, C_QB, 512), KC, 512)
            for h in range(4):
                fm_block(wt, wb, h * 128, QTd[:, h, :], B(f"oT{4 + h}"), 0.125)
            wt, wb = load_w(wsrc(w_in, 0, D, C_KB, 512), KC, 512)
            for h in range(4):
                fm_block(wt, wb, h * 128, KTd[:, h, tok0:tok0 + 512], B(f"KTd{h}"), None)
            for j, i in enumerate(tiles):
                p_, pb_ = tm_block(wt, wb, j)
                st, sbuf_, sds = next_stage()
                E = evac_eng()
                E.op(copy_op(E, st[:], p_[:, :]), reads=[pb_], writes=[sbuf_])
                SP.dma(sds, kdp[R0 + i * 128:R0 + (i + 1) * 128, :], st[:], reads=[sbuf_])
            wt, wb = load_w(wsrc(w_in, 0, D, C_VB, 512), KC, 512)
            for j, i in enumerate(tiles):
                p_, pb_ = tm_block(wt, wb, j)
                st, sbuf_, sds = next_stage()
                ACT.op(copy_op(ACT, st[:], p_[:, :]), reads=[pb_], writes=[sbuf_])
                DVE.op(lambda e, i=i, st=st: e.tensor_copy(out=Vd[:, i, :, 0:128], in_=st[:, :].rearrange("p (h d) -> p h d", d=128)),
                       reads=[sbuf_], writes=[B("Vd")])
                SP.dma(sds, vdp[R0 + i * 128:R0 + (i + 1) * 128, :], st[:], reads=[sbuf_])

            if STOP == 'A':
                return
            nkb = 4 * c + 4
            p_, pb_ = next_ps()
            PE.op(lambda e, p_=p_: e.matmul(p_[:, 0:8], lhsT=elast_f[:], rhs=c_all[:, 4 * c + 1, :], start=True, stop=True),
                  reads=[B("elast_f"), B("c_all")], writes=[pb_])
            DVE.op(lambda e, p_=p_: e.tensor_copy(out=rc_bc[:], in_=p_[:, 0:8]), reads=[pb_], writes=[B("rc_bc")])
            DVE.op(lambda e: e.tensor_tensor(out=biasF[:, 0:nkb, :], in0=rc_bc[:].unsqueeze(1).to_broadcast([128, nkb, 8]),
                                             in1=c_all[:, 0:nkb, :], op=ALU.subtract),
                   reads=[B("rc_bc"), B("c_all")], writes=[B("biasF")])
            for h in range(4):
                DVE.op(lambda e, h=h: e.tensor_scalar_add(out=biasD[:, 0:nkb, h], in0=posS[:, 0:nkb, h],
                                                          scalar1=-SLOPES[h] * (512 * c + 256)),
                       reads=[B("posS")], writes=[B("biasD")])
            PT0 = big[:, 0:16 * 512].rearrange("p (k q) -> p k q", q=512)

            def ptv(bi, kb):
                if bi == 0:
                    return PT0[:, kb, :]
                return wring[1][:, kb, :] if kb < 8 else wring[2][:, kb - 8, :]

            def ptb(bi, kb):
                if bi == 0:
                    return B(f"big{kb}")
                return B("wring1") if kb < 8 else B("wring2")

            def scores(bi, KT, ktb, QT, qtb, kslot, r0, bias_ap, bias_buf):
                for kb in range(nkb):
                    j0 = max(0, kb - 4 * c)
                    nq = (4 - j0) * 128
                    p_, pb_ = next_ps()
                    PE.op(lambda e, p_=p_, kb=kb, j0=j0, nq=nq: e.matmul(
                        p_[:, 0:nq], lhsT=KT[r0:r0 + 64, kslot, kb * 128:(kb + 1) * 128], rhs=QT[r0:r0 + 64, kslot, j0 * 128:512],
                        start=True, stop=True), reads=[ktb, qtb], writes=[pb_])
                    ACT.op(lambda e, p_=p_, kb=kb, j0=j0, nq=nq: e.activation(
                        out=ptv(bi, kb)[:, j0 * 128:512], in_=p_[:, 0:nq], func=AF.Exp, bias=bias_ap(kb), scale=1.0),
                        reads=[pb_, bias_buf], writes=[ptb(bi, kb)])
                    if kb >= 4 * c:
                        DVE.op(lambda e, kb=kb, j0=j0: e.tensor_tensor(out=ptv(bi, kb)[:, j0 * 128:(j0 + 1) * 128],
                                                                       in0=ptv(bi, kb)[:, j0 * 128:(j0 + 1) * 128], in1=tri_b[:], op=ALU.mult),
                               reads=[ptb(bi, kb), B("tri_b")], writes=[ptb(bi, kb)])

            def pv(bi, j, V, vbuf, h, ncol):
                i = 4 * c + j
                p_, pb_ = next_ps()
                PE.group(mm(p_[:, 0:ncol], [(ptv(bi, kb)[:, j * 128:(j + 1) * 128], V[:, kb, h, :]) for kb in range(i + 1)]),
                         reads=[ptb(bi, kb) for kb in range(i + 1)] + [vbuf], writes=[pb_])
                return p_, pb_

            tasks = [("f", h, 0) for h in range(8)] + [("d", h, m) for h in range(4) for m in range(2)]

            def do_scores(ti):
                kind, h, m = tasks[ti]
                if kind == "f":
                    hp, r0 = h // 2, (h % 2) * 64
                    scores(ti % 2, KTf, B(f"KTf{hp}"), QTf, B(f"oT{hp}"), hp, r0, lambda kb, h=h: biasF[:, kb, h:h + 1], B("biasF"))
                else:
                    scores(ti % 2, KTd, B(f"KTd{h}"), QTd, B(f"oT{4 + h}"), h, m * 64, lambda kb, h=h: biasD[:, kb, h:h + 1], B("biasD"))

            def do_pv(ti):
                kind, h, m = tasks[ti]
                bi = ti % 2
                for j in range(4):
                    if kind == "f":
                        p_, pb_ = pv(bi, j, Vf, B("Vf"), h, 65)
                        DVE.op(lambda e, p_=p_: e.reciprocal(out=stat[:, 0:1], in_=p_[:, 64:65]), reads=[pb_], writes=[B("stat")])
                        DVE.op(lambda e, p_=p_, j=j, h=h: e.tensor_scalar_mul(out=oAB[:, j, h * 64:(h + 1) * 64], in0=p_[:, 0:64], scalar1=stat[:, 0:1]),
                               reads=[pb_, B("stat")], writes=[B(f"oAB{j}")])
                        continue
                    p_, pb_ = pv(bi, j, Vd, B("Vd"), h, 129)
                    if m == 0:
                        DVE.op(lambda e, p_=p_, j=j: e.reciprocal(out=stat[:, 4 + j:5 + j], in_=p_[:, 128:129]), reads=[pb_], writes=[B("statd")])
                        DVE.op(lambda e, p_=p_, j=j: e.tensor_scalar_mul(out=d1buf[:, j, :], in0=p_[:, 0:128], scalar1=stat[:, 4 + j:5 + j]),
                               reads=[pb_, B("statd")], writes=[B(f"d1_{j}")])
                    else:
                        DVE.op(lambda e, p_=p_: e.reciprocal(out=stat[:, 8:9], in_=p_[:, 128:129]), reads=[pb_], writes=[B("stat2")])
                        DVE.op(lambda e: e.tensor_tensor(out=stat[:, 9:10], in0=stat[:, 8:9], in1=lamc[:, 0:1], op=ALU.mult),
                               reads=[B("stat2"), B("lamc")], writes=[B("stat2")])
                        DVE.op(lambda e, p_=p_, j=j: e.scalar_tensor_tensor(out=dtmp[:, 0, :], in0=p_[:, 0:128], scalar=stat[:, 9:10],
                                                                             in1=d1buf[:, j, :], op0=ALU.mult, op1=ALU.add),
                               reads=[pb_, B("stat2"), B(f"d1_{j}")], writes=[B("dtmp0")])
                        ACT.op(lambda e: e.activation(out=dtmp[:, 1, :], in_=dtmp[:, 0, :], func=AF.Square),
                               reads=[B("dtmp0")], writes=[B("dtmp1")])
                        DVE.op(lambda e: e.reduce_sum(out=stat[:, 10:11], in_=dtmp[:, 1, :], axis=mybir.AxisListType.X),
                               reads=[B("dtmp1")], writes=[B("stat3")])
                        ACT.op(lambda e: e.activation(out=stat[:, 11:12], in_=stat[:, 10:11], func=AF.Sqrt, bias=LN_EPS, scale=1.0 / 128),
                               reads=[B("stat3")], writes=[B("stat3")])
                        DVE.op(lambda e: e.reciprocal(out=stat[:, 12:13], in_=stat[:, 11:12]), reads=[B("stat3")], writes=[B("stat3")])
                        DVE.op(lambda e, j=j, h=h: e.scalar_tensor_tensor(out=oAB[:, j, 512 + h * 128:512 + (h + 1) * 128], in0=dtmp[:, 0, :],
                                                                           scalar=stat[:, 12:13], in1=gsub, op0=ALU.mult, op1=ALU.mult),
                               reads=[B("dtmp0"), B("stat3"), B("gsub")], writes=[B(f"oAB{j}")])

            do_scores(0)
            for ti in range(len(tasks)):
                if ti + 1 < len(tasks):
                    do_scores(ti + 1)
                do_pv(ti)

            if STOP == 'B':
                return
            phase_c(4, False, mcol, None, lambda i: yp[R0 + i * 128:R0 + (i + 1) * 128, :], tiles, hbufs)

        idx_all = stage[3][:, :].bitcast(I32)
        SP.dma(DS("pts"), idx_all[:, 0:NS * 256], pt.partition_broadcast(128), writes=[B("stage3")])
        DVE.op(lambda e: e.tensor_copy(out=stage[2][:, 0:NS * 256], in_=idx_all[:, 0:NS * 256]), reads=[B("stage3")], writes=[B("stage2")])
        POOL.op(lambda e: e.iota(small[:, 20:21], pattern=[[0, 1]], base=0, channel_multiplier=1, allow_small_or_imprecise_dtypes=True),
                writes=[B("small20")])
        DVE.op(lambda e: e.tensor_scalar(out=stage[2][:, 0:NS * 256], in0=stage[2][:, 0:NS * 256], scalar1=128.0, scalar2=small[:, 20:21],
                                         op0=ALU.mult, op1=ALU.add), reads=[B("stage2"), B("small20")], writes=[B("stage2")])
        DVE.op(lambda e: e.tensor_copy(out=idx_all[:, 0:NS * 256], in_=stage[2][:, 0:NS * 256]), reads=[B("stage2")], writes=[B("stage3")])
        ptT = cst[:, 6, :].bitcast(I32)
        with nc.allow_non_contiguous_dma(reason="tiny page-table transpose"):
            SP.dma(DS("ptT"), ptT[:, 0:NS * 2], pt.rearrange("o (c p) -> p (o c)", p=128), writes=[B("ptT")])

        def gather(dsem, out, src2d, idx_col):
            ins = nc.gpsimd.indirect_dma_start(out=out, out_offset=None, in_=src2d,
                                               in_offset=bass.IndirectOffsetOnAxis(ap=idx_col, axis=0))
            dsem.count += 16
            ins.then_inc(dsem.sem, 16)
        maskS_f = sb("maskS_f", [128, 128]); maskS_b = sb("maskS_b", [128, 128], BF16); MB_f = sb("MB_f", [128, 128])
        alibS = sb("alibS", [128, 16, 8]); alibN = sb("alibN", [128, 16]); negcs = sb("negcs", [128, 8])
        m3 = maskS_f[:, :].rearrange("p (b q) -> p b q", q=8)
        POOL.op(lambda e: e.affine_select(out=m3, in_=ones_f[:, :].rearrange("p (b q) -> p b q", q=8), pattern=[[-8, 16], [0, 8]],
                                          compare_op=ALU.is_ge, fill=0.0, base=0, channel_multiplier=1), reads=[B("ones_f")], writes=[B("maskS")])
        POOL.op(lambda e: e.affine_select(out=m3, in_=m3, pattern=[[8, 16], [1, 8]], compare_op=ALU.is_ge, fill=0.0, base=0,
                                          channel_multiplier=-1), reads=[B("maskS")], writes=[B("maskS")])
        DVE.op(lambda e: e.tensor_copy(out=maskS_b[:], in_=maskS_f[:]), reads=[B("maskS")], writes=[B("maskSb")])
        mb3 = MB_f[:, :].rearrange("p (b w) -> p b w", w=16)
        POOL.op(lambda e: e.affine_select(out=mb3, in_=ones_f[:, :].rearrange("p (b w) -> p b w", w=16), pattern=[[-16, 8], [-1, 16]],
                                          compare_op=ALU.is_gt, fill=0.0, base=0, channel_multiplier=1), reads=[B("ones_f")], writes=[B("MB")])
        POOL.op(lambda e: e.affine_select(out=mb3, in_=mb3, pattern=[[16, 8], [0, 16]], compare_op=ALU.is_ge, fill=0.0, base=15,
                                          channel_multiplier=-1), reads=[B("MB")], writes=[B("MB")])
        for hm in range(8):
            DVE.op(lambda e, hm=hm: e.tensor_scalar_add(out=alibS[:, :, hm], in0=posS[:, :, hm // 2], scalar1=-2048.0 * SLOPES[hm // 2]),
                   reads=[B("posS")], writes=[B("alibS")])
        DVE.op(lambda e: e.reduce_sum(out=alibN[:, 8:9], in_=maskS_f[:, :], axis=mybir.AxisListType.X), reads=[B("maskS")], writes=[B("alibN")])
        DVE.op(lambda e: e.tensor_scalar(out=alibN[:, 9:10], in0=alibN[:, 8:9], scalar1=-1.0, scalar2=8.0, op0=ALU.mult, op1=ALU.add),
               reads=[B("alibN")], writes=[B("alibN")])
        for hm in range(8):
            DVE.op(lambda e, hm=hm: e.tensor_scalar_mul(out=alibN[:, hm:hm + 1], in0=alibN[:, 9:10], scalar1=SLOPES[hm // 2]),
                   reads=[B("alibN")], writes=[B("alibN")])
        QBD = oAB[:, 1:3, :].rearrange("p a (k s c) -> p (a k) s c", s=16, c=16)
        biasAll = xs[:, 1:3, :].rearrange("p a (h b g) -> p (a h) b g", b=16, g=16)
        Lst = xs[:, 3, :]
        Pf = big[:, 0:2048].bitcast(F32).rearrange("p (r h) -> p h r", h=8)
        KTs = big[:, 0:4096].rearrange("p (k n) -> p k n", n=512)
        PTs = big[:, 4096:4096 + 17 * 128].rearrange("p (k n) -> p k n", n=128)
        PTb = [B(f"big{q}") for q in range(8, 13)]
        KTsb = [B(f"big{q}") for q in range(8)]
        osb = tmpB[0][:, :].bitcast(BF16)
        v3 = lambda ap: ap.rearrange("p (s q) -> p s q", q=8)
        clf3 = clf.rearrange("n (a c) -> n a c", a=1)

        def sample_tile(ts):
            r0 = NP + 16 * ts
            hbufs = [B(f"hT{k}") for k in range(KC)]
            mbc = lambda idx: modT[:, idx, r0:r0 + 16].unsqueeze(2).to_broadcast([128, 16, 8])
            rows = slice(ts * 128, (ts + 1) * 128)
            SP.dma(DS("x0"), xs[:, 0, :], xs_in[rows, :], writes=[B("xs0")])
            for k in range(KC):
                p_, pb_ = next_ps()
                PE.op(lambda e, k=k, p_=p_: e.transpose(out=p_[:, 0:128], in_=xs[:, 0, k * 128:(k + 1) * 128], identity=ident_f[:]),
                      reads=[B("xs0"), B("ident_f")], writes=[pb_])
                DVE.op(lambda e, k=k, p_=p_: e.tensor_tensor(out=v3(tmpA[0][:, 0:128]), in0=v3(p_[:, 0:128]), in1=mbc(SC1 + k), op=ALU.mult),
                       reads=[pb_, B("modT")], writes=[B("tmpA0")])
                DVE.op(lambda e, k=k: e.tensor_tensor(out=v3(hT[:, k, 0:128]), in0=v3(tmpA[0][:, 0:128]), in1=mbc(SH1 + k), op=ALU.add),
                       reads=[B("tmpA0"), B("modT")], writes=[B(f"hT{k}")])
            if ts == 0:
                DVE.op(lambda e: e.memset(oAB[:, 1:3, :], 0.0), writes=[B("oAB1"), B("oAB2")])
            qb = [B("oAB1"), B("oAB2")]

            def fm(wt, wb, col0):
                p_, pb_ = next_ps()
                PE.group(mm(p_[:, 0:128], [(wt[:, k, col0:col0 + 128], hT[:, k, 0:128]) for k in range(KC)]), reads=hbufs + [wb], writes=[pb_])
                return p_, pb_

            def tmj(wt, wb, ncols=512):
                p_, pb_ = next_ps()
                PE.group(mm(p_[:, 0:ncols], [(hT[:, k, 0:128], wt[:, k, 0:ncols]) for k in range(KC)]), reads=hbufs + [wb], writes=[pb_])
                return p_, pb_

            def q_block(col, blk0):
                wt, wb = load_w(wsrc(w_in, 0, D, col, 512), KC, 512)
                for q in range(4):
                    p_, pb_ = fm(wt, wb, q * 128)
                    DVE.op(lambda e, p_=p_, q=q: e.tensor_scalar_mul(out=QBD[0:64, blk0 + q, :, 0:8], in0=v3(p_[0:64, 0:128]), scalar1=0.125),
                           reads=[pb_], writes=qb)
                    DVE.op(lambda e, p_=p_, q=q: e.tensor_scalar_mul(out=QBD[64:128, blk0 + q, :, 8:16], in0=v3(p_[64:128, 0:128]), scalar1=0.125),
                           reads=[pb_], writes=qb)

            def k_block(col, KT, nm, oap):
                wt, wb = load_w(wsrc(w_in, 0, D, col, 512), KC, 512)
                for q in range(4):
                    p_, pb_ = fm(wt, wb, q * 128)
                    E = evac_eng()
                    E.op(copy_op(E, KT[:, q, T:T + 128], p_[:, 0:128]), reads=[pb_], writes=[B(f"{nm}{q}")])
                p_, pb_ = tmj(wt, wb)
                st, sbuf_, sds = next_stage()
                E = evac_eng()
                E.op(copy_op(E, st[:], p_[:, :]), reads=[pb_], writes=[sbuf_])
                SP.dma(sds, oap[rows, :], st[:], reads=[sbuf_])

            def v_block(col, V, vb, hd, oap):
                wt, wb = load_w(wsrc(w_in, 0, D, col, 512), KC, 512)
                p_, pb_ = tmj(wt, wb)
                st, sbuf_, sds = next_stage()
                ACT.op(copy_op(ACT, st[:], p_[:, :]), reads=[pb_], writes=[sbuf_])
                DVE.op(lambda e, st=st: e.tensor_copy(out=V[:, 16, :, 0:hd], in_=st[:, :].rearrange("p (h d) -> p h d", d=hd)),
                       reads=[sbuf_], writes=[vb])
                SP.dma(sds, oap[rows, :], st[:], reads=[sbuf_])

            q_block(C_QA, 0)
            k_block(C_KA, KTf, "KTf", kfs)
            v_block(C_VA, Vf, B("Vf"), 64, vfs)
            p_, pb_ = next_ps()
            PE.group(mm(p_[:, 0:8], [(hT[:, k, 0:128], wfa[:, k, :]) for k in range(KC)]), reads=hbufs + [B("wfa")], writes=[pb_])
            DVE.op(lambda e: e.tensor_tensor(out=fa_t[:], in0=p_[:, 0:8], in1=bfb, op=ALU.add), reads=[pb_, B("bfb")], writes=[B("fa_t")])
            ACT.op(lambda e: e.activation(out=fa_t[:], in_=fa_t[:], func=AF.Exp, scale=-1.0), reads=[B("fa_t")], writes=[B("fa_t")])
            ACT.op(lambda e: e.activation(out=fa_t[:], in_=fa_t[:], func=AF.Ln, bias=1.0, scale=1.0), reads=[B("fa_t")], writes=[B("fa_t")])
            DVE.op(lambda e: e.tensor_scalar_mul(out=lf_t[:, 0, :], in0=fa_t[:], scalar1=-1.0), reads=[B("fa_t")], writes=[B("lf0")])
            SP.dma(DS("lf0"), lfs[rows, :], lf_t[:, 0, :], reads=[B("lf0")])
            p2, pb2 = next_ps()
            PE.op(lambda e: e.matmul(p2[:, 0:8], lhsT=maskS_f[:], rhs=lf_t[:, 0, :], start=True, stop=True),
                  reads=[B("maskS"), B("lf0")], writes=[pb2])
            DVE.op(lambda e: e.tensor_scalar_mul(out=negcs[:], in0=p2[:, 0:8], scalar1=-1.0), reads=[pb2], writes=[B("negcs")])
            q_block(C_QB, 4)
            k_block(C_KB, KTd, "KTd", kds)
            v_block(C_VB, Vd, B("Vd"), 128, vds)
            if STOP == 'SA':
                return

            for hf in range(2):
                dsl = DS("lg")
                POOL._wait(Eng._deps([B("ptT")], [B("xs3")]))
                gather(dsl, Lst[:, :], clf, ptT[:, ts * 2 + hf:ts * 2 + hf + 1])
                B("xs3").w = (dsl.sem, dsl.count)
                B("xs3").r = {}
                L3 = Lst.rearrange("p (r h) -> p h r", h=8)
                for h in range(8):
                    DVE.op(lambda e, h=h: e.tensor_tensor_scan(out=Pf[:, h, :], data0=ones_f[:, :], data1=L3[:, h, :], initial=0.0,
                                                               op0=ALU.mult, op1=ALU.add), reads=[B("xs3"), B("ones_f")], writes=[B(f"big{h // 2}")])
                pfb = [B(f"big{q}") for q in range(4)]
                pE, pEb = next_ps()
                PE.op(lambda e, pE=pE: e.matmul(pE[:, 0:8], lhsT=MB_f[:], rhs=Pf[:, :, 127], start=True, stop=True),
                      reads=pfb + [B("MB")], writes=[pEb])
                DVE.op(lambda e, pE=pE: e.tensor_tensor(out=small[:, 8:16], in0=pE[:, 0:8], in1=Pf[:, :, 127], op=ALU.add),
                       reads=pfb + [pEb], writes=[B("smallTE")])
                DVE.op(lambda e: e.tensor_tensor(out=Pf, in0=small[:, 8:16].unsqueeze(2).to_broadcast([128, 8, 128]), in1=Pf, op=ALU.subtract),
                       reads=pfb + [B("smallTE")], writes=pfb)
                for hh in range(2):
                    p_, pb_ = next_ps()
                    for q in range(4):
                        h = hh * 4 + q
                        PE.op(lambda e, p_=p_, q=q, h=h: e.transpose(out=p_[:, q * 128:(q + 1) * 128], in_=Pf[:, h, :], identity=ident_f[:]),
                              reads=pfb + [B("ident_f")], writes=[pb_])
                    E = evac_eng()
                    E.op(copy_op(E, biasAll[:, hh * 4:hh * 4 + 4, hf * 8:hf * 8 + 8, :],
                                 p_[:, :].rearrange("p (q b g) -> p q b g", b=8, g=16)), reads=[pb_], writes=[B("xs1"), B("xs2")])
            if STOP == 'SB':
                return

            dso = DS("osc")
            for bl in range(16):
                for pg in range(4):
                    iA = wr_i[0] % 3; wr_i[0] += 1
                    iB = wr_i[0] % 3; wr_i[0] += 1
                    sl = [wflat[iA], wflat[iB]]
                    slb = [B(f"wring{iA}"), B(f"wring{iB}")]
                    sld = [DS(f"wring{iA}"), DS(f"wring{iB}")]
                    POOL._wait(Eng._deps([B("stage3")], slb))
                    for j in range(4):
                        col = ts * 256 + bl * 16 + pg * 4 + j
                        gather(sld[j // 2], sl[j // 2][:, (j % 2) * 2048:(j % 2 + 1) * 2048], cpool, idx_all[:, col:col + 1])
                    for q in range(2):
                        slb[q].w = (sld[q].sem, sld[q].count); slb[q].r = {}
                    pgv = lambda j, t: sl[j // 2][:, (j % 2) * 2048 + t * 512:(j % 2) * 2048 + (t + 1) * 512]
                    for j in range(4):
                        g = pg * 4 + j
                        DVE.op(lambda e, j=j, g=g: e.tensor_copy(out=Vf[:, g, :, 0:64], in_=pgv(j, 1).rearrange("p (h d) -> p h d", d=64)),
                               reads=[slb[j // 2]], writes=[B("Vf")])
                        DVE.op(lambda e, j=j, g=g: e.tensor_copy(out=Vd[:, g, :, 0:128], in_=pgv(j, 3).rearrange("p (h d) -> p h d", d=128)),
                               reads=[slb[j // 2]], writes=[B("Vd")])
                    for j in range(4):
                        for hh in range(2):
                            p_, pb_ = next_ps()
                            PE.group([(lambda e, p_=p_, q=q, j=j, hh=hh: e.matmul(p_[:, q * 128:(q + 1) * 128], lhsT=pgv(j, 2 * hh)[:, q * 128:(q + 1) * 128],
                                                                                      rhs=ident_b[:], start=True, stop=True)) for q in range(4)],
                                     reads=[slb[j // 2], B("ident_b")], writes=[pb_])
                            E = evac_eng()
                            E.op(copy_op(E, KTs[:, 4 * hh:4 * hh + 4, j * 128:(j + 1) * 128], p_[:, :].rearrange("p (q n) -> p q n", n=128)),
                                 reads=[pb_], writes=KTsb[4 * hh:4 * hh + 4])
                    pS, pSb = next_ps()
                    PE.group([(lambda e, pS=pS, j=j, blk=blk: e.matmul(pS[:, j * 128 + blk * 16:j * 128 + blk * 16 + 16], lhsT=KTs[:, blk, j * 128:(j + 1) * 128],
                                                                        rhs=QBD[:, blk, bl, :], start=True, stop=True)) for j in range(4) for blk in range(8)],
                             reads=KTsb + qb, writes=[pSb])
                    tv = tmpA[0][:, :].rearrange("p (j c) -> p j c", c=128)
                    sv = pS[:, :].rearrange("p (j c) -> p j c", c=128)
                    hq = lambda ap: ap.rearrange("p j (h q) -> p j h q", q=8)
                    DVE.op(lambda e, pg=pg: e.tensor_tensor(out=hq(tv[:, :, 0:64]), in0=hq(sv[:, :, 0:64]),
                                                            in1=biasAll[:, :, bl, pg * 4:pg * 4 + 4].rearrange("p h g -> p g h").unsqueeze(3).to_broadcast([128, 4, 8, 8]),
                                                            op=ALU.add), reads=[pSb, B("xs1"), B("xs2")], writes=[B("tmpA0")])
                    DVE.op(lambda e, pg=pg: e.tensor_tensor(out=hq(tv[:, :, 64:128]), in0=hq(sv[:, :, 64:128]),
                                                            in1=alibS[:, pg * 4:pg * 4 + 4, :].unsqueeze(3).to_broadcast([128, 4, 8, 8]),
                                                            op=ALU.add), reads=[pSb, B("alibS")], writes=[B("tmpA0")])
                    ACT.op(lambda e, pg=pg: e.activation(out=PTs[:, pg * 4:pg * 4 + 4, :], in_=tv, func=AF.Exp), reads=[B("tmpA0")], writes=PTb)
                pS, pSb = next_ps()
                PE.group([(lambda e, pS=pS, blk=blk: e.matmul(pS[:, blk * 16:blk * 16 + 16], lhsT=(KTf if blk < 4 else KTd)[:, blk % 4, T:T + 128],
                                                               rhs=QBD[:, blk, bl, :], start=True, stop=True)) for blk in range(8)],
                         reads=[B(f"KTf{q}") for q in range(4)] + [B(f"KTd{q}") for q in range(4)] + qb, writes=[pSb])
                t1 = tmpA[1][:, 0:128]
                h2 = lambda ap: ap.rearrange("p (h q) -> p h q", q=8)
                DVE.op(lambda e: e.tensor_tensor(out=h2(t1[:, 0:64]), in0=h2(pS[:, 0:64]), in1=negcs[:, :].unsqueeze(2).to_broadcast([128, 8, 8]), op=ALU.add),
                       reads=[pSb, B("negcs")], writes=[B("tmpA1")])
                DVE.op(lambda e: e.tensor_tensor(out=h2(t1[:, 64:128]), in0=h2(pS[:, 64:128]), in1=alibN[:, 0:8].unsqueeze(2).to_broadcast([128, 8, 8]), op=ALU.add),
                       reads=[pSb, B("alibN")], writes=[B("tmpA1")])
                ACT.op(lambda e: e.activation(out=tmpA[1][:, 128:256], in_=t1, func=AF.Exp), reads=[B("tmpA1")], writes=[B("tmpA1")])
                DVE.op(lambda e: e.tensor_tensor(out=h2(PTs[:, 16, :]), in0=h2(tmpA[1][:, 128:256]),
                                                 in1=maskS_f[:, bl * 8:bl * 8 + 8].unsqueeze(1).to_broadcast([128, 16, 8]), op=ALU.mult),
                       reads=[B("tmpA1"), B("maskS")], writes=PTb)
                banks = [(list(range(0, 4)), 65), (list(range(4, 8)), 65), ([8, 9, 10], 129), ([11, 12, 13], 129), ([14, 15], 129)]
                res = {}
                for hms, ncol in banks:
                    p_, pb_ = next_ps()
                    fns = []
                    for n_, hm in enumerate(hms):
                        h = hm if hm < 8 else (hm - 8) // 2
                        V = Vf if hm < 8 else Vd
                        fns += mm(p_[0:8, n_ * ncol:(n_ + 1) * ncol], [(PTs[:, kb, hm * 8:(hm + 1) * 8], V[:, kb, h, :]) for kb in range(17)])
                        res[hm] = (p_, pb_, n_ * ncol)
                    PE.group(fns, reads=PTb + [B("Vf"), B("Vd")], writes=[pb_])
                for hm in range(16):
                    p_, pb_, off = res[hm]
                    if hm < 8:
                        DVE.op(lambda e, p_=p_, off=off: e.reciprocal(out=stat[0:8, 0:1], in_=p_[0:8, off + 64:off + 65]), reads=[pb_], writes=[B("stat")])
                        DVE.op(lambda e, p_=p_, off=off, hm=hm: e.tensor_scalar_mul(out=osb[0:8, hm * 64:(hm + 1) * 64], in0=p_[0:8, off:off + 64],
                                                                                     scalar1=stat[0:8, 0:1]), reads=[pb_, B("stat")], writes=[B("tmpB0")])
                    else:
                        h, m = (hm - 8) // 2, (hm - 8) % 2
                        DVE.op(lambda e, p_=p_, off=off: e.reciprocal(out=stat[0:8, 8:9], in_=p_[0:8, off + 128:off + 129]), reads=[pb_], writes=[B("stat2")])
                        if m == 0:
                            DVE.op(lambda e, p_=p_, off=off, h=h: e.tensor_scalar_mul(out=d1buf[0:8, h, :], in0=p_[0:8, off:off + 128], scalar1=stat[0:8, 8:9]),
                                   reads=[pb_, B("stat2")], writes=[B("d1_0")])
                        else:
                            DVE.op(lambda e: e.tensor_tensor(out=stat[0:8, 9:10], in0=stat[0:8, 8:9], in1=lamc[0:8, 0:1], op=ALU.mult),
                                   reads=[B("stat2"), B("lamc")], writes=[B("stat2")])
                            DVE.op(lambda e, p_=p_, off=off, h=h: e.scalar_tensor_tensor(out=d1buf[0:8, h, :], in0=p_[0:8, off:off + 128], scalar=stat[0:8, 9:10],
                                                                                          in1=d1buf[0:8, h, :], op0=ALU.mult, op1=ALU.add),
                                   reads=[pb_, B("stat2"), B("d1_0")], writes=[B("d1_0")])
                sq = tmpA[1][0:8, 0:512].rearrange("p (h e) -> p h e", e=128)
                ACT.op(lambda e: e.activation(out=sq, in_=d1buf[0:8, :, :], func=AF.Square), reads=[B("d1_0")], writes=[B("tmpA1")])
                DVE.op(lambda e: e.reduce_sum(out=stat[0:8, 10:14], in_=sq, axis=mybir.AxisListType.X), reads=[B("tmpA1")], writes=[B("stat3")])
                ACT.op(lambda e: e.activation(out=stat[0:8, 10:14], in_=stat[0:8, 10:14], func=AF.Sqrt, bias=LN_EPS, scale=1.0 / 128),
                       reads=[B("stat3")], writes=[B("stat3")])
                DVE.op(lambda e: e.reciprocal(out=stat[0:8, 10:14], in_=stat[0:8, 10:14]), reads=[B("stat3")], writes=[B("stat3")])
                DVE.op(lambda e: e.tensor_tensor(out=d1buf[0:8, :, :], in0=d1buf[0:8, :, :], in1=stat[0:8, 10:14].unsqueeze(2).to_broadcast([8, 4, 128]), op=ALU.mult),
                       reads=[B("d1_0"), B("stat3")], writes=[B("d1_0")])
                DVE.op(lambda e: e.tensor_tensor(out=osb[0:8, 512:1024].rearrange("p (h e) -> p h e", e=128), in0=d1buf[0:8, :, :],
                                                 in1=gsub[0:8, :].unsqueeze(1).to_broadcast([8, 4, 128]), op=ALU.mult),
                       reads=[B("d1_0"), B("gsub")], writes=[B("tmpB0")])
                SP.dma(dso, osc[ts * 128 + bl * 8:ts * 128 + bl * 8 + 8, :], osb[0:8, :], reads=[B("tmpB0")], writes=[B("osc")])
            B("osc").w = (dso.sem, dso.count)
            SP.dma(dso, oAB[:, 0, :], osc[rows, :], reads=[B("osc")], writes=[B("oAB0")])
            if STOP == 'SC':
                return
            phase_c(1, True, None, mbc, lambda i: ys[rows, :], [0], hbufs)

        nch = int(os.environ.get("KDEV_NCH", NCH))
        if STOP == 'setup':
            nch = 0
        for pi in range(NP):
            for c in range(nch):
                chunk(c, pi)
        if STOP != 'setup' and os.environ.get('KDEV_NOSAMPLE') is None:
            for ts in range(NS):
                sample_tile(ts)

        SP._wait({d.sem: d.count for d in _ds.values() if d.count})
        SP._wait({E.sem: E.count for E in (PE, ACT, DVE, POOL)})
    return nc


_INPUT_KEYS = None


def kernel(**inputs):
    ncores = int(os.environ.get("KDEV_CORES", 8))
    NP = int(os.environ.get("KDEV_NP", 8 // ncores))
    NS = int(os.environ.get("KDEV_NS", 8 // ncores))
    n_phys = int(inputs["cache_k_fox"].shape[1])
    nc = build_program(NP=NP, NS=NS, n_phys=n_phys)
    f = lambda a: np.ascontiguousarray(np.asarray(a))
    rr = lambda k: np.asarray(inputs[k]).reshape(n_phys * 128, 512)
    pools = {
        "cpool": np.concatenate([rr("cache_k_fox"), rr("cache_v_fox"), rr("cache_k_diff"), rr("cache_v_diff")], axis=1),
        "clf": f(inputs["cache_logf_fox"]).reshape(n_phys, 1024),
    }
    shared = {
        "w_ada": f(inputs["w_ada"][0]), "b_ada": f(inputs["b_ada"]), "w_in": f(inputs["w_in"][0]),
        "b_forget": f(inputs["b_forget"]), "lq1": f(inputs["lambda_q1"]), "lk1": f(inputs["lambda_k1"]),
        "lq2": f(inputs["lambda_q2"]), "lk2": f(inputs["lambda_k2"]), "subln": f(inputs["subln_gain"]),
        "w_ba": f(inputs["w_branch_a"][0]), "w_bb": f(inputs["w_branch_b"][0]), "w_out": f(inputs["w_out"][0]),
        "ln1g": f(inputs["ln1_gain"]), "ln1b": f(inputs["ln1_bias"]),
        "w_fg": f(inputs["w_ffn_gate"][0]), "w_fu": f(inputs["w_ffn_up"][0]), "w_fd": f(inputs["w_ffn_down"][0]),
        "ln2g": f(inputs["ln2_gain"]), "ln2b": f(inputs["ln2_bias"]),
    }
    in_maps = []
    for b in range(ncores):
        sq = slice(16 * NS * b, 16 * NS * (b + 1))
        m = {
            "xp": f(inputs["x_prompt"][NP * b:NP * (b + 1)]).reshape(NP * T, D),
            "xs": f(inputs["x_sample"][sq]).reshape(NS * 128, D),
            "cp": f(inputs["c_prompt"][NP * b:NP * (b + 1)]), "cs": f(inputs["c_sample"][sq]),
            "pt": f(inputs["page_table"][sq]).reshape(1, NS * 256).astype(np.int32),
        }
        m.update(pools)
        m.update(shared)
        in_maps.append(m)
    res = run_bass_kernel_spmd(nc, in_maps, core_ids=list(range(ncores)))
    R = res.results
    nb = len(R)

    def cat(name, shape):
        return np.stack([R[b][name] for b in range(nb)], 0).reshape(shape)
    npq, nsq = nb * NP, nb * NS * 16
    outs = (cat("yp", (npq, T, D)), cat("ys", (nsq, 8, D)),
            cat("kfp", (1, npq, T, 8, 64)), cat("vfp", (1, npq, T, 8, 64)), cat("lfp", (1, npq, T, 8)),
            cat("kdp", (1, npq, T, 4, 2, 64)), cat("vdp", (1, npq, T, 4, 128)),
            cat("kfs", (1, nsq, 8, 8, 64)), cat("vfs", (1, nsq, 8, 8, 64)), cat("lfs", (1, nsq, 8, 8)),
            cat("kds", (1, nsq, 8, 4, 2, 64)), cat("vds", (1, nsq, 8, 4, 128)))
    return tuple(np.ascontiguousarray(o.astype(np.float32)) for o in outs)
```
